# Optimizing a Trainium2 kernel written in Bass

```python
import jax, jax.numpy as jnp
from jax import lax
import numpy as np

D_MODEL = 1024
BATCH = 4
SEQ = 8192
DEPTH = 1

PLE_DIM = 256
MLA_HEADS = 4
QK_NOPE_DIM = 128
QK_ROPE_DIM = 64
QK_HEAD_DIM = QK_NOPE_DIM + QK_ROPE_DIM
V_HEAD_DIM = 128
Q_LORA_RANK = 256
KV_LORA_RANK = 128
ROPE_THETA = 10000.0
MLA_WIDTH = MLA_HEADS * V_HEAD_DIM
Q_BLOCK = 128
MLSTM_HEADS = 4
MLSTM_HEAD_DIM = 128
MLSTM_WIDTH = MLSTM_HEADS * MLSTM_HEAD_DIM
CONV_WIDTH = 4
CHUNK = 64
D_MIX = MLA_WIDTH + MLSTM_WIDTH
IN_SIZES = (Q_LORA_RANK, KV_LORA_RANK, QK_ROPE_DIM, MLSTM_WIDTH, MLSTM_WIDTH, MLSTM_WIDTH, MLSTM_WIDTH, MLSTM_HEADS, MLSTM_HEADS)
D_IN = sum(IN_SIZES)
D_FF = ((8 * D_MODEL + 3 * 256 - 1) // (3 * 256)) * 256
EPS = 1e-6

kernel_name = "hymba_mla_mlstm_layer"


def rms_norm(x, g):
    xf = x.astype(jnp.float32)
    y = xf * lax.rsqrt(jnp.mean(xf * xf, axis=-1, keepdims=True) + EPS)
    return (y * g.astype(jnp.float32)).astype(x.dtype)


def split_cols(z, sizes):
    offs = np.cumsum(sizes)[:-1].tolist()
    return jnp.split(z, offs, axis=-1)


def rope_cos_sin(positions):
    inv_freq = ROPE_THETA ** (-jnp.arange(0, QK_ROPE_DIM, 2, dtype=jnp.float32) / QK_ROPE_DIM)
    ang = positions.astype(jnp.float32)[..., None] * inv_freq
    return jnp.cos(ang), jnp.sin(ang)


def apply_rope(t, cos, sin):
    tf = t.astype(jnp.float32)
    t1, t2 = tf[..., :QK_ROPE_DIM // 2], tf[..., QK_ROPE_DIM // 2:]
    return jnp.concatenate([t1 * cos - t2 * sin, t2 * cos + t1 * sin], axis=-1).astype(t.dtype)


def causal_mla(q, k, v):
    B, S, H, _ = q.shape
    nb = S // Q_BLOCK
    qb = (q * (QK_HEAD_DIM ** -0.5)).reshape(B, nb, Q_BLOCK, H, QK_HEAD_DIM).transpose(1, 0, 3, 2, 4)
    key_pos = jnp.arange(S)

    def one_block(args):
        q_blk, blk = args
        s = jnp.einsum('bhqd,bkhd->bhqk', q_blk, k).astype(jnp.float32)
        q_pos = blk * Q_BLOCK + jnp.arange(Q_BLOCK)
        s = jnp.where(key_pos[None, :] <= q_pos[:, None], s, -jnp.inf)
        pr = jax.nn.softmax(s, axis=-1).astype(v.dtype)
        return jnp.einsum('bhqk,bkhv->bqhv', pr, v)

    out = lax.map(one_block, (qb, jnp.arange(nb)))
    return out.transpose(1, 0, 2, 3, 4).reshape(B, S, H * V_HEAD_DIM)


def causal_dwconv(t, w, b):
    S = t.shape[1]
    tp = jnp.pad(t, ((0, 0), (CONV_WIDTH - 1, 0), (0, 0)))
    acc = b
    for j in range(CONV_WIDTH):
        acc = acc + w[j] * tp[:, j:j + S]
    return acc


def mlstm_chunkwise(q, k, v, i_pre, f_pre):
    out_dtype = q.dtype
    B, S, NH, DH = q.shape
    nc = S // CHUNK
    f32 = jnp.float32

    def chunks4(t):
        return t.astype(f32).reshape(B, nc, CHUNK, NH, DH).transpose(1, 0, 3, 2, 4)

    def chunks3(t):
        return t.astype(f32).reshape(B, nc, CHUNK, NH).transpose(1, 0, 3, 2)

    qc_all = chunks4(q)
    kc_all = chunks4(k) * (DH ** -0.5)
    vc_all = chunks4(v)
    ic_all = chunks3(i_pre)
    fc_all = jax.nn.log_sigmoid(chunks3(f_pre))
    tril = jnp.tril(jnp.ones((CHUNK, CHUNK), dtype=bool))

    def step(carry, xs):
        C, n, m = carry
        qc, kc, vc, ic, fc = xs
        b = jnp.cumsum(fc, axis=-1)
        g = b[..., -1]
        d = b[..., :, None] - b[..., None, :] + ic[..., None, :]
        d = jnp.where(tril, d, -jnp.inf)
        m_inter = b + m[..., None]
        m_j = jnp.maximum(m_inter, d.max(axis=-1))
        s = jnp.einsum('bhld,bhsd->bhls', qc, kc) * jnp.exp(d - m_j[..., None])
        inter = jnp.exp(m_inter - m_j)
        num = jnp.einsum('bhls,bhsv->bhlv', s, vc) + inter[..., None] * jnp.einsum('bhld,bhdv->bhlv', qc, C)
        den = s.sum(axis=-1) + inter * jnp.einsum('bhld,bhd->bhl', qc, n)
        h = num / jnp.maximum(jnp.abs(den), jnp.exp(-m_j))[..., None]
        a = g[..., None] - b + ic
        m_new = jnp.maximum(g + m, a.max(axis=-1))
        decay = jnp.exp(g + m - m_new)
        w = jnp.exp(a - m_new[..., None])
        C_new = decay[..., None, None] * C + jnp.einsum('bhs,bhsd,bhsv->bhdv', w, kc, vc)
        n_new = decay[..., None] * n + jnp.einsum('bhs,bhsd->bhd', w, kc)
        return (C_new, n_new, m_new), h

    init = (jnp.zeros((B, NH, DH, DH), f32), jnp.zeros((B, NH, DH), f32), jnp.zeros((B, NH), f32))
    _, h = lax.scan(step, init, (qc_all, kc_all, vc_all, ic_all, fc_all))
    return h.transpose(1, 0, 3, 2, 4).reshape(B, S, NH, DH).astype(out_dtype)


def setup_inputs(seed: int = 0) -> dict:
    key = jax.random.key(seed)
    ks = jax.random.split(key, 26)
    f32 = jnp.float32

    def nrm(k, shape, scale):
        return jax.random.normal(k, shape, f32) * scale

    def gain(k, shape):
        return 1.0 + 0.05 * jax.random.normal(k, shape, f32)

    L = DEPTH
    x = nrm(ks[0], (BATCH, SEQ, D_MODEL), 1.0)
    p = nrm(ks[1], (DEPTH, BATCH, SEQ, PLE_DIM), 1.0)
    offset = jax.random.randint(ks[2], (BATCH, 1), 0, 1024, dtype=jnp.int32)
    positions = (offset + jnp.arange(SEQ, dtype=jnp.int32)[None, :]).astype(jnp.int32)
    return {
        "x": x,
        "p": p,
        "positions": positions,
        "attn_pre_norm": gain(ks[3], (L, D_MODEL)),
        "attn_post_norm": gain(ks[4], (L, D_MODEL)),
        "w_in": nrm(ks[5], (L, D_MODEL, D_IN), D_MODEL ** -0.5),
        "q_norm": gain(ks[6], (L, Q_LORA_RANK)),
        "kv_norm": gain(ks[7], (L, KV_LORA_RANK)),
        "w_uq": nrm(ks[8], (L, Q_LORA_RANK, MLA_HEADS * QK_HEAD_DIM), Q_LORA_RANK ** -0.5),
        "w_ukv": nrm(ks[9], (L, KV_LORA_RANK, MLA_HEADS * (QK_NOPE_DIM + V_HEAD_DIM)), KV_LORA_RANK ** -0.5),
        "conv_w": nrm(ks[10], (L, CONV_WIDTH, 2 * MLSTM_WIDTH), CONV_WIDTH ** -0.5),
        "conv_b": nrm(ks[11], (L, 2 * MLSTM_WIDTH), 0.02),
        "gate_bias_i": nrm(ks[12], (L, MLSTM_HEADS), 0.1),
        "gate_bias_f": 3.0 + nrm(ks[13], (L, MLSTM_HEADS), 0.5),
        "mlstm_norm": gain(ks[14], (L, MLSTM_WIDTH)),
        "w_out": nrm(ks[15], (L, D_MIX, D_MODEL), D_MIX ** -0.5),
        "ffn_pre_norm": gain(ks[16], (L, D_MODEL)),
        "ffn_post_norm": gain(ks[17], (L, D_MODEL)),
        "w_gate": nrm(ks[18], (L, D_MODEL, D_FF), D_MODEL ** -0.5),
        "w_up": nrm(ks[19], (L, D_MODEL, D_FF), D_MODEL ** -0.5),
        "w_down": nrm(ks[20], (L, D_FF, D_MODEL), D_FF ** -0.5),
        "w_ple_proj": nrm(ks[21], (L, PLE_DIM, D_MODEL), PLE_DIM ** -0.5),
        "w_ple_gate": nrm(ks[22], (L, D_MODEL, D_MODEL), D_MODEL ** -0.5),
    }


def reference(x, p, positions, attn_pre_norm, attn_post_norm, w_in, q_norm, kv_norm, w_uq, w_ukv,
              conv_w, conv_b, gate_bias_i, gate_bias_f, mlstm_norm, w_out, ffn_pre_norm, ffn_post_norm,
              w_gate, w_up, w_down, w_ple_proj, w_ple_gate):
    B, S, _ = x.shape
    cos, sin = rope_cos_sin(positions)
    hd = (B, S, MLSTM_HEADS, MLSTM_HEAD_DIM)
    h = x
    for i in range(DEPTH):
        u = rms_norm(h, attn_pre_norm[i])
        c_q, c_kv, k_rope, m_q, m_k, m_v, m_o, m_i, m_f = split_cols(u @ w_in[i], IN_SIZES)

        q = (rms_norm(c_q, q_norm[i]) @ w_uq[i]).reshape(B, S, MLA_HEADS, QK_HEAD_DIM)
        kv = (rms_norm(c_kv, kv_norm[i]) @ w_ukv[i]).reshape(B, S, MLA_HEADS, QK_NOPE_DIM + V_HEAD_DIM)
        q_nope, q_pe = q[..., :QK_NOPE_DIM], q[..., QK_NOPE_DIM:]
        k_nope, v = kv[..., :QK_NOPE_DIM], kv[..., QK_NOPE_DIM:]
        q_pe = apply_rope(q_pe, cos[:, :, None], sin[:, :, None])
        k_pe = apply_rope(k_rope, cos, sin)[:, :, None]
        q = jnp.concatenate([q_nope, q_pe], axis=-1)
        k = jnp.concatenate([k_nope, jnp.broadcast_to(k_pe, (B, S, MLA_HEADS, QK_ROPE_DIM))], axis=-1)
        mla_out = causal_mla(q, k, v)

        qk = jax.nn.silu(causal_dwconv(jnp.concatenate([m_q, m_k], axis=-1), conv_w[i], conv_b[i]))
        mq, mk = qk[..., :MLSTM_WIDTH], qk[..., MLSTM_WIDTH:]
        hm = mlstm_chunkwise(mq.reshape(hd), mk.reshape(hd), m_v.reshape(hd),
                             m_i + gate_bias_i[i], m_f + gate_bias_f[i])
        hm = rms_norm(hm, mlstm_norm[i].reshape(MLSTM_HEADS, MLSTM_HEAD_DIM)).reshape(B, S, MLSTM_WIDTH)
        mlstm_out = jax.nn.sigmoid(m_o) * hm

        mix = jnp.concatenate([mla_out, mlstm_out], axis=-1) @ w_out[i]
        h = h + rms_norm(mix, attn_post_norm[i])

        f = rms_norm(h, ffn_pre_norm[i])
        f = (jax.nn.silu(f @ w_gate[i]) * (f @ w_up[i])) @ w_down[i]
        h = h + rms_norm(f, ffn_post_norm[i])

        h = h + jax.nn.sigmoid(h @ w_ple_gate[i]) * (p[i] @ w_ple_proj[i])
    return h
```

```python
import math
import os
from contextlib import ExitStack

import numpy as np
import concourse.bass as bass
import concourse.mybir as mybir
from concourse.bass_utils import run_bass_kernel_spmd

F32 = mybir.dt.float32
BF16 = mybir.dt.bfloat16
I32 = mybir.dt.int32
AF = mybir.ActivationFunctionType
ALU = mybir.AluOpType

D = 1024
DC = 8
DIN = 2504
DFF = 2816
NFF = 22
PLE = 256
EPS = 1e-6
QSCALE = 192.0 ** -0.5
LNK = math.log(128.0 ** -0.5)
TWO_PI = 2.0 * math.pi * (1.0 - 1e-6)
NEG = -30000.0

EPOCH = 12000
NDMA = 4


class Tok:
    __slots__ = ("sem", "val", "implied", "eng")

    def __init__(self, sem, val, implied, eng):
        self.sem = sem
        self.val = val
        self.implied = implied
        self.eng = eng


class Sched:
    ENGS = ("pe", "act", "dve", "pool", "sp")

    def __init__(self, nc, stack):
        self.nc = nc
        self.stack = stack
        self.ops = {e: [] for e in self.ENGS}
        self.cnt = {e: 0 for e in self.ENGS}
        self.esems = {e: [] for e in self.ENGS}
        self.known = {e: {} for e in self.ENGS}
        self.dsems = {}
        self.dcnt = {}
        self.dlast = {}
        self.ndma = {e: 0 for e in self.ENGS}
        self.last_write = {}
        self.readers = {}
        self.alias = {}
        self.rec = None

    def _expand(self, keys):
        out = []
        for k in keys:
            out.extend(self.alias.get(k, (k,)))
        return out

    def _newsem(self, name):
        return self.stack.enter_context(self.nc.semaphore(name))

    def _esem(self, eng, epoch):
        lst = self.esems[eng]
        while len(lst) <= epoch:
            lst.append(self._newsem("s_%s_%d" % (eng, len(lst))))
        return lst[epoch]

    def _need(self, eng, tok, waits):
        if tok is None:
            return
        kn = self.known[eng]
        if kn.get(tok.sem, 0) >= tok.val:
            return
        if tok.eng == "pe" and eng == "pe":
            return
        waits[tok.sem] = max(waits.get(tok.sem, 0), tok.val)
        kn[tok.sem] = tok.val
        for s, v in tok.implied.items():
            if kn.get(s, 0) < v:
                kn[s] = v

    def replay_rr(self, streams):
        idx = [0] * len(streams)
        left = sum(len(x) for x in streams)
        while left:
            for i, st in enumerate(streams):
                if idx[i] < len(st):
                    self.op(*st[idx[i]])
                    idx[i] += 1
                    left -= 1

    def op(self, eng, fn, reads=(), writes=(), dma=False):
        if self.rec is not None:
            self.rec.append((eng, fn, list(reads), list(writes), dma))
            return None
        reads = self._expand(reads)
        writes = self._expand(writes)
        waits = {}
        for k in reads:
            self._need(eng, self.last_write.get(k), waits)
            if k.startswith("pb"):
                for t in self.readers.get(k, ()):
                    if t.eng != eng:
                        self._need(eng, t, waits)
        for k in writes:
            self._need(eng, self.last_write.get(k), waits)
            for t in self.readers.get(k, ()):
                self._need(eng, t, waits)
        if dma:
            i = self.ndma[eng]
            self.ndma[eng] += 1
            key = (eng, i % NDMA)
            if key not in self.dsems:
                self.dsems[key] = self._newsem("d_%s_%d" % key)
                self.dcnt[key] = 0
            self._need(eng, self.dlast.get(key), waits)
            self.dcnt[key] += 1
            sem = self.dsems[key]
            val = 16 * self.dcnt[key]
            inc = 16
            implied = dict(self.known[eng])
        else:
            n = self.cnt[eng]
            self.cnt[eng] += 1
            sem = self._esem(eng, n // EPOCH)
            val = n % EPOCH + 1
            inc = 1
            implied = dict(self.known[eng])
            for ep in range(n // EPOCH):
                implied[self.esems[eng][ep]] = EPOCH
        tok = Tok(sem, val, implied, eng)
        if dma:
            self.dlast[key] = tok
        self.ops[eng].append((list(waits.items()), fn, sem, inc))
        for k in reads:
            self.readers.setdefault(k, []).append(tok)
        for k in writes:
            self.last_write[k] = tok
            self.readers[k] = []
        return tok

    def barrier(self):
        toks = []
        for e in self.ENGS:
            n = self.cnt[e]
            if n:
                imp = {self.esems[e][ep]: EPOCH for ep in range((n - 1) // EPOCH)}
                toks.append(Tok(self._esem(e, (n - 1) // EPOCH), (n - 1) % EPOCH + 1, imp, e))
        toks += list(self.dlast.values())
        for e in self.ENGS:
            waits = {}
            for t in toks:
                self._need(e, t, waits)
            if waits:
                self.ops[e].append((list(waits.items()), None, None, 0))
        self.last_write = {}
        self.readers = {}

    def emit(self):
        nc = self.nc
        engmap = {"pe": "tensor", "act": "scalar", "dve": "vector", "pool": "gpsimd", "sp": "sync"}
        with nc.Block() as block:
            for e in self.ENGS:
                ops = self.ops[e]
                if not ops:
                    continue

                def body(engobj, ops=ops):
                    for waits, fn, sem, inc in ops:
                        for s, v in waits:
                            engobj.wait_ge(s, v)
                        if fn is not None:
                            fn(engobj).then_inc(sem, inc)

                getattr(block, engmap[e])(body)


class Rot:
    def __init__(self, items):
        self.items = items
        self.i = 0

    def next(self):
        it = self.items[self.i % len(self.items)]
        self.i += 1
        return it


def build(S, debug=False, phases=("A1", "B", "A2", "Bp", "C")):
    NT = S // 512
    NO = NT // 2
    SO = S // 2
    NB = S // 128
    nc = bass.Bass("TRN2", target_bir_lowering=False)
    dr = lambda n, s, d, k="ExternalInput": nc.dram_tensor(n, s, d, kind=k)
    xT = dr("xT", [D, S], F32)
    pos = dr("pos", [1, S], I32)
    pT = dr("pT", [PLE, SO], F32)
    cst = dr("cst", [128, 96], F32)
    gainrow = dr("gainrow", [1, 512], F32)
    gbrow = dr("gbrow", [1, 8], F32)
    kflag = dr("kflag", [1, S], F32)
    validT = dr("validT", [128, NB], F32)
    cm = dr("cm", [128, 4 * 512 + 4 * 128], F32)
    w_in = dr("w_in", [D, DIN], F32)
    w_uq = dr("w_uq", [256, 768], F32)
    w_ukv = dr("w_ukv", [128, 1024], F32)
    w_out = dr("w_out", [D, D], F32)
    wgu = dr("wgu", [NFF, 128, DC, 256], F32)
    wdn = dr("wdn", [DC, 128, NFF, 128], F32)
    w_pp = dr("w_pp", [PLE, D], F32)
    w_pg = dr("w_pg", [D, D], F32)
    yT = dr("yT", [D, SO], F32, "ExternalOutput")
    h1d = dr("h1d", [D, SO], F32, "Internal")
    wgu_s = dr("wgu_s", [NFF, 128, DC * 256], BF16, "Internal")
    wdn_s = dr("wdn_s", [DC, 128, NFF * 128], BF16, "Internal")
    if debug:
        dbg_cat = dr("dbg_cat", [128, 8 * SO], F32, "ExternalOutput")
        dbg_h1 = dr("dbg_h1", [D, SO], F32, "ExternalOutput")

    G_APRE, G_APOST, G_FPRE, G_FPOST = 0, 8, 16, 24
    C_QN, C_KVN, C_CW, C_CB, C_INVF = 32, 34, 35, 67, 75

    with ExitStack() as top:
        sch = Sched(nc, top)
        OP = sch.op

        def sbuf(st, n, s, d):
            return st.enter_context(nc.sbuf_tensor(n, s, d))

        def mm(out, lhsT, rhs, start, stop, reads, writes):
            OP("pe", lambda e: e.matmul(out, lhsT, rhs, start=start, stop=stop), reads, writes)

        def act(out, in_, func, reads, writes, eng="act", **kw):
            OP(eng, lambda e: e.activation(out=out, in_=in_, func=func, **kw), reads, writes)

        def tt(out, in0, in1, op, reads, writes, eng="dve"):
            OP(eng, lambda e: e.tensor_tensor(out=out, in0=in0, in1=in1, op=op), reads, writes)

        def ts(out, in0, s1, s2, op0, op1, reads, writes, eng="dve"):
            if s2 is None:
                OP(eng, lambda e: e.tensor_scalar(out=out, in0=in0, scalar1=s1, scalar2=None, op0=op0), reads, writes)
            else:
                OP(eng, lambda e: e.tensor_scalar(out=out, in0=in0, scalar1=s1, scalar2=s2, op0=op0, op1=op1), reads, writes)

        def stt(out, in0, sc, in1, op0, op1, reads, writes, eng="dve"):
            OP(eng, lambda e: e.scalar_tensor_tensor(out=out, in0=in0, scalar=sc, in1=in1, op0=op0, op1=op1), reads, writes)

        def cp(out, in_, reads, writes, eng="dve"):
            if eng == "act":
                OP("act", lambda e: e.copy(out=out, in_=in_), reads, writes)
            else:
                OP(eng, lambda e: e.tensor_copy(out=out, in_=in_), reads, writes)

        def dma(eng, out, in_, reads, writes):
            OP(eng, lambda e: e.dma_start(out=out, in_=in_), reads, writes, dma=True)

        def dmac(eng, dst, src, nch, reads, writes):
            for w in writes:
                sch.alias.setdefault(w, ["%s#%d" % (w, c) for c in range(nch)])
            for c in range(nch):
                dma(eng, dst[:, c, :], src[c * 128:(c + 1) * 128, :], reads, ["%s#%d" % (w, c) for w in writes])

        def dmac_out(eng, dst, src, nch, reads, writes):
            for c in range(nch):
                dma(eng, dst[c * 128:(c + 1) * 128, :], src[:, c, :], reads, writes)

        PB = [top.enter_context(nc.psum_tensor("pb%d" % i, [128, 512], F32)) for i in range(8)]
        PK = ["pb%d" % i for i in range(8)]

        cs = sbuf(top, "cs", [128, 96], F32)
        identb = sbuf(top, "identb", [128, 128], BF16)
        identf = sbuf(top, "identf", [128, 128], F32)
        onesb = sbuf(top, "onesb", [128, 128], BF16)
        onesf = sbuf(top, "onesf", [128, 128], F32)
        trif = sbuf(top, "trif", [128, 128], F32)
        mposf = sbuf(top, "mposf", [128, 128], F32)
        catA = sbuf(top, "catA", [128, 4, SO], BF16)
        dma("sp", cs[:], cst[:, :], [], ["cs"])
        dma("sp", trif[:], cm[:, 2048:2176], [], ["trif"])
        dma("sp", mposf[:], cm[:, 2176:2304], [], ["mposf"])
        dma("sp", identf[:], cm[:, 2304:2432], [], ["identf"])
        dma("pool", identb[:], cm[:, 2304:2432], [], ["identb"])
        OP("dve", lambda e: e.memset(onesb[:], 1.0), [], ["onesb"])
        OP("dve", lambda e: e.memset(onesf[:], 1.0), [], ["onesf"])

        def rms_scale(ps_ap, n, out_ap, tmp_ap, rk, wk):
            act(tmp_ap, ps_ap, AF.Ln, rk, [wk + "_t"], bias=EPS, scale=1.0 / n)
            act(out_ap, tmp_ap, AF.Exp, [wk + "_t"], [wk], scale=-0.5)

        with ExitStack() as pa:
            wA = sbuf(pa, "wA", [128, DC, 448], BF16)
            wArot = sbuf(pa, "wArot", [128, DC, 64], BF16)
            wuq = sbuf(pa, "wuq", [128, 2, 768], BF16)
            wuqrot = sbuf(pa, "wuqrot", [128, 2, 4, 64], BF16)
            wukv = sbuf(pa, "wukv", [128, 1024], BF16)
            cqn = sbuf(pa, "cqn", [128, 2, SO], BF16)
            ckvn = sbuf(pa, "ckvn", [128, S], BF16)
            kpe = sbuf(pa, "kpe", [65, S], BF16)
            qpe = sbuf(pa, "qpe", [65, 4, SO], BF16)
            m4 = sbuf(pa, "m4", [128, 4, 512], BF16)
            dma("pool", m4[:].rearrange("p a q -> p (a q)"), cm[:, 0:2048], [], ["m4"])
            dmac("pool", wA, w_in[:, 0:448], DC, [], ["wA"])
            dmac("pool", wuq, w_uq[:, :], 2, [], ["wuq"])
            dma("pool", wukv[:], w_ukv[:, :], [], ["wukv"])
            dma("pool", kpe[64:65, :], kflag[:, :], [], ["kpe_flag"])
            OP("dve", lambda e: e.memset(qpe[64:65, :, :], 1.0), [], ["qpe_one"])
            for n in range(NFF):
                dma("pool", wgu_s[n, :, :], wgu[n, :, :, :].rearrange("p c f -> p (c f)"), [], ["wgus%d" % n])
            for ct in range(DC):
                dma("pool", wdn_s[ct, :, :], wdn[ct, :, :, :].rearrange("p n f -> p (n f)"), [], ["wdns%d" % ct])
            OP("act", lambda e: e.mul(out=wArot[:, :, 0:32], in_=wA[:, :, 416:448], mul=-1.0), ["wA"], ["wArot_a"])
            cp(wArot[:, :, 32:64], wA[:, :, 384:416], ["wA"], ["wArot_b"])
            for h in range(4):
                b0 = h * 192 + 128
                OP("act", lambda e, h=h, b0=b0: e.mul(out=wuqrot[:, :, h, 0:32], in_=wuq[:, :, b0 + 32:b0 + 64], mul=-1.0),
                   ["wuq"], ["wuqrot_a%d" % h])
                cp(wuqrot[:, :, h, 32:64], wuq[:, :, b0:b0 + 32], ["wuq"], ["wuqrot_b%d" % h])
            WROT = ["wArot_a", "wArot_b"]
            WQROT = ["wuqrot_a%d" % h for h in range(4)] + ["wuqrot_b%d" % h for h in range(4)]

            with ExitStack() as p1:
                xt = sbuf(p1, "xt", [128, DC, 512], F32)
                u = sbuf(p1, "u", [128, DC, 512], BF16)
                sq = [sbuf(p1, "sq%d" % i, [128, 512], BF16) for i in range(3)]
                sqr = Rot([(sq[i], "sq%d" % i) for i in range(3)])
                lnt = sbuf(p1, "lnt", [128, 512], F32)
                rbc = sbuf(p1, "rbc", [128, 512], F32)
                r2 = sbuf(p1, "r2", [128, 512], F32)
                raw = sbuf(p1, "raw", [128, 2, 512], F32)
                pi = sbuf(p1, "pi", [64, 512], I32)
                tr = [sbuf(p1, "tr%d" % i, [64, 512], F32) for i in range(4)]
                tri_i = sbuf(p1, "tri_i", [64, 512], I32)
                sinT = sbuf(p1, "sinT", [64, 512], F32)
                cosT = sbuf(p1, "cosT", [64, 512], F32)
                t1 = sbuf(p1, "t1", [64, 512], F32)
                t2 = sbuf(p1, "t2", [64, 512], F32)
                psr = Rot([(PB[i], PK[i]) for i in range(8)])

                for v in (range(NT) if "A1" in phases else []):
                    own = (v % 2 == 1)
                    j = v // 2
                    tsl = slice(v * 512, (v + 1) * 512)
                    osl = slice(j * 512, (j + 1) * 512)
                    dmac("sp", xt, xT[:, tsl], DC, [], ["xt"])
                    dma("sp", pi[:], pos[0:1, tsl].partition_broadcast(64), [], ["pi"])
                    pss, pssk = psr.next()
                    for c in range(DC):
                        sqt, sqk = sqr.next()
                        act(sqt[:], xt[:, c, :], AF.Square, ["xt#%d" % c], [sqk])
                        mm(pss[:], onesb[:], sqt[:], c == 0, c == DC - 1, ["onesb", sqk], [pssk])
                    rms_scale(pss[:], D, rbc[:], lnt[:], [pssk], "rbc")
                    for c in range(DC):
                        stt(u[:, c, :], xt[:, c, :], cs[:, G_APRE + c:G_APRE + c + 1], rbc[:], ALU.mult, ALU.mult,
                            ["xt#%d" % c, "cs", "rbc"], ["u%d" % c])
                    UK = ["u%d" % c for c in range(DC)]
                    LVL = int(os.environ.get("A1LVL", "9"))
                    if LVL < 2:
                        continue
                    cp(tr[0][:], pi[:], ["pi"], ["tr0"])
                    ts(tr[1][:], tr[0][:], cs[0:64, C_INVF:C_INVF + 1], 0.0, ALU.mult, ALU.add, ["tr0", "cs"], ["tr1"])
                    cp(tri_i[:], tr[1][:], ["tr1"], ["tri_i"])
                    cp(tr[2][:], tri_i[:], ["tri_i"], ["tr2"])
                    tt(tr[3][:], tr[1][:], tr[2][:], ALU.subtract, ["tr1", "tr2"], ["tr3"])
                    act(sinT[:], tr[3][:], AF.Sin, ["tr3"], ["sinT"], scale=TWO_PI)
                    ts(tr[1][:], tr[1][:], 0.25, None, ALU.add, None, ["tr1"], ["tr1"])
                    cp(tri_i[:], tr[1][:], ["tr1"], ["tri_i"])
                    cp(tr[2][:], tri_i[:], ["tri_i"], ["tr2"])
                    tt(tr[3][:], tr[1][:], tr[2][:], ALU.subtract, ["tr1", "tr2"], ["tr3"])
                    act(cosT[:], tr[3][:], AF.Sin, ["tr3"], ["cosT"], scale=TWO_PI)
                    if LVL < 3:
                        continue
                    pkv, pkvk = psr.next()
                    for c in range(DC):
                        mm(pkv[:], wA[:, c, 256:384], u[:, c, :], c == 0, c == DC - 1, ["wA", "u%d" % c], [pkvk])
                    sqt, sqk = sqr.next()
                    act(sqt[:], pkv[:], AF.Square, [pkvk], [sqk])
                    cp(raw[:, 0, :], pkv[:], [pkvk], ["raw0"], eng="act")
                    ps2, ps2k = psr.next()
                    mm(ps2[:], onesb[:], sqt[:], True, True, ["onesb", sqk], [ps2k])
                    rms_scale(ps2[:], 128, r2[:], lnt[:], [ps2k], "r2")
                    stt(ckvn[:, tsl], raw[:, 0, :], cs[:, C_KVN:C_KVN + 1], r2[:], ALU.mult, ALU.mult,
                        ["raw0", "cs", "r2"], ["ckvn%d" % v])
                    if LVL < 4:
                        continue
                    pkr, pkrk = psr.next()
                    pkq, pkqk = psr.next()
                    for c in range(DC):
                        mm(pkr[0:64, :], wA[:, c, 384:448], u[:, c, :], c == 0, c == DC - 1, ["wA", "u%d" % c], [pkrk])
                    for c in range(DC):
                        mm(pkq[0:64, :], wArot[:, c, :], u[:, c, :], c == 0, c == DC - 1, WROT + ["u%d" % c], [pkqk])
                    tt(t1[:], pkr[0:64, :], cosT[:], ALU.mult, [pkrk, "cosT"], ["t1"])
                    tt(t2[:], pkq[0:64, :], sinT[:], ALU.mult, [pkqk, "sinT"], ["t2"])
                    tt(kpe[0:64, tsl], t1[:], t2[:], ALU.add, ["t1", "t2"], ["kpe%d" % v])
                    if not own or LVL < 5:
                        continue
                    pq = [psr.next(), psr.next()]
                    for l in range(2):
                        for c in range(DC):
                            mm(pq[l][0][:], wA[:, c, l * 128:(l + 1) * 128], u[:, c, :], c == 0, c == DC - 1,
                               ["wA", "u%d" % c], [pq[l][1]])
                    ps2, ps2k = psr.next()
                    for l in range(2):
                        sqt, sqk = sqr.next()
                        act(sqt[:], pq[l][0][:], AF.Square, [pq[l][1]], [sqk])
                        cp(raw[:, l, :], pq[l][0][:], [pq[l][1]], ["raw%d" % l], eng="act")
                        mm(ps2[:], onesb[:], sqt[:], l == 0, l == 1, ["onesb", sqk], [ps2k])
                    rms_scale(ps2[:], 256, r2[:], lnt[:], [ps2k], "r2")
                    for l in range(2):
                        stt(cqn[:, l, osl], raw[:, l, :], cs[:, C_QN + l:C_QN + l + 1], r2[:], ALU.mult, ALU.mult,
                            ["raw%d" % l, "cs", "r2"], ["cqn%d_%d" % (j, l)])
                    for h in (range(4) if LVL >= 6 else []):
                        pqr, pqrk = psr.next()
                        pqq, pqqk = psr.next()
                        b0 = h * 192 + 128
                        for l in range(2):
                            mm(pqr[0:64, :], wuq[:, l, b0:b0 + 64], cqn[:, l, osl], l == 0, l == 1,
                               ["wuq", "cqn%d_%d" % (j, l)], [pqrk])
                        for l in range(2):
                            mm(pqq[0:64, :], wuqrot[:, l, h, :], cqn[:, l, osl], l == 0, l == 1,
                               WQROT + ["cqn%d_%d" % (j, l)], [pqqk])
                        tt(t1[:], pqr[0:64, :], cosT[:], ALU.mult, [pqrk, "cosT"], ["t1"])
                        tt(t2[:], pqq[0:64, :], sinT[:], ALU.mult, [pqqk, "sinT"], ["t2"])
                        tt(t1[:], t1[:], t2[:], ALU.add, ["t1", "t2"], ["t1"])
                        ts(qpe[0:64, h, osl], t1[:], QSCALE, None, ALU.mult, None, ["t1"], ["qpe%d_%d" % (j, h)])
            sch.barrier()

            with ExitStack() as p2:
                Kh = sbuf(p2, "Kh", [128, S], BF16)
                Vh = sbuf(p2, "Vh", [128, NB, 128], BF16)
                Qn = sbuf(p2, "Qn", [128, SO], BF16)
                ptl = [sbuf(p2, "pt%d" % i, [128, 512], BF16) for i in range(4)]
                rden = sbuf(p2, "rden", [128, 512], F32)
                dsum = sbuf(p2, "dsum", [128, 512], F32)
                for h in (range(4) if "B" in phases else []):
                    psr = Rot([(PB[i], PK[i]) for i in range(4)])
                    ci = 0
                    for v in range(NT):
                        tsl = slice(v * 512, (v + 1) * 512)
                        ps, pk = psr.next()
                        mm(ps[:], wukv[:, h * 256:h * 256 + 128], ckvn[:, tsl], True, True, ["wukv"], [pk])
                        cp(Kh[:, tsl], ps[:], [pk], ["Kh%d" % v], eng=("act" if ci % 2 else "dve"))
                        ci += 1
                        ps, pk = psr.next()
                        for bl in range(4):
                            mm(ps[:, bl * 128:(bl + 1) * 128], ckvn[:, v * 512 + bl * 128:v * 512 + (bl + 1) * 128],
                               wukv[:, h * 256 + 128:h * 256 + 256], True, True, ["wukv"], [pk])
                        cp(Vh[:, v * 4:(v + 1) * 4, :].rearrange("p a d -> p (a d)"), ps[:], [pk], ["Vh%d" % v],
                           eng=("act" if ci % 2 else "dve"))
                        ci += 1
                    for j in range(NO):
                        osl = slice(j * 512, (j + 1) * 512)
                        ps, pk = psr.next()
                        for l in range(2):
                            mm(ps[:], wuq[:, l, h * 192:h * 192 + 128], cqn[:, l, osl], l == 0, l == 1, ["wuq"], [pk])
                        OP("act", lambda e, ps=ps, osl=osl: e.mul(out=Qn[:, osl], in_=ps[:], mul=QSCALE), [pk], ["Qn%d" % j])
                    pss = Rot([(PB[i], PK[i]) for i in range(4)])
                    pacc = Rot([((PB[4], PK[4]), (PB[5], PK[5])), ((PB[6], PK[6]), (PB[7], PK[7]))])
                    ptr = Rot([(ptl[i], "pt%d" % i) for i in range(4)])
                    for j in range(NO):
                        osl = slice(j * 512, (j + 1) * 512)
                        nfull = 4 * (2 * j + 1)
                        nkb = nfull + 4
                        (oacc, oak), (dacc, dak) = pacc.next()
                        pend = []

                        def issue_qk(kb):
                            st, stk = pss.next()
                            ksl = slice(kb * 128, (kb + 1) * 128)
                            mm(st[:], Kh[:, ksl], Qn[:, osl], True, False, ["Kh%d" % (kb // 4), "Qn%d" % j], [stk])
                            if kb >= nfull:
                                mm(st[:], identb[:], m4[:, kb - nfull, :], False, False, ["identb", "m4"], [stk])
                            mm(st[:], kpe[0:65, ksl], qpe[0:65, h, osl], False, True, [], [stk])
                            pt, ptk = ptr.next()
                            act(pt[:], st[:], AF.Exp, [stk], [ptk])
                            pend.append((kb, pt, ptk))

                        def issue_pv():
                            kb, pt, ptk = pend.pop(0)
                            mm(oacc[:], Vh[:, kb, :], pt[:], kb == 0, kb == nkb - 1, ["Vh%d" % (kb // 4), ptk], [oak])
                            if kb == 0:
                                cp(dsum[:], pt[:], [ptk], ["dsum"])
                            else:
                                tt(dsum[:], dsum[:], pt[:], ALU.add, ["dsum", ptk], ["dsum"])

                        for kb in range(nkb):
                            issue_qk(kb)
                            if len(pend) > 2:
                                issue_pv()
                        while pend:
                            issue_pv()
                        mm(dacc[:], onesf[:], dsum[:], True, True, ["onesf", "dsum"], [dak])
                        OP("dve", lambda e, dacc=dacc: e.reciprocal(out=rden[:], in_=dacc[:]), [dak], ["rden"])
                        tt(catA[:, h, osl], oacc[:], rden[:], ALU.mult, [oak, "rden"], ["catA%d_%d" % (h, j)])
            sch.barrier()

        with ExitStack() as pm:
            catB = sbuf(pm, "catB", [128, 4, SO], BF16)
            with ExitStack() as p3:
                wM = sbuf(p3, "wM", [128, DC, 2056], BF16)
                dmac("pool", wM, w_in[:, 448:2504], DC, [], ["wM"])
                gainbc = sbuf(p3, "gainbc", [128, 512], F32)
                gbbc = sbuf(p3, "gbbc", [128, 8], F32)
                vld = sbuf(p3, "vld", [128, NB], F32)
                vldb = sbuf(p3, "vldb", [128, NB], BF16)
                dma("sp", gainbc[:], gainrow[0:1, :].partition_broadcast(128), [], ["gainbc"])
                dma("sp", gbbc[:], gbrow[0:1, :].partition_broadcast(128), [], ["gbbc"])
                dma("sp", vld[:], validT[:, :], [], ["vld"])
                cp(vldb[:], vld[:], ["vld"], ["vldb"])
                xt = sbuf(p3, "xt2", [128, DC, 512], F32)
                u = sbuf(p3, "u2", [128, DC, 512], BF16)
                sq = [sbuf(p3, "sq2_%d" % i, [128, 512], BF16) for i in range(3)]
                sqr = Rot([(sq[i], "sq%d" % i) for i in range(3)])
                lnt = sbuf(p3, "lnt2", [128, 512], F32)
                rbc = sbuf(p3, "rbc2", [128, 512], F32)
                pc = sbuf(p3, "pc", [128, 8, 515], F32)
                cacc = [sbuf(p3, "cacc%d" % i, [128, 512], F32) for i in range(2)]
                caccr = Rot([(cacc[i], "cacc%d" % i) for i in range(2)])
                ctmp = sbuf(p3, "ctmp", [128, 512], F32)
                qkT = sbuf(p3, "qkT", [128, 8, 512], BF16)
                vtm = sbuf(p3, "vtm", [128, 4, 512], BF16)
                og = sbuf(p3, "og", [128, 4, 512], BF16)
                gsbs = [sbuf(p3, "gsb%d" % i, [128, 8], F32) for i in range(4)]
                lfns = [sbuf(p3, "lfn%d" % i, [128, 4], F32) for i in range(4)]
                sms = [sbuf(p3, "sm%d" % i, [128, 64], F32) for i in range(4)]
                trilfs = [sbuf(p3, "trilf%d" % i, [128, 4, 128], F32) for i in range(4)]
                dTs = [sbuf(p3, "dT%d" % i, [128, 4, 128], BF16) for i in range(4)]
                sTs = [sbuf(p3, "sT%d" % i, [128, 4, 128], BF16) for i in range(4)]
                tmpn = sbuf(p3, "tmpn", [128, 4, 128], F32)
                num = sbuf(p3, "num", [128, 4, 128], F32)
                sqj = sbuf(p3, "sqj", [128, 128], F32)
                GO = sbuf(p3, "GO", [128, 512], F32)
                mout = sbuf(p3, "mout", [128, 512], BF16)
                kws = [sbuf(p3, "kw%d" % i, [128, 4, 128], BF16) for i in range(4)]
                Cst = sbuf(p3, "Cst", [128, 4, 128], F32)
                nst = sbuf(p3, "nst", [128, 4], F32)
                Cbf = sbuf(p3, "Cbf", [128, 4, 128], BF16)
                nbf = sbuf(p3, "nbf", [128, 4], BF16)
                OP("dve", lambda e: e.memset(pc[:], 0.0), [], ["pc"])
                OP("dve", lambda e: e.memset(Cst[:], 0.0), [], ["Cst"])
                OP("dve", lambda e: e.memset(nst[:], 0.0), [], ["nst"])
                OP("dve", lambda e: e.memset(Cbf[:], 0.0), [], ["Cbf"])
                OP("dve", lambda e: e.memset(nbf[:], 0.0), [], ["nbf"])
                psr = Rot([(PB[i], PK[i]) for i in range(7)])
                MQ, MK, MV, MO, MG = 0, 512, 1024, 1536, 2048
                pst = Rot([(PB[i], PK[i]) for i in (4, 5, 6)])

                for v in (range(NT) if "A2" in phases else []):
                    own = (v % 2 == 1)
                    j = v // 2
                    tsl = slice(v * 512, (v + 1) * 512)
                    dmac("sp", xt, xT[:, tsl], DC, [], ["xt"])
                    pss, pssk = psr.next()
                    for c in range(DC):
                        sqt, sqk = sqr.next()
                        act(sqt[:], xt[:, c, :], AF.Square, ["xt#%d" % c], [sqk])
                        mm(pss[:], onesb[:], sqt[:], c == 0, c == DC - 1, ["onesb", sqk], [pssk])
                    rms_scale(pss[:], D, rbc[:], lnt[:], [pssk], "rbc")
                    for c in range(DC):
                        stt(u[:, c, :], xt[:, c, :], cs[:, G_APRE + c:G_APRE + c + 1], rbc[:], ALU.mult, ALU.mult,
                            ["xt#%d" % c, "cs", "rbc"], ["u%d" % c])
                    UK = ["u%d" % c for c in range(DC)]
                    cp(pc[:, :, 0:3], pc[:, :, 512:515], ["pc"] + ["pc%d" % ct for ct in range(8)], ["pc"])
                    for ct in range(8):
                        ps, pk = psr.next()
                        for c in range(DC):
                            mm(ps[:], wM[:, c, ct * 128:(ct + 1) * 128], u[:, c, :], c == 0, c == DC - 1,
                               ["wM", "u%d" % c], [pk])
                        cp(pc[:, ct, 3:515], ps[:], [pk, "pc"], ["pc%d" % ct], eng="act")
                        if ct < 4 and not own:
                            continue
                        ca, cak = caccr.next()
                        w0 = C_CW + ct * 4
                        ceng = "pool" if (ct % 2 and os.environ.get("NOPOOL") is None) else "dve"
                        ts(ca[:], pc[:, ct, 0:512], cs[:, w0:w0 + 1], cs[:, C_CB + ct:C_CB + ct + 1], ALU.mult, ALU.add,
                           ["pc", "pc%d" % ct, "cs"], [cak], eng=ceng)
                        for tap in range(1, 4):
                            if ceng == "pool":
                                ts(ctmp[:], pc[:, ct, tap:tap + 512], cs[:, w0 + tap:w0 + tap + 1], None, ALU.mult, None,
                                   ["pc", "pc%d" % ct, "cs"], ["ctmp"], eng="pool")
                                tt(ca[:], ca[:], ctmp[:], ALU.add, [cak, "ctmp"], [cak], eng="pool")
                            else:
                                stt(ca[:], pc[:, ct, tap:tap + 512], cs[:, w0 + tap:w0 + tap + 1], ca[:], ALU.mult, ALU.add,
                                    ["pc", "pc%d" % ct, "cs", cak], [cak])
                        act(qkT[:, ct, :], ca[:], AF.Silu, [cak], ["qkT%d" % ct])
                    gps, gpk = PB[7], PK[7]
                    for bl in range(4):
                        bs = slice(bl * 128, (bl + 1) * 128)
                        ps, pk = psr.next()
                        for c in range(DC):
                            mm(ps[:], u[:, c, bs], wM[:, c, MV:MV + 512], c == 0, c == DC - 1, ["wM", "u%d" % c], [pk])
                        cp(vtm[:, bl, :], ps[:], [pk], ["vtm%d" % bl], eng="act")
                        if own:
                            ps, pk = psr.next()
                            for c in range(DC):
                                mm(ps[:], u[:, c, bs], wM[:, c, MO:MO + 512], c == 0, c == DC - 1, ["wM", "u%d" % c], [pk])
                            act(og[:, bl, :], ps[:], AF.Sigmoid, [pk], ["og%d" % bl])
                        for c in range(DC):
                            mm(gps[:, bl * 8:(bl + 1) * 8], u[:, c, bs], wM[:, c, MG:MG + 8], c == 0, c == DC - 1,
                               ["wM", "u%d" % c], [gpk])
                    def blk_pre(bl):
                        X = "_%d" % bl
                        gsb, lfn, sm, sT, kw = gsbs[bl], lfns[bl], sms[bl], sTs[bl], kws[bl]
                        trilf, trk = trilfs[bl], "trilf%d" % bl
                        dT, dtk = dTs[bl], "dT%d" % bl
                        bs = slice(bl * 128, (bl + 1) * 128)
                        tt(gsb[:], gps[:, bl * 8:(bl + 1) * 8], gbbc[:], ALU.add, [gpk, "gbbc"], ["gsb" + X])
                        act(sm[:, 0:4], gsb[:, 4:8], AF.Exp, ["gsb" + X], ["sm_e" + X], scale=-1.0)
                        act(lfn[:], sm[:, 0:4], AF.Ln, ["sm_e" + X], ["lfn" + X], bias=1.0)
                        cps, cpk = PB[bl], PK[bl]
                        mm(cps[:, 0:4], trif[:], lfn[:], True, True, ["trif", "lfn" + X], [cpk])
                        mm(cps[:, 4:8], onesf[:], lfn[:], True, True, ["onesf", "lfn" + X], [cpk])
                        ts(sm[:, 4:8], gsb[:, 0:4], LNK, None, ALU.add, None, ["gsb" + X], ["sm_bd" + X])
                        tt(sm[:, 4:8], sm[:, 4:8], cps[:, 0:4], ALU.add, ["sm_bd" + X, cpk], ["sm_bd" + X])
                        tt(sm[:, 8:12], sm[:, 4:8], cps[:, 4:8], ALU.subtract, ["sm_bd" + X, cpk], ["sm_wl" + X])
                        act(sm[:, 12:16], sm[:, 8:12], AF.Exp, ["sm_wl" + X], ["sm_w" + X])
                        act(sm[:, 16:20], cps[:, 4:8], AF.Exp, [cpk], ["sm_eg" + X], scale=-1.0)
                        act(sm[:, 20:24], cps[:, 0:4], AF.Exp, [cpk], ["sm_eb" + X], scale=-1.0)
                        if own:
                            tt(trilf[:], trif[:].unsqueeze(1).to_broadcast([128, 4, 128]),
                               lfn[:].unsqueeze(2).to_broadcast([128, 4, 128]), ALU.mult, ["trif", "lfn" + X], [trk])
                            bps, bpk = PB[bl], PK[bl]
                            for h in range(4):
                                mm(bps[:, h * 128:(h + 1) * 128], onesf[:], trilf[:, h, :], True, False, ["onesf", trk], [bpk])
                                mm(bps[:, h * 128:(h + 1) * 128], identf[:], mposf[:], False, True, ["identf", "mposf"], [bpk])
                            for h in range(4):
                                act(dT[:, h, :], bps[:, h * 128:(h + 1) * 128], AF.Exp, [bpk, "sm_bd" + X], [dtk + "h%d" % h],
                                    bias=sm[:, 4 + h:5 + h], scale=-1.0)
                            kps, kpk = PB[bl], PK[bl]
                            for h in range(4):
                                mm(kps[:, h * 128:(h + 1) * 128], qkT[:, 4 + h, bs], qkT[:, h, bs], True, True,
                                   ["qkT%d" % (4 + h), "qkT%d" % h], [kpk])
                            tt(sT[:].rearrange("p h j -> p (h j)"), kps[:], dT[:].rearrange("p h j -> p (h j)"), ALU.mult,
                               [kpk] + [dtk + "h%d" % h for h in range(4)], ["sT" + X])
                        tps, tpk = PB[bl], PK[bl]
                        for h in range(4):
                            mm(tps[:, h * 128:(h + 1) * 128], qkT[:, 4 + h, bs], identb[:], True, True,
                               ["qkT%d" % (4 + h), "identb"], [tpk])
                        tt(kw[:], tps[:].rearrange("p (h d) -> p h d", h=4),
                           sm[:, 12:16].unsqueeze(2).to_broadcast([128, 4, 128]), ALU.mult, [tpk, "sm_w" + X], ["kw" + X])

                    def blk_tail(bl):
                        X = "_%d" % bl
                        sm, sT, kw = sms[bl], sTs[bl], kws[bl]
                        gb = v * 4 + bl
                        bs = slice(bl * 128, (bl + 1) * 128)
                        tok0 = j * 512 + bl * 128
                        if own:
                            ips, ipk = pst.next()
                            eps_, epk = pst.next()
                            for h in range(4):
                                hs = slice(h * 128, (h + 1) * 128)
                                mm(ips[:, hs], sT[:, h, :], vtm[:, bl, hs], True, True, ["sT" + X, "vtm%d" % bl], [ipk])
                                mm(PB[7][:, 32 + h:33 + h], sT[:, h, :], vldb[:, gb:gb + 1], True, True, ["sT" + X, "vldb"], [PK[7]])
                                mm(eps_[:, hs], qkT[:, h, bs], Cbf[:, h, :], True, True, ["qkT%d" % h, "Cbf"], [epk])
                                mm(PB[7][:, 36 + h:37 + h], qkT[:, h, bs], nbf[:, h:h + 1], True, True, ["qkT%d" % h, "nbf"], [PK[7]])
                        ups, upk = pst.next()
                        for h in range(4):
                            hs = slice(h * 128, (h + 1) * 128)
                            mm(ups[:, hs], kw[:, h, :], vtm[:, bl, hs], True, True, ["kw" + X, "vtm%d" % bl], [upk])
                            mm(PB[7][:, 40 + h:41 + h], kw[:, h, :], vldb[:, gb:gb + 1], True, True, ["kw" + X, "vldb"], [PK[7]])
                        tt(Cst[:], Cst[:], sm[:, 16:20].unsqueeze(2).to_broadcast([128, 4, 128]), ALU.mult, ["Cst", "sm_eg" + X], ["Cst"])
                        tt(Cst[:].rearrange("p h d -> p (h d)"), Cst[:].rearrange("p h d -> p (h d)"), ups[:], ALU.add,
                           ["Cst", upk], ["Cst"])
                        tt(nst[:], nst[:], sm[:, 16:20], ALU.mult, ["nst", "sm_eg" + X], ["nst"])
                        tt(nst[:], nst[:], PB[7][:, 40:44], ALU.add, ["nst", PK[7]], ["nst"])
                        cp(Cbf[:], Cst[:], ["Cst"], ["Cbf"], eng="act")
                        cp(nbf[:], nst[:], ["nst"], ["nbf"])
                        if not own:
                            return
                        tt(sm[:, 24:28], PB[7][:, 36:40], sm[:, 20:24], ALU.mult, [PK[7], "sm_eb" + X], ["sm_d1" + X])
                        tt(sm[:, 28:32], sm[:, 24:28], PB[7][:, 32:36], ALU.add, ["sm_d1" + X, PK[7]], ["sm_den" + X])
                        act(sm[:, 32:36], sm[:, 28:32], AF.Abs, ["sm_den" + X], ["sm_abs" + X])
                        ts(sm[:, 32:36], sm[:, 32:36], 1.0, None, ALU.max, None, ["sm_abs" + X], ["sm_abs" + X])
                        OP("dve", lambda e: e.reciprocal(out=sm[:, 36:40], in_=sm[:, 32:36]), ["sm_abs" + X], ["sm_rden" + X])
                        tt(tmpn[:], eps_[:].rearrange("p (h d) -> p h d", h=4),
                           sm[:, 20:24].unsqueeze(2).to_broadcast([128, 4, 128]), ALU.mult, [epk, "sm_eb" + X], ["tmpn"])
                        tt(num[:].rearrange("p h d -> p (h d)"), ips[:], tmpn[:].rearrange("p h d -> p (h d)"), ALU.add,
                           [ipk, "tmpn"], ["num"])
                        OP("dve", lambda e: e.memset(sm[:, 40:44], 0.0), [], ["sm_ss%d" % h + X for h in range(4)])
                        for h in range(4):
                            act(sqj[:], num[:, h, :], AF.Square, ["num"], ["sqj", "sm_ss%d" % h + X], accum_out=sm[:, 40 + h:41 + h])
                        tt(sm[:, 44:48], sm[:, 36:40], sm[:, 36:40], ALU.mult, ["sm_rden" + X], ["sm_r2" + X])
                        tt(sm[:, 44:48], sm[:, 44:48], sm[:, 40:44], ALU.mult, ["sm_r2" + X] + ["sm_ss%d" % h + X for h in range(4)], ["sm_r2" + X])
                        act(sm[:, 48:52], sm[:, 44:48], AF.Ln, ["sm_r2" + X], ["sm_ln" + X], bias=EPS, scale=1.0 / 128)
                        act(sm[:, 52:56], sm[:, 48:52], AF.Exp, ["sm_ln" + X], ["sm_rs" + X], scale=-0.5)
                        tt(sm[:, 56:60], sm[:, 52:56], sm[:, 36:40], ALU.mult, ["sm_rs" + X, "sm_rden" + X], ["sm_f" + X])
                        tt(GO[:], gainbc[:], og[:, bl, :], ALU.mult, ["gainbc", "og%d" % bl], ["GO"], eng=("dve" if os.environ.get("NOPOOL") else "pool"))
                        tt(tmpn[:], num[:], sm[:, 56:60].unsqueeze(2).to_broadcast([128, 4, 128]), ALU.mult,
                           ["num", "sm_f" + X], ["tmpn"])
                        tt(mout[:], tmpn[:].rearrange("p h d -> p (h d)"), GO[:], ALU.mult, ["tmpn", "GO"], ["mout"])
                        tps, tpk = pst.next()
                        for h in range(4):
                            mm(tps[:, h * 128:(h + 1) * 128], mout[:, h * 128:(h + 1) * 128], identb[:], True, True,
                               ["mout", "identb"], [tpk])
                        cp(catB[:, :, tok0:tok0 + 128], tps[:].rearrange("p (h t) -> p h t", h=4), [tpk],
                           ["catB%d_%d" % (j, bl)], eng="act")

                    streams = []
                    for bl in range(4):
                        sch.rec = []
                        blk_pre(bl)
                        streams.append(sch.rec)
                        sch.rec = None
                    sch.replay_rr(streams)
                    for bl in range(4):
                        blk_tail(bl)
            sch.barrier()

            with ExitStack() as p4:
                wo = sbuf(p4, "wo", [128, DC, D], BF16)
                dmac("pool", wo, w_out[:, :], DC, [], ["wo"])
                xt = sbuf(p4, "xt4", [128, DC, 512], F32)
                mix = sbuf(p4, "mix", [128, DC, 512], F32)
                sq = [sbuf(p4, "sq4_%d" % i, [128, 512], BF16) for i in range(3)]
                sqr = Rot([(sq[i], "sq%d" % i) for i in range(3)])
                lnt = sbuf(p4, "lnt4", [128, 512], F32)
                rbc = sbuf(p4, "rbc4", [128, 512], F32)
                tm = [sbuf(p4, "tm4_%d" % i, [128, 512], F32) for i in range(2)]
                tmr = Rot([(tm[i], "tm%d" % i) for i in range(2)])
                h1 = sbuf(p4, "h1_4", [128, DC, 512], F32)
                psr = Rot([(PB[i], PK[i]) for i in range(6)])
                for j in (range(NO) if "Bp" in phases else []):
                    v = 2 * j + 1
                    tsl = slice(v * 512, (v + 1) * 512)
                    osl = slice(j * 512, (j + 1) * 512)
                    dmac("sp", xt, xT[:, tsl], DC, [], ["xt"])
                    pss, pssk = PB[7], PK[7]
                    for ct in range(DC):
                        ps, pk = psr.next()
                        for f in range(8):
                            rhs = catA[:, f, osl] if f < 4 else catB[:, f - 4, osl]
                            mm(ps[:], wo[:, f, ct * 128:(ct + 1) * 128], rhs, f == 0, f == 7, ["wo"], [pk])
                        sqt, sqk = sqr.next()
                        act(sqt[:], ps[:], AF.Square, [pk], [sqk])
                        cp(mix[:, ct, :], ps[:], [pk], ["mix%d" % ct])
                        mm(pss[:], onesb[:], sqt[:], ct == 0, ct == DC - 1, ["onesb", sqk], [pssk])
                    rms_scale(pss[:], D, rbc[:], lnt[:], [pssk], "rbc")
                    for ct in range(DC):
                        t_, tk = tmr.next()
                        tt(t_[:], mix[:, ct, :], rbc[:], ALU.mult, ["mix%d" % ct, "rbc"], [tk])
                        stt(h1[:, ct, :], t_[:], cs[:, G_APOST + ct:G_APOST + ct + 1], xt[:, ct, :], ALU.mult, ALU.add,
                            [tk, "cs", "xt"], ["h1"])
                    dmac_out("sp", h1d[:, osl], h1, DC, ["h1"], ["h1d%d" % j])
                    if debug:
                        dmac_out("sp", dbg_h1[:, osl], h1, DC, ["h1"], [])
                if debug:
                    for hh in range(4):
                        dma("pool", dbg_cat[:, hh * SO:(hh + 1) * SO], catA[:, hh, :], [], [])
                        dma("pool", dbg_cat[:, (4 + hh) * SO:(5 + hh) * SO], catB[:, hh, :], [], [])
            sch.barrier()

        with ExitStack() as p5:
            wpg = sbuf(p5, "wpg", [128, DC, D], BF16)
            wpp = sbuf(p5, "wpp", [128, 2, D], BF16)
            dmac("pool", wpg, w_pg[:, :], DC, [], ["wpg"])
            dmac("pool", wpp, w_pp[:, :], 2, [], ["wpp"])
            wg = [sbuf(p5, "wg%d" % i, [128, DC, 256], BF16) for i in range(3)]
            wgr = Rot([(wg[i], "wg%d" % i) for i in range(3)])
            wd = [sbuf(p5, "wd%d" % i, [128, NFF, 128], BF16) for i in range(2)]
            wdr = Rot([(wd[i], "wd%d" % i) for i in range(2)])
            h1 = sbuf(p5, "h1_5", [128, DC, 512], F32)
            fo = sbuf(p5, "fo", [128, DC, 512], F32)
            fn = sbuf(p5, "fn", [128, DC, 512], BF16)
            actT = sbuf(p5, "actT", [128, NFF, 512], BF16)
            sq = [sbuf(p5, "sq5_%d" % i, [128, 512], BF16) for i in range(3)]
            sqr = Rot([(sq[i], "sq%d" % i) for i in range(3)])
            lnt = sbuf(p5, "lnt5", [128, 512], F32)
            rbc = sbuf(p5, "rbc5", [128, 512], F32)
            tm = [sbuf(p5, "tm5_%d" % i, [128, 512], F32) for i in range(3)]
            tmr = Rot([(tm[i], "tm%d" % i) for i in range(3)])
            pt_f = sbuf(p5, "pt_f", [128, 2, 512], F32)
            pt_b = sbuf(p5, "pt_b", [128, 2, 512], BF16)
            psr = Rot([(PB[i], PK[i]) for i in range(7)])
            for j in (range(NO) if "C" in phases else []):
                osl = slice(j * 512, (j + 1) * 512)
                dmac("sp", h1, h1d[:, osl], DC, ["h1d%d" % j], ["h1"])
                dmac("sp", pt_f, pT[:, osl], 2, [], ["pt_f"])
                cp(pt_b[:], pt_f[:], ["pt_f"], ["pt_b"])
                pss, pssk = PB[7], PK[7]
                for c in range(DC):
                    sqt, sqk = sqr.next()
                    act(sqt[:], h1[:, c, :], AF.Square, ["h1"], [sqk])
                    mm(pss[:], onesb[:], sqt[:], c == 0, c == DC - 1, ["onesb", sqk], [pssk])
                rms_scale(pss[:], D, rbc[:], lnt[:], [pssk], "rbc")
                for c in range(DC):
                    stt(fn[:, c, :], h1[:, c, :], cs[:, G_FPRE + c:G_FPRE + c + 1], rbc[:], ALU.mult, ALU.mult,
                        ["h1", "cs", "rbc"], ["fn%d" % c])
                for n in range(NFF):
                    w_, wk = wgr.next()
                    dma("sp", w_[:].rearrange("p c f -> p (c f)"), wgu_s[n, :, :], [], [wk])
                    pg, pgk = psr.next()
                    pu, puk = psr.next()
                    for c in range(DC):
                        mm(pg[:], w_[:, c, 0:128], fn[:, c, :], c == 0, c == DC - 1, [wk, "fn%d" % c], [pgk])
                    for c in range(DC):
                        mm(pu[:], w_[:, c, 128:256], fn[:, c, :], c == 0, c == DC - 1, [wk, "fn%d" % c], [puk])
                    t_, tk = tmr.next()
                    act(t_[:], pg[:], AF.Silu, [pgk], [tk])
                    tt(actT[:, n, :], pu[:], t_[:], ALU.mult, [puk, tk], ["actT%d" % n])
                pss, pssk = PB[7], PK[7]
                for ct in range(DC):
                    w_, wk = wdr.next()
                    dma("sp", w_[:].rearrange("p n f -> p (n f)"), wdn_s[ct, :, :], [], [wk])
                    ps, pk = psr.next()
                    for n in range(NFF):
                        mm(ps[:], w_[:, n, :], actT[:, n, :], n == 0, n == NFF - 1, [wk, "actT%d" % n], [pk])
                    sqt, sqk = sqr.next()
                    act(sqt[:], ps[:], AF.Square, [pk], [sqk])
                    cp(fo[:, ct, :], ps[:], [pk], ["fo%d" % ct])
                    mm(pss[:], onesb[:], sqt[:], ct == 0, ct == DC - 1, ["onesb", sqk], [pssk])
                rms_scale(pss[:], D, rbc[:], lnt[:], [pssk], "rbc")
                for ct in range(DC):
                    t_, tk = tmr.next()
                    tt(t_[:], fo[:, ct, :], rbc[:], ALU.mult, ["fo%d" % ct, "rbc"], [tk])
                    stt(h1[:, ct, :], t_[:], cs[:, G_FPOST + ct:G_FPOST + ct + 1], h1[:, ct, :], ALU.mult, ALU.add,
                        [tk, "cs", "h1"], ["h1"])
                    cp(fn[:, ct, :], h1[:, ct, :], ["h1"], ["fn%d" % ct], eng="act")
                for ct in range(DC):
                    pg, pgk = psr.next()
                    pp, ppk = psr.next()
                    for c in range(DC):
                        mm(pg[:], wpg[:, c, ct * 128:(ct + 1) * 128], fn[:, c, :], c == 0, c == DC - 1, ["wpg", "fn%d" % c], [pgk])
                    for c in range(2):
                        mm(pp[:], wpp[:, c, ct * 128:(ct + 1) * 128], pt_b[:, c, :], c == 0, c == 1, ["wpp", "pt_b"], [ppk])
                    t_, tk = tmr.next()
                    act(t_[:], pg[:], AF.Sigmoid, [pgk], [tk])
                    tt(t_[:], pp[:], t_[:], ALU.mult, [ppk, tk], [tk])
                    tt(fo[:, ct, :], t_[:], h1[:, ct, :], ALU.add, [tk, "h1"], ["fo%d" % ct])
                dmac_out("sp", yT[:, osl], fo, DC, ["fo%d" % c for c in range(DC)], ["yT%d" % j])
            sch.barrier()
        sch.emit()
    return nc


def _consts():
    cmv = np.zeros((128, 4 * 512 + 4 * 128), np.float32)
    k = np.arange(128)[:, None]
    q = np.arange(512)[None, :]
    for a in range(4):
        cmv[:, a * 512:(a + 1) * 512] = np.where(k + a * 128 <= q, 0.0, NEG)
    t = np.arange(128)[:, None]
    jj = np.arange(128)[None, :]
    cmv[:, 2048:2176] = (t <= jj).astype(np.float32)
    cmv[:, 2176:2304] = np.where(t > jj, -NEG, 0.0)
    cmv[:, 2304:2432] = np.eye(128, dtype=np.float32)
    return cmv


def prep(inputs, S, ncores):
    f = lambda a: np.ascontiguousarray(np.asarray(a, dtype=np.float32))
    x = np.asarray(inputs["x"], np.float32)
    p = np.asarray(inputs["p"], np.float32)[0]
    positions = np.asarray(inputs["positions"]).astype(np.int32)
    SO = S // 2
    NB = S // 128
    cst = np.zeros((128, 96), np.float32)
    cst[:, 0:8] = f(inputs["attn_pre_norm"])[0].reshape(8, 128).T
    cst[:, 8:16] = f(inputs["attn_post_norm"])[0].reshape(8, 128).T
    cst[:, 16:24] = f(inputs["ffn_pre_norm"])[0].reshape(8, 128).T
    cst[:, 24:32] = f(inputs["ffn_post_norm"])[0].reshape(8, 128).T
    cst[:, 32:34] = f(inputs["q_norm"])[0].reshape(2, 128).T
    cst[:, 34:35] = f(inputs["kv_norm"])[0].reshape(1, 128).T
    cw = f(inputs["conv_w"])[0]
    cst[:, 35:67] = cw.reshape(4, 8, 128).transpose(2, 1, 0).reshape(128, 32)
    cst[:, 67:75] = f(inputs["conv_b"])[0].reshape(8, 128).T
    inv = (10000.0 ** (-np.arange(0, 64, 2, dtype=np.float32) / 64)).astype(np.float32)
    cst[0:64, 75] = np.concatenate([inv, inv]) / np.float32(2 * np.pi)
    gainrow = f(inputs["mlstm_norm"])[0].reshape(1, 512)
    gbrow = np.concatenate([f(inputs["gate_bias_i"])[0], f(inputs["gate_bias_f"])[0]]).reshape(1, 8)
    cmv = _consts()
    wg = f(inputs["w_gate"])[0].reshape(8, 128, NFF, 128)
    wu = f(inputs["w_up"])[0].reshape(8, 128, NFF, 128)
    wgu = np.ascontiguousarray(np.concatenate([wg, wu], axis=3).transpose(2, 1, 0, 3))
    wdn = np.ascontiguousarray(f(inputs["w_down"])[0].reshape(NFF, 128, 8, 128).transpose(2, 1, 0, 3))
    shared = {
        "cst": cst, "gainrow": gainrow, "gbrow": gbrow, "cm": cmv,
        "w_in": f(inputs["w_in"])[0], "w_uq": f(inputs["w_uq"])[0], "w_ukv": f(inputs["w_ukv"])[0],
        "w_out": f(inputs["w_out"])[0], "wgu": wgu, "wdn": wdn,
        "w_pp": f(inputs["w_ple_proj"])[0], "w_pg": f(inputs["w_ple_gate"])[0],
    }
    maps = []
    own_idx = []
    for core in range(ncores):
        b, par = core // 2, core % 2
        xb = x[b]
        if par == 1:
            xv = xb
            pv = positions[b]
            valid = np.ones(S, np.float32)
        else:
            xv = np.concatenate([np.zeros((512, D), np.float32), xb[:S - 512]], axis=0)
            pv = np.concatenate([np.zeros(512, np.int32), positions[b][:S - 512]])
            valid = np.concatenate([np.zeros(512, np.float32), np.ones(S - 512, np.float32)])
        view_tok = np.arange(S).reshape(S // 512, 512)[1::2].reshape(-1)
        glob = view_tok if par == 1 else view_tok - 512
        own_idx.append((b, glob))
        m = dict(shared)
        m["xT"] = np.ascontiguousarray(xv.T)
        m["pos"] = np.ascontiguousarray(pv.reshape(1, S))
        m["pT"] = np.ascontiguousarray(p[b][glob].T)
        m["kflag"] = np.where(valid > 0, 0.0, NEG).astype(np.float32).reshape(1, S)
        m["validT"] = np.ascontiguousarray(valid.reshape(NB, 128).T)
        maps.append(m)
    return maps, own_idx


_NC_CACHE = {}


def kernel(**inputs):
    x = np.asarray(inputs["x"])
    B, S, _ = x.shape
    ncores = 2 * B
    if S not in _NC_CACHE:
        _NC_CACHE[S] = build(S)
    nc = _NC_CACHE[S]
    maps, own_idx = prep(inputs, S, ncores)
    res = run_bass_kernel_spmd(nc, maps, core_ids=list(range(ncores)))
    out = np.zeros((B, S, D), np.float32)
    for core in range(ncores):
        b, glob = own_idx[core]
        out[b, glob, :] = np.asarray(res.results[core]["yT"]).T
    return out
```

```python
import math
import os
from contextlib import ExitStack

import numpy as np
import concourse.bass as bass
import concourse.mybir as mybir
from concourse.bass_utils import run_bass_kernel_spmd

F32 = mybir.dt.float32
BF16 = mybir.dt.bfloat16
I32 = mybir.dt.int32
AF = mybir.ActivationFunctionType
ALU = mybir.AluOpType

D = 1024
DC = 8
DIN = 2504
DFF = 2816
NFF = 22
PLE = 256
EPS = 1e-6
QSCALE = 192.0 ** -0.5
LNK = math.log(128.0 ** -0.5)
TWO_PI = 2.0 * math.pi * (1.0 - 1e-6)
NEG = -30000.0

EPOCH = 12000
NDMA = 4


class Tok:
    __slots__ = ("sem", "val", "implied", "eng")

    def __init__(self, sem, val, implied, eng):
        self.sem = sem
        self.val = val
        self.implied = implied
        self.eng = eng


class Sched:
    ENGS = ("pe", "act", "dve", "pool", "sp")

    def __init__(self, nc, stack):
        self.nc = nc
        self.stack = stack
        self.ops = {e: [] for e in self.ENGS}
        self.cnt = {e: 0 for e in self.ENGS}
        self.esems = {e: [] for e in self.ENGS}
        self.known = {e: {} for e in self.ENGS}
        self.dsems = {}
        self.dcnt = {}
        self.dlast = {}
        self.ndma = {e: 0 for e in self.ENGS}
        self.last_write = {}
        self.readers = {}
        self.alias = {}
        self.rec = None

    def _expand(self, keys):
        out = []
        for k in keys:
            out.extend(self.alias.get(k, (k,)))
        return out

    def _newsem(self, name):
        return self.stack.enter_context(self.nc.semaphore(name))

    def _esem(self, eng, epoch):
        lst = self.esems[eng]
        while len(lst) <= epoch:
            lst.append(self._newsem("s_%s_%d" % (eng, len(lst))))
        return lst[epoch]

    def _need(self, eng, tok, waits):
        if tok is None:
            return
        kn = self.known[eng]
        if kn.get(tok.sem, 0) >= tok.val:
            return
        if tok.eng == "pe" and eng == "pe":
            return
        waits[tok.sem] = max(waits.get(tok.sem, 0), tok.val)
        kn[tok.sem] = tok.val
        for s, v in tok.implied.items():
            if kn.get(s, 0) < v:
                kn[s] = v

    def replay_rr(self, streams):
        idx = [0] * len(streams)
        left = sum(len(x) for x in streams)
        while left:
            for i, st in enumerate(streams):
                if idx[i] < len(st):
                    self.op(*st[idx[i]])
                    idx[i] += 1
                    left -= 1

    def op(self, eng, fn, reads=(), writes=(), dma=False):
        if self.rec is not None:
            self.rec.append((eng, fn, list(reads), list(writes), dma))
            return None
        reads = self._expand(reads)
        writes = self._expand(writes)
        waits = {}
        for k in reads:
            self._need(eng, self.last_write.get(k), waits)
            if k.startswith("pb"):
                for t in self.readers.get(k, ()):
                    if t.eng != eng:
                        self._need(eng, t, waits)
        for k in writes:
            self._need(eng, self.last_write.get(k), waits)
            for t in self.readers.get(k, ()):
                self._need(eng, t, waits)
        if dma:
            i = self.ndma[eng]
            self.ndma[eng] += 1
            key = (eng, i % NDMA)
            if key not in self.dsems:
                self.dsems[key] = self._newsem("d_%s_%d" % key)
                self.dcnt[key] = 0
            self._need(eng, self.dlast.get(key), waits)
            self.dcnt[key] += 1
            sem = self.dsems[key]
            val = 16 * self.dcnt[key]
            inc = 16
            implied = dict(self.known[eng])
        else:
            n = self.cnt[eng]
            self.cnt[eng] += 1
            sem = self._esem(eng, n // EPOCH)
            val = n % EPOCH + 1
            inc = 1
            implied = dict(self.known[eng])
            for ep in range(n // EPOCH):
                implied[self.esems[eng][ep]] = EPOCH
        tok = Tok(sem, val, implied, eng)
        if dma:
            self.dlast[key] = tok
        self.ops[eng].append((list(waits.items()), fn, sem, inc))
        for k in reads:
            self.readers.setdefault(k, []).append(tok)
        for k in writes:
            self.last_write[k] = tok
            self.readers[k] = []
        return tok

    def barrier(self):
        toks = []
        for e in self.ENGS:
            n = self.cnt[e]
            if n:
                imp = {self.esems[e][ep]: EPOCH for ep in range((n - 1) // EPOCH)}
                toks.append(Tok(self._esem(e, (n - 1) // EPOCH), (n - 1) % EPOCH + 1, imp, e))
        toks += list(self.dlast.values())
        for e in self.ENGS:
            waits = {}
            for t in toks:
                self._need(e, t, waits)
            if waits:
                self.ops[e].append((list(waits.items()), None, None, 0))
        self.last_write = {}
        self.readers = {}

    def emit(self):
        nc = self.nc
        engmap = {"pe": "tensor", "act": "scalar", "dve": "vector", "pool": "gpsimd", "sp": "sync"}
        with nc.Block() as block:
            for e in self.ENGS:
                ops = self.ops[e]
                if not ops:
                    continue

                def body(engobj, ops=ops):
                    for waits, fn, sem, inc in ops:
                        for s, v in waits:
                            engobj.wait_ge(s, v)
                        if fn is not None:
                            fn(engobj).then_inc(sem, inc)

                getattr(block, engmap[e])(body)


class Rot:
    def __init__(self, items):
        self.items = items
        self.i = 0

    def next(self):
        it = self.items[self.i % len(self.items)]
        self.i += 1
        return it


def build(S, debug=False, phases=("A1", "B", "A2", "Bp", "C")):
    NT = S // 512
    NO = NT // 2
    SO = S // 2
    NB = S // 128
    nc = bass.Bass("TRN2", target_bir_lowering=False)
    dr = lambda n, s, d, k="ExternalInput": nc.dram_tensor(n, s, d, kind=k)
    xT = dr("xT", [D, S], F32)
    pos = dr("pos", [1, S], I32)
    pT = dr("pT", [PLE, SO], F32)
    cst = dr("cst", [128, 96], F32)
    gainrow = dr("gainrow", [1, 512], F32)
    gbrow = dr("gbrow", [1, 8], F32)
    kflag = dr("kflag", [1, S], F32)
    validT = dr("validT", [128, NB], F32)
    cm = dr("cm", [128, 4 * 512 + 4 * 128], F32)
    w_in = dr("w_in", [D, DIN], F32)
    w_uq = dr("w_uq", [256, 768], F32)
    w_ukv = dr("w_ukv", [128, 1024], F32)
    w_out = dr("w_out", [D, D], F32)
    wgu = dr("wgu", [NFF, 128, DC, 256], F32)
    wdn = dr("wdn", [DC, 128, NFF, 128], F32)
    w_pp = dr("w_pp", [PLE, D], F32)
    w_pg = dr("w_pg", [D, D], F32)
    yT = dr("yT", [D, SO], F32, "ExternalOutput")
    h1d = dr("h1d", [D, SO], F32, "Internal")
    wgu_s = dr("wgu_s", [NFF, 128, DC * 256], BF16, "Internal")
    wdn_s = dr("wdn_s", [DC, 128, NFF * 128], BF16, "Internal")
    if debug:
        dbg_cat = dr("dbg_cat", [128, 8 * SO], F32, "ExternalOutput")
        dbg_h1 = dr("dbg_h1", [D, SO], F32, "ExternalOutput")

    G_APRE, G_APOST, G_FPRE, G_FPOST = 0, 8, 16, 24
    C_QN, C_KVN, C_CW, C_CB, C_INVF = 32, 34, 35, 67, 75

    with ExitStack() as top:
        sch = Sched(nc, top)
        OP = sch.op

        def sbuf(st, n, s, d):
            return st.enter_context(nc.sbuf_tensor(n, s, d))

        def mm(out, lhsT, rhs, start, stop, reads, writes):
            OP("pe", lambda e: e.matmul(out, lhsT, rhs, start=start, stop=stop), reads, writes)

        def act(out, in_, func, reads, writes, eng="act", **kw):
            OP(eng, lambda e: e.activation(out=out, in_=in_, func=func, **kw), reads, writes)

        def tt(out, in0, in1, op, reads, writes, eng="dve"):
            OP(eng, lambda e: e.tensor_tensor(out=out, in0=in0, in1=in1, op=op), reads, writes)

        def ts(out, in0, s1, s2, op0, op1, reads, writes, eng="dve"):
            if s2 is None:
                OP(eng, lambda e: e.tensor_scalar(out=out, in0=in0, scalar1=s1, scalar2=None, op0=op0), reads, writes)
            else:
                OP(eng, lambda e: e.tensor_scalar(out=out, in0=in0, scalar1=s1, scalar2=s2, op0=op0, op1=op1), reads, writes)

        def stt(out, in0, sc, in1, op0, op1, reads, writes, eng="dve"):
            OP(eng, lambda e: e.scalar_tensor_tensor(out=out, in0=in0, scalar=sc, in1=in1, op0=op0, op1=op1), reads, writes)

        def cp(out, in_, reads, writes, eng="dve"):
            if eng == "act":
                OP("act", lambda e: e.copy(out=out, in_=in_), reads, writes)
            else:
                OP(eng, lambda e: e.tensor_copy(out=out, in_=in_), reads, writes)

        def dma(eng, out, in_, reads, writes):
            OP(eng, lambda e: e.dma_start(out=out, in_=in_), reads, writes, dma=True)

        def dmac(eng, dst, src, nch, reads, writes):
            for w in writes:
                sch.alias.setdefault(w, ["%s#%d" % (w, c) for c in range(nch)])
            for c in range(nch):
                dma(eng, dst[:, c, :], src[c * 128:(c + 1) * 128, :], reads, ["%s#%d" % (w, c) for w in writes])

        def dmac_out(eng, dst, src, nch, reads, writes):
            for c in range(nch):
                dma(eng, dst[c * 128:(c + 1) * 128, :], src[:, c, :], reads, writes)

        PB = [top.enter_context(nc.psum_tensor("pb%d" % i, [128, 512], F32)) for i in range(8)]
        PK = ["pb%d" % i for i in range(8)]

        cs = sbuf(top, "cs", [128, 96], F32)
        identb = sbuf(top, "identb", [128, 128], BF16)
        identf = sbuf(top, "identf", [128, 128], F32)
        onesb = sbuf(top, "onesb", [128, 128], BF16)
        onesf = sbuf(top, "onesf", [128, 128], F32)
        trif = sbuf(top, "trif", [128, 128], F32)
        mposf = sbuf(top, "mposf", [128, 128], F32)
        catA = sbuf(top, "catA", [128, 4, SO], BF16)
        dma("sp", cs[:], cst[:, :], [], ["cs"])
        dma("sp", trif[:], cm[:, 2048:2176], [], ["trif"])
        dma("sp", mposf[:], cm[:, 2176:2304], [], ["mposf"])
        dma("sp", identf[:], cm[:, 2304:2432], [], ["identf"])
        dma("pool", identb[:], cm[:, 2304:2432], [], ["identb"])
        OP("dve", lambda e: e.memset(onesb[:], 1.0), [], ["onesb"])
        OP("dve", lambda e: e.memset(onesf[:], 1.0), [], ["onesf"])

        def rms_scale(ps_ap, n, out_ap, tmp_ap, rk, wk):
            act(tmp_ap, ps_ap, AF.Ln, rk, [wk + "_t"], bias=EPS, scale=1.0 / n)
            act(out_ap, tmp_ap, AF.Exp, [wk + "_t"], [wk], scale=-0.5)

        with ExitStack() as pa:
            wA = sbuf(pa, "wA", [128, DC, 448], BF16)
            wArot = sbuf(pa, "wArot", [128, DC, 64], BF16)
            wuq = sbuf(pa, "wuq", [128, 2, 768], BF16)
            wuqrot = sbuf(pa, "wuqrot", [128, 2, 4, 64], BF16)
            wukv = sbuf(pa, "wukv", [128, 1024], BF16)
            cqn = sbuf(pa, "cqn", [128, 2, SO], BF16)
            ckvn = sbuf(pa, "ckvn", [128, S], BF16)
            kpe = sbuf(pa, "kpe", [65, S], BF16)
            qpe = sbuf(pa, "qpe", [65, 4, SO], BF16)
            m4 = sbuf(pa, "m4", [128, 4, 512], BF16)
            dma("pool", m4[:].rearrange("p a q -> p (a q)"), cm[:, 0:2048], [], ["m4"])
            stg = [sbuf(pa, "stg%d" % i, [128, 448], F32) for i in range(2)]
            sch.alias["wA"] = ["wA#%d" % c for c in range(DC)]
            for c in range(DC):
                dma("sp", stg[c % 2][:], w_in[c * 128:(c + 1) * 128, 0:448], [], ["stg%d" % (c % 2)])
                cp(wA[:, c, :], stg[c % 2][:], ["stg%d" % (c % 2)], ["wA#%d" % c], eng=("act" if c % 2 else "dve"))
            dmac("pool", wuq, w_uq[:, :], 2, [], ["wuq"])
            dma("pool", wukv[:], w_ukv[:, :], [], ["wukv"])
            dma("pool", kpe[64:65, :], kflag[:, :], [], ["kpe_flag"])
            OP("dve", lambda e: e.memset(qpe[64:65, :, :], 1.0), [], ["qpe_one"])
            for n in range(NFF):
                dma("pool", wgu_s[n, :, :], wgu[n, :, :, :].rearrange("p c f -> p (c f)"), [], ["wgus%d" % n])
            for ct in range(DC):
                dma("pool", wdn_s[ct, :, :], wdn[ct, :, :, :].rearrange("p n f -> p (n f)"), [], ["wdns%d" % ct])
            OP("act", lambda e: e.mul(out=wArot[:, :, 0:32], in_=wA[:, :, 416:448], mul=-1.0), ["wA"], ["wArot_a"])
            cp(wArot[:, :, 32:64], wA[:, :, 384:416], ["wA"], ["wArot_b"])
            for h in range(4):
                b0 = h * 192 + 128
                OP("act", lambda e, h=h, b0=b0: e.mul(out=wuqrot[:, :, h, 0:32], in_=wuq[:, :, b0 + 32:b0 + 64], mul=-1.0),
                   ["wuq"], ["wuqrot_a%d" % h])
                cp(wuqrot[:, :, h, 32:64], wuq[:, :, b0:b0 + 32], ["wuq"], ["wuqrot_b%d" % h])
            WROT = ["wArot_a", "wArot_b"]
            WQROT = ["wuqrot_a%d" % h for h in range(4)] + ["wuqrot_b%d" % h for h in range(4)]

            with ExitStack() as p1:
                xt = sbuf(p1, "xt", [128, DC, 512], F32)
                u = sbuf(p1, "u", [128, DC, 512], BF16)
                sq = [sbuf(p1, "sq%d" % i, [128, 512], BF16) for i in range(3)]
                sqr = Rot([(sq[i], "sq%d" % i) for i in range(3)])
                lnt = sbuf(p1, "lnt", [128, 512], F32)
                rbc = sbuf(p1, "rbc", [128, 512], F32)
                r2 = sbuf(p1, "r2", [128, 512], F32)
                raw = sbuf(p1, "raw", [128, 2, 512], F32)
                pi = sbuf(p1, "pi", [64, 512], I32)
                tr = [sbuf(p1, "tr%d" % i, [64, 512], F32) for i in range(4)]
                tri_i = sbuf(p1, "tri_i", [64, 512], I32)
                sinT = sbuf(p1, "sinT", [64, 512], F32)
                cosT = sbuf(p1, "cosT", [64, 512], F32)
                t1 = sbuf(p1, "t1", [64, 512], F32)
                t2 = sbuf(p1, "t2", [64, 512], F32)
                psr = Rot([(PB[i], PK[i]) for i in range(8)])

                for v in (range(NT) if "A1" in phases else []):
                    own = (v % 2 == 1)
                    j = v // 2
                    tsl = slice(v * 512, (v + 1) * 512)
                    osl = slice(j * 512, (j + 1) * 512)
                    dmac("sp", xt, xT[:, tsl], DC, [], ["xt"])
                    dma("sp", pi[:], pos[0:1, tsl].partition_broadcast(64), [], ["pi"])
                    pss, pssk = psr.next()
                    for c in range(DC):
                        sqt, sqk = sqr.next()
                        act(sqt[:], xt[:, c, :], AF.Square, ["xt#%d" % c], [sqk])
                        mm(pss[:], onesb[:], sqt[:], c == 0, c == DC - 1, ["onesb", sqk], [pssk])
                    rms_scale(pss[:], D, rbc[:], lnt[:], [pssk], "rbc")
                    for c in range(DC):
                        stt(u[:, c, :], xt[:, c, :], cs[:, G_APRE + c:G_APRE + c + 1], rbc[:], ALU.mult, ALU.mult,
                            ["xt#%d" % c, "cs", "rbc"], ["u%d" % c])
                    UK = ["u%d" % c for c in range(DC)]
                    LVL = int(os.environ.get("A1LVL", "9"))
                    if LVL < 2:
                        continue
                    cp(tr[0][:], pi[:], ["pi"], ["tr0"])
                    ts(tr[1][:], tr[0][:], cs[0:64, C_INVF:C_INVF + 1], 0.0, ALU.mult, ALU.add, ["tr0", "cs"], ["tr1"])
                    cp(tri_i[:], tr[1][:], ["tr1"], ["tri_i"])
                    cp(tr[2][:], tri_i[:], ["tri_i"], ["tr2"])
                    tt(tr[3][:], tr[1][:], tr[2][:], ALU.subtract, ["tr1", "tr2"], ["tr3"])
                    act(sinT[:], tr[3][:], AF.Sin, ["tr3"], ["sinT"], scale=TWO_PI)
                    ts(tr[1][:], tr[1][:], 0.25, None, ALU.add, None, ["tr1"], ["tr1"])
                    cp(tri_i[:], tr[1][:], ["tr1"], ["tri_i"])
                    cp(tr[2][:], tri_i[:], ["tri_i"], ["tr2"])
                    tt(tr[3][:], tr[1][:], tr[2][:], ALU.subtract, ["tr1", "tr2"], ["tr3"])
                    act(cosT[:], tr[3][:], AF.Sin, ["tr3"], ["cosT"], scale=TWO_PI)
                    if LVL < 3:
                        continue
                    pkv, pkvk = psr.next()
                    for c in range(DC):
                        mm(pkv[:], wA[:, c, 256:384], u[:, c, :], c == 0, c == DC - 1, ["wA", "u%d" % c], [pkvk])
                    sqt, sqk = sqr.next()
                    act(sqt[:], pkv[:], AF.Square, [pkvk], [sqk])
                    cp(raw[:, 0, :], pkv[:], [pkvk], ["raw0"], eng="act")
                    ps2, ps2k = psr.next()
                    mm(ps2[:], onesb[:], sqt[:], True, True, ["onesb", sqk], [ps2k])
                    rms_scale(ps2[:], 128, r2[:], lnt[:], [ps2k], "r2")
                    stt(ckvn[:, tsl], raw[:, 0, :], cs[:, C_KVN:C_KVN + 1], r2[:], ALU.mult, ALU.mult,
                        ["raw0", "cs", "r2"], ["ckvn%d" % v])
                    if LVL < 4:
                        continue
                    pkr, pkrk = psr.next()
                    pkq, pkqk = psr.next()
                    for c in range(DC):
                        mm(pkr[0:64, :], wA[:, c, 384:448], u[:, c, :], c == 0, c == DC - 1, ["wA", "u%d" % c], [pkrk])
                    for c in range(DC):
                        mm(pkq[0:64, :], wArot[:, c, :], u[:, c, :], c == 0, c == DC - 1, WROT + ["u%d" % c], [pkqk])
                    tt(t1[:], pkr[0:64, :], cosT[:], ALU.mult, [pkrk, "cosT"], ["t1"])
                    tt(t2[:], pkq[0:64, :], sinT[:], ALU.mult, [pkqk, "sinT"], ["t2"])
                    tt(kpe[0:64, tsl], t1[:], t2[:], ALU.add, ["t1", "t2"], ["kpe%d" % v])
                    if not own or LVL < 5:
                        continue
                    pq = [psr.next(), psr.next()]
                    for l in range(2):
                        for c in range(DC):
                            mm(pq[l][0][:], wA[:, c, l * 128:(l + 1) * 128], u[:, c, :], c == 0, c == DC - 1,
                               ["wA", "u%d" % c], [pq[l][1]])
                    ps2, ps2k = psr.next()
                    for l in range(2):
                        sqt, sqk = sqr.next()
                        act(sqt[:], pq[l][0][:], AF.Square, [pq[l][1]], [sqk])
                        cp(raw[:, l, :], pq[l][0][:], [pq[l][1]], ["raw%d" % l], eng="act")
                        mm(ps2[:], onesb[:], sqt[:], l == 0, l == 1, ["onesb", sqk], [ps2k])
                    rms_scale(ps2[:], 256, r2[:], lnt[:], [ps2k], "r2")
                    for l in range(2):
                        stt(cqn[:, l, osl], raw[:, l, :], cs[:, C_QN + l:C_QN + l + 1], r2[:], ALU.mult, ALU.mult,
                            ["raw%d" % l, "cs", "r2"], ["cqn%d_%d" % (j, l)])
                    for h in (range(4) if LVL >= 6 else []):
                        pqr, pqrk = psr.next()
                        pqq, pqqk = psr.next()
                        b0 = h * 192 + 128
                        for l in range(2):
                            mm(pqr[0:64, :], wuq[:, l, b0:b0 + 64], cqn[:, l, osl], l == 0, l == 1,
                               ["wuq", "cqn%d_%d" % (j, l)], [pqrk])
                        for l in range(2):
                            mm(pqq[0:64, :], wuqrot[:, l, h, :], cqn[:, l, osl], l == 0, l == 1,
                               WQROT + ["cqn%d_%d" % (j, l)], [pqqk])
                        tt(t1[:], pqr[0:64, :], cosT[:], ALU.mult, [pqrk, "cosT"], ["t1"])
                        tt(t2[:], pqq[0:64, :], sinT[:], ALU.mult, [pqqk, "sinT"], ["t2"])
                        tt(t1[:], t1[:], t2[:], ALU.add, ["t1", "t2"], ["t1"])
                        ts(qpe[0:64, h, osl], t1[:], QSCALE, None, ALU.mult, None, ["t1"], ["qpe%d_%d" % (j, h)])
            sch.barrier()

            with ExitStack() as p2:
                Kh = sbuf(p2, "Kh", [128, S], BF16)
                Vh = sbuf(p2, "Vh", [128, NB, 128], BF16)
                Qn = sbuf(p2, "Qn", [128, SO], BF16)
                ptl = [sbuf(p2, "pt%d" % i, [128, 512], BF16) for i in range(5)]
                rden = sbuf(p2, "rden", [128, 512], F32)
                dsum = sbuf(p2, "dsum", [128, 512], F32)
                for h in (range(4) if "B" in phases else []):
                    psr = Rot([(PB[i], PK[i]) for i in range(4)])
                    ci = 0
                    for v in range(NT):
                        tsl = slice(v * 512, (v + 1) * 512)
                        ps, pk = psr.next()
                        mm(ps[:], wukv[:, h * 256:h * 256 + 128], ckvn[:, tsl], True, True, ["wukv"], [pk])
                        cp(Kh[:, tsl], ps[:], [pk], ["Kh%d" % v], eng=("act" if ci % 2 else "dve"))
                        ci += 1
                        ps, pk = psr.next()
                        for bl in range(4):
                            mm(ps[:, bl * 128:(bl + 1) * 128], ckvn[:, v * 512 + bl * 128:v * 512 + (bl + 1) * 128],
                               wukv[:, h * 256 + 128:h * 256 + 256], True, True, ["wukv"], [pk])
                        cp(Vh[:, v * 4:(v + 1) * 4, :].rearrange("p a d -> p (a d)"), ps[:], [pk], ["Vh%d" % v],
                           eng=("act" if ci % 2 else "dve"))
                        ci += 1
                    for j in range(NO):
                        osl = slice(j * 512, (j + 1) * 512)
                        ps, pk = psr.next()
                        for l in range(2):
                            mm(ps[:], wuq[:, l, h * 192:h * 192 + 128], cqn[:, l, osl], l == 0, l == 1, ["wuq"], [pk])
                        OP("act", lambda e, ps=ps, osl=osl: e.mul(out=Qn[:, osl], in_=ps[:], mul=QSCALE), [pk], ["Qn%d" % j])
                    pss = Rot([(PB[i], PK[i]) for i in range(5)])
                    pacc = Rot([((PB[5], PK[5]), (PB[7], PK[7])), ((PB[6], PK[6]), (PB[7], PK[7]))])
                    ptr = Rot([(ptl[i], "pt%d" % i) for i in range(5)])
                    for j in range(NO):
                        osl = slice(j * 512, (j + 1) * 512)
                        nfull = 4 * (2 * j + 1)
                        nkb = nfull + 4
                        (oacc, oak), (dacc, dak) = pacc.next()
                        pend = []

                        def issue_qk(kb):
                            st, stk = pss.next()
                            ksl = slice(kb * 128, (kb + 1) * 128)
                            mm(st[:], Kh[:, ksl], Qn[:, osl], True, False, ["Kh%d" % (kb // 4), "Qn%d" % j], [stk])
                            if kb >= nfull:
                                mm(st[:], identb[:], m4[:, kb - nfull, :], False, False, ["identb", "m4"], [stk])
                            mm(st[:], kpe[0:65, ksl], qpe[0:65, h, osl], False, True, [], [stk])
                            pt, ptk = ptr.next()
                            act(pt[:], st[:], AF.Exp, [stk], [ptk])
                            pend.append((kb, pt, ptk))

                        def issue_pv():
                            kb, pt, ptk = pend.pop(0)
                            mm(oacc[:], Vh[:, kb, :], pt[:], kb == 0, kb == nkb - 1, ["Vh%d" % (kb // 4), ptk], [oak])
                            if kb == 0:
                                cp(dsum[:], pt[:], [ptk], ["dsum"])
                            else:
                                tt(dsum[:], dsum[:], pt[:], ALU.add, ["dsum", ptk], ["dsum"])

                        for kb in range(nkb):
                            issue_qk(kb)
                            if len(pend) > 3:
                                issue_pv()
                        while pend:
                            issue_pv()
                        mm(dacc[:], onesf[:], dsum[:], True, True, ["onesf", "dsum"], [dak])
                        OP("dve", lambda e, dacc=dacc: e.reciprocal(out=rden[:], in_=dacc[:]), [dak], ["rden"])
                        tt(catA[:, h, osl], oacc[:], rden[:], ALU.mult, [oak, "rden"], ["catA%d_%d" % (h, j)])
            sch.barrier()

        with ExitStack() as pm:
            catB = sbuf(pm, "catB", [128, 4, SO], BF16)
            with ExitStack() as p3:
                wM = sbuf(p3, "wM", [128, DC, 2056], BF16)
                dmac("pool", wM, w_in[:, 448:2504], DC, [], ["wM"])
                gainbc = sbuf(p3, "gainbc", [128, 512], F32)
                gbbc = sbuf(p3, "gbbc", [128, 8], F32)
                vld = sbuf(p3, "vld", [128, NB], F32)
                vldb = sbuf(p3, "vldb", [128, NB], BF16)
                dma("sp", gainbc[:], gainrow[0:1, :].partition_broadcast(128), [], ["gainbc"])
                dma("sp", gbbc[:], gbrow[0:1, :].partition_broadcast(128), [], ["gbbc"])
                dma("sp", vld[:], validT[:, :], [], ["vld"])
                cp(vldb[:], vld[:], ["vld"], ["vldb"])
                xt = sbuf(p3, "xt2", [128, DC, 512], F32)
                u = sbuf(p3, "u2", [128, DC, 512], BF16)
                sq = [sbuf(p3, "sq2_%d" % i, [128, 512], BF16) for i in range(3)]
                sqr = Rot([(sq[i], "sq%d" % i) for i in range(3)])
                lnt = sbuf(p3, "lnt2", [128, 512], F32)
                rbc = sbuf(p3, "rbc2", [128, 512], F32)
                pc = sbuf(p3, "pc", [128, 8, 515], F32)
                cacc = [sbuf(p3, "cacc%d" % i, [128, 512], F32) for i in range(2)]
                caccr = Rot([(cacc[i], "cacc%d" % i) for i in range(2)])
                ctmp = sbuf(p3, "ctmp", [128, 512], F32)
                qkT = sbuf(p3, "qkT", [128, 8, 512], BF16)
                vtm = sbuf(p3, "vtm", [128, 4, 512], BF16)
                og = sbuf(p3, "og", [128, 4, 512], BF16)
                gsbs = [sbuf(p3, "gsb%d" % i, [128, 8], F32) for i in range(4)]
                lfns = [sbuf(p3, "lfn%d" % i, [128, 4], F32) for i in range(4)]
                sms = [sbuf(p3, "sm%d" % i, [128, 64], F32) for i in range(4)]
                trilfs = [sbuf(p3, "trilf%d" % i, [128, 4, 128], F32) for i in range(4)]
                dTs = [sbuf(p3, "dT%d" % i, [128, 4, 128], BF16) for i in range(4)]
                sTs = [sbuf(p3, "sT%d" % i, [128, 4, 128], BF16) for i in range(4)]
                tmpn = sbuf(p3, "tmpn", [128, 4, 128], F32)
                num = sbuf(p3, "num", [128, 4, 128], F32)
                sqj = sbuf(p3, "sqj", [128, 128], F32)
                GO = sbuf(p3, "GO", [128, 512], F32)
                mouts = [sbuf(p3, "mout%d" % i, [128, 512], BF16) for i in range(2)]
                kws = [sbuf(p3, "kw%d" % i, [128, 4, 128], BF16) for i in range(4)]
                Cst = sbuf(p3, "Cst", [128, 4, 128], F32)
                nst = sbuf(p3, "nst", [128, 4], F32)
                Cbf = sbuf(p3, "Cbf", [128, 4, 128], BF16)
                nbf = sbuf(p3, "nbf", [128, 4], BF16)
                OP("dve", lambda e: e.memset(pc[:], 0.0), [], ["pc"])
                OP("dve", lambda e: e.memset(Cst[:], 0.0), [], ["Cst"])
                OP("dve", lambda e: e.memset(nst[:], 0.0), [], ["nst"])
                OP("dve", lambda e: e.memset(Cbf[:], 0.0), [], ["Cbf"])
                OP("dve", lambda e: e.memset(nbf[:], 0.0), [], ["nbf"])
                psr = Rot([(PB[i], PK[i]) for i in range(7)])
                MQ, MK, MV, MO, MG = 0, 512, 1024, 1536, 2048
                pst = Rot([(PB[i], PK[i]) for i in (4, 5, 6)])

                for v in (range(NT) if "A2" in phases else []):
                    own = (v % 2 == 1)
                    j = v // 2
                    tsl = slice(v * 512, (v + 1) * 512)
                    dmac("sp", xt, xT[:, tsl], DC, [], ["xt"])
                    pss, pssk = psr.next()
                    for c in range(DC):
                        sqt, sqk = sqr.next()
                        act(sqt[:], xt[:, c, :], AF.Square, ["xt#%d" % c], [sqk])
                        mm(pss[:], onesb[:], sqt[:], c == 0, c == DC - 1, ["onesb", sqk], [pssk])
                    rms_scale(pss[:], D, rbc[:], lnt[:], [pssk], "rbc")
                    for c in range(DC):
                        stt(u[:, c, :], xt[:, c, :], cs[:, G_APRE + c:G_APRE + c + 1], rbc[:], ALU.mult, ALU.mult,
                            ["xt#%d" % c, "cs", "rbc"], ["u%d" % c])
                    UK = ["u%d" % c for c in range(DC)]
                    cp(pc[:, :, 0:3], pc[:, :, 512:515], ["pc"] + ["pc%d" % ct for ct in range(8)], ["pc"])
                    for ct in range(8):
                        ps, pk = psr.next()
                        for c in range(DC):
                            mm(ps[:], wM[:, c, ct * 128:(ct + 1) * 128], u[:, c, :], c == 0, c == DC - 1,
                               ["wM", "u%d" % c], [pk])
                        cp(pc[:, ct, 3:515], ps[:], [pk, "pc"], ["pc%d" % ct], eng="act")
                        if ct < 4 and not own:
                            continue
                        ca, cak = caccr.next()
                        w0 = C_CW + ct * 4
                        ceng = "dve"
                        ts(ca[:], pc[:, ct, 0:512], cs[:, w0:w0 + 1], cs[:, C_CB + ct:C_CB + ct + 1], ALU.mult, ALU.add,
                           ["pc", "pc%d" % ct, "cs"], [cak], eng=ceng)
                        for tap in range(1, 4):
                            if ceng == "pool":
                                ts(ctmp[:], pc[:, ct, tap:tap + 512], cs[:, w0 + tap:w0 + tap + 1], None, ALU.mult, None,
                                   ["pc", "pc%d" % ct, "cs"], ["ctmp"], eng="pool")
                                tt(ca[:], ca[:], ctmp[:], ALU.add, [cak, "ctmp"], [cak], eng="pool")
                            else:
                                stt(ca[:], pc[:, ct, tap:tap + 512], cs[:, w0 + tap:w0 + tap + 1], ca[:], ALU.mult, ALU.add,
                                    ["pc", "pc%d" % ct, "cs", cak], [cak])
                        act(qkT[:, ct, :], ca[:], AF.Silu, [cak], ["qkT%d" % ct])
                    gps, gpk = PB[7], PK[7]
                    for bl in range(4):
                        bs = slice(bl * 128, (bl + 1) * 128)
                        ps, pk = psr.next()
                        for c in range(DC):
                            mm(ps[:], u[:, c, bs], wM[:, c, MV:MV + 512], c == 0, c == DC - 1, ["wM", "u%d" % c], [pk])
                        cp(vtm[:, bl, :], ps[:], [pk], ["vtm%d" % bl], eng="act")
                        if own:
                            ps, pk = psr.next()
                            for c in range(DC):
                                mm(ps[:], u[:, c, bs], wM[:, c, MO:MO + 512], c == 0, c == DC - 1, ["wM", "u%d" % c], [pk])
                            act(og[:, bl, :], ps[:], AF.Sigmoid, [pk], ["og%d" % bl])
                        for c in range(DC):
                            mm(gps[:, bl * 8:(bl + 1) * 8], u[:, c, bs], wM[:, c, MG:MG + 8], c == 0, c == DC - 1,
                               ["wM", "u%d" % c], [gpk])
                    def blk_pre(bl):
                        X = "_%d" % bl
                        gsb, lfn, sm, sT, kw = gsbs[bl], lfns[bl], sms[bl], sTs[bl], kws[bl]
                        trilf, trk = trilfs[bl], "trilf%d" % bl
                        dT, dtk = dTs[bl], "dT%d" % bl
                        bs = slice(bl * 128, (bl + 1) * 128)
                        tt(gsb[:], gps[:, bl * 8:(bl + 1) * 8], gbbc[:], ALU.add, [gpk, "gbbc"], ["gsb" + X])
                        act(sm[:, 0:4], gsb[:, 4:8], AF.Exp, ["gsb" + X], ["sm_e" + X], scale=-1.0)
                        act(lfn[:], sm[:, 0:4], AF.Ln, ["sm_e" + X], ["lfn" + X], bias=1.0)
                        cps, cpk = PB[bl], PK[bl]
                        mm(cps[:, 0:4], trif[:], lfn[:], True, True, ["trif", "lfn" + X], [cpk])
                        mm(cps[:, 4:8], onesf[:], lfn[:], True, True, ["onesf", "lfn" + X], [cpk])
                        ts(sm[:, 4:8], gsb[:, 0:4], LNK, None, ALU.add, None, ["gsb" + X], ["sm_bd" + X])
                        tt(sm[:, 4:8], sm[:, 4:8], cps[:, 0:4], ALU.add, ["sm_bd" + X, cpk], ["sm_bd" + X])
                        tt(sm[:, 8:12], sm[:, 4:8], cps[:, 4:8], ALU.subtract, ["sm_bd" + X, cpk], ["sm_wl" + X])
                        act(sm[:, 12:16], sm[:, 8:12], AF.Exp, ["sm_wl" + X], ["sm_w" + X])
                        act(sm[:, 16:20], cps[:, 4:8], AF.Exp, [cpk], ["sm_eg" + X], scale=-1.0)
                        act(sm[:, 20:24], cps[:, 0:4], AF.Exp, [cpk], ["sm_eb" + X], scale=-1.0)
                        if own:
                            tt(trilf[:], trif[:].unsqueeze(1).to_broadcast([128, 4, 128]),
                               lfn[:].unsqueeze(2).to_broadcast([128, 4, 128]), ALU.mult, ["trif", "lfn" + X], [trk])
                            bps, bpk = PB[bl], PK[bl]
                            for h in range(4):
                                mm(bps[:, h * 128:(h + 1) * 128], onesf[:], trilf[:, h, :], True, False, ["onesf", trk], [bpk])
                                mm(bps[:, h * 128:(h + 1) * 128], identf[:], mposf[:], False, True, ["identf", "mposf"], [bpk])
                            for h in range(4):
                                act(dT[:, h, :], bps[:, h * 128:(h + 1) * 128], AF.Exp, [bpk, "sm_bd" + X], [dtk + "h%d" % h],
                                    bias=sm[:, 4 + h:5 + h], scale=-1.0)
                            kps, kpk = PB[bl], PK[bl]
                            for h in range(4):
                                mm(kps[:, h * 128:(h + 1) * 128], qkT[:, 4 + h, bs], qkT[:, h, bs], True, True,
                                   ["qkT%d" % (4 + h), "qkT%d" % h], [kpk])
                            tt(sT[:].rearrange("p h j -> p (h j)"), kps[:], dT[:].rearrange("p h j -> p (h j)"), ALU.mult,
                               [kpk] + [dtk + "h%d" % h for h in range(4)], ["sT" + X])
                        tps, tpk = PB[bl], PK[bl]
                        for h in range(4):
                            mm(tps[:, h * 128:(h + 1) * 128], qkT[:, 4 + h, bs], identb[:], True, True,
                               ["qkT%d" % (4 + h), "identb"], [tpk])
                        tt(kw[:], tps[:].rearrange("p (h d) -> p h d", h=4),
                           sm[:, 12:16].unsqueeze(2).to_broadcast([128, 4, 128]), ALU.mult, [tpk, "sm_w" + X], ["kw" + X])

                    def blk_tail(bl):
                        X = "_%d" % bl
                        sm, sT, kw = sms[bl], sTs[bl], kws[bl]
                        gb = v * 4 + bl
                        bs = slice(bl * 128, (bl + 1) * 128)
                        tok0 = j * 512 + bl * 128
                        if own:
                            ips, ipk = pst.next()
                            eps_, epk = pst.next()
                            for h in range(4):
                                hs = slice(h * 128, (h + 1) * 128)
                                mm(ips[:, hs], sT[:, h, :], vtm[:, bl, hs], True, True, ["sT" + X, "vtm%d" % bl], [ipk])
                                mm(PB[7][:, 32 + h:33 + h], sT[:, h, :], vldb[:, gb:gb + 1], True, True, ["sT" + X, "vldb"], [PK[7]])
                                mm(eps_[:, hs], qkT[:, h, bs], Cbf[:, h, :], True, True, ["qkT%d" % h, "Cbf"], [epk])
                                mm(PB[7][:, 36 + h:37 + h], qkT[:, h, bs], nbf[:, h:h + 1], True, True, ["qkT%d" % h, "nbf"], [PK[7]])
                        ups, upk = pst.next()
                        for h in range(4):
                            hs = slice(h * 128, (h + 1) * 128)
                            mm(ups[:, hs], kw[:, h, :], vtm[:, bl, hs], True, True, ["kw" + X, "vtm%d" % bl], [upk])
                            mm(PB[7][:, 40 + h:41 + h], kw[:, h, :], vldb[:, gb:gb + 1], True, True, ["kw" + X, "vldb"], [PK[7]])
                        tt(Cst[:], Cst[:], sm[:, 16:20].unsqueeze(2).to_broadcast([128, 4, 128]), ALU.mult, ["Cst", "sm_eg" + X], ["Cst"])
                        tt(Cst[:].rearrange("p h d -> p (h d)"), Cst[:].rearrange("p h d -> p (h d)"), ups[:], ALU.add,
                           ["Cst", upk], ["Cst"])
                        tt(nst[:], nst[:], sm[:, 16:20], ALU.mult, ["nst", "sm_eg" + X], ["nst"])
                        tt(nst[:], nst[:], PB[7][:, 40:44], ALU.add, ["nst", PK[7]], ["nst"])
                        cp(Cbf[:], Cst[:], ["Cst"], ["Cbf"], eng="act")
                        cp(nbf[:], nst[:], ["nst"], ["nbf"])
                        if not own:
                            return
                        tt(sm[:, 24:28], PB[7][:, 36:40], sm[:, 20:24], ALU.mult, [PK[7], "sm_eb" + X], ["sm_d1" + X])
                        tt(sm[:, 28:32], sm[:, 24:28], PB[7][:, 32:36], ALU.add, ["sm_d1" + X, PK[7]], ["sm_den" + X])
                        act(sm[:, 32:36], sm[:, 28:32], AF.Abs, ["sm_den" + X], ["sm_abs" + X])
                        ts(sm[:, 32:36], sm[:, 32:36], 1.0, None, ALU.max, None, ["sm_abs" + X], ["sm_abs" + X])
                        OP("dve", lambda e: e.reciprocal(out=sm[:, 36:40], in_=sm[:, 32:36]), ["sm_abs" + X], ["sm_rden" + X])
                        tt(tmpn[:], eps_[:].rearrange("p (h d) -> p h d", h=4),
                           sm[:, 20:24].unsqueeze(2).to_broadcast([128, 4, 128]), ALU.mult, [epk, "sm_eb" + X], ["tmpn"])
                        tt(num[:].rearrange("p h d -> p (h d)"), ips[:], tmpn[:].rearrange("p h d -> p (h d)"), ALU.add,
                           [ipk, "tmpn"], ["num"])
                        OP("dve", lambda e: e.memset(sm[:, 40:44], 0.0), [], ["sm_ss%d" % h + X for h in range(4)])
                        for h in range(4):
                            act(sqj[:], num[:, h, :], AF.Square, ["num"], ["sqj", "sm_ss%d" % h + X], accum_out=sm[:, 40 + h:41 + h])
                        tt(sm[:, 44:48], sm[:, 36:40], sm[:, 36:40], ALU.mult, ["sm_rden" + X], ["sm_r2" + X])
                        tt(sm[:, 44:48], sm[:, 44:48], sm[:, 40:44], ALU.mult, ["sm_r2" + X] + ["sm_ss%d" % h + X for h in range(4)], ["sm_r2" + X])
                        act(sm[:, 48:52], sm[:, 44:48], AF.Ln, ["sm_r2" + X], ["sm_ln" + X], bias=EPS, scale=1.0 / 128)
                        act(sm[:, 52:56], sm[:, 48:52], AF.Exp, ["sm_ln" + X], ["sm_rs" + X], scale=-0.5)
                        tt(sm[:, 56:60], sm[:, 52:56], sm[:, 36:40], ALU.mult, ["sm_rs" + X, "sm_rden" + X], ["sm_f" + X])
                        tt(GO[:], gainbc[:], og[:, bl, :], ALU.mult, ["gainbc", "og%d" % bl], ["GO"], eng=("dve" if os.environ.get("NOPOOL") else "pool"))
                        tt(tmpn[:], num[:], sm[:, 56:60].unsqueeze(2).to_broadcast([128, 4, 128]), ALU.mult,
                           ["num", "sm_f" + X], ["tmpn"])
                        mout = mouts[bl % 2]
                        tt(mout[:], tmpn[:].rearrange("p h d -> p (h d)"), GO[:], ALU.mult, ["tmpn", "GO"], ["mout%d" % (bl % 2)])

                    def blk_tail_b(bl):
                        if not own:
                            return
                        mout = mouts[bl % 2]
                        tok0 = j * 512 + bl * 128
                        tps, tpk = PB[bl], PK[bl]
                        for h in range(4):
                            mm(tps[:, h * 128:(h + 1) * 128], mout[:, h * 128:(h + 1) * 128], identb[:], True, True,
                               ["mout%d" % (bl % 2), "identb"], [tpk])
                        cp(catB[:, :, tok0:tok0 + 128], tps[:].rearrange("p (h t) -> p h t", h=4), [tpk],
                           ["catB%d_%d" % (j, bl)], eng="act")

                    streams = []
                    for bl in range(4):
                        sch.rec = []
                        blk_pre(bl)
                        streams.append(sch.rec)
                        sch.rec = None
                    sch.replay_rr(streams)
                    for bl in range(4):
                        blk_tail(bl)
                        if bl > 0:
                            blk_tail_b(bl - 1)
                    blk_tail_b(3)
            sch.barrier()

            with ExitStack() as p4:
                wo = sbuf(p4, "wo", [128, DC, D], BF16)
                dmac("pool", wo, w_out[:, :], DC, [], ["wo"])
                xts4 = [sbuf(p4, "xt4_%d" % i, [128, DC, 512], F32) for i in range(2)]
                mix = sbuf(p4, "mix", [128, DC, 512], F32)
                sq = [sbuf(p4, "sq4_%d" % i, [128, 512], BF16) for i in range(3)]
                sqr = Rot([(sq[i], "sq%d" % i) for i in range(3)])
                lnt = sbuf(p4, "lnt4", [128, 512], F32)
                rbc = sbuf(p4, "rbc4", [128, 512], F32)
                tm = [sbuf(p4, "tm4_%d" % i, [128, 512], F32) for i in range(2)]
                tmr = Rot([(tm[i], "tm%d" % i) for i in range(2)])
                h1 = sbuf(p4, "h1_4", [128, DC, 512], F32)
                psr = Rot([(PB[i], PK[i]) for i in range(6)])
                for j in (range(NO) if "Bp" in phases else []):
                    v = 2 * j + 1
                    tsl = slice(v * 512, (v + 1) * 512)
                    osl = slice(j * 512, (j + 1) * 512)
                    xt = xts4[j % 2]
                    xk = "xq%d" % (j % 2)
                    dmac("sp", xt, xT[:, tsl], DC, [], [xk])
                    pss, pssk = PB[7], PK[7]
                    for ct in range(DC):
                        ps, pk = psr.next()
                        for f in range(8):
                            rhs = catA[:, f, osl] if f < 4 else catB[:, f - 4, osl]
                            mm(ps[:], wo[:, f, ct * 128:(ct + 1) * 128], rhs, f == 0, f == 7, ["wo"], [pk])
                        sqt, sqk = sqr.next()
                        act(sqt[:], ps[:], AF.Square, [pk], [sqk])
                        cp(mix[:, ct, :], ps[:], [pk], ["mix%d" % ct])
                        mm(pss[:], onesb[:], sqt[:], ct == 0, ct == DC - 1, ["onesb", sqk], [pssk])
                    rms_scale(pss[:], D, rbc[:], lnt[:], [pssk], "rbc")
                    for ct in range(DC):
                        t_, tk = tmr.next()
                        tt(t_[:], mix[:, ct, :], rbc[:], ALU.mult, ["mix%d" % ct, "rbc"], [tk])
                        stt(h1[:, ct, :], t_[:], cs[:, G_APOST + ct:G_APOST + ct + 1], xt[:, ct, :], ALU.mult, ALU.add,
                            [tk, "cs", xk], ["h1"])
                    dmac_out("sp", h1d[:, osl], h1, DC, ["h1"], ["h1d%d" % j])
                    if debug:
                        dmac_out("sp", dbg_h1[:, osl], h1, DC, ["h1"], [])
                if debug:
                    for hh in range(4):
                        dma("pool", dbg_cat[:, hh * SO:(hh + 1) * SO], catA[:, hh, :], [], [])
                        dma("pool", dbg_cat[:, (4 + hh) * SO:(5 + hh) * SO], catB[:, hh, :], [], [])
            sch.barrier()

        with ExitStack() as p5:
            wpg = sbuf(p5, "wpg", [128, DC, D], BF16)
            wpp = sbuf(p5, "wpp", [128, 2, D], BF16)
            dmac("pool", wpg, w_pg[:, :], DC, [], ["wpg"])
            dmac("pool", wpp, w_pp[:, :], 2, [], ["wpp"])
            wg = [sbuf(p5, "wg%d" % i, [128, DC, 256], BF16) for i in range(3)]
            wgr = Rot([(wg[i], "wg%d" % i) for i in range(3)])
            wd = [sbuf(p5, "wd%d" % i, [128, NFF, 128], BF16) for i in range(2)]
            wdr = Rot([(wd[i], "wd%d" % i) for i in range(2)])
            h1s = [sbuf(p5, "h1_5%d" % i, [128, DC, 512], F32) for i in range(2)]
            fo = sbuf(p5, "fo", [128, DC, 512], F32)
            fn = sbuf(p5, "fn", [128, DC, 512], BF16)
            h2b = sbuf(p5, "h2b", [128, DC, 512], BF16)
            actT = sbuf(p5, "actT", [128, NFF, 512], BF16)
            sq = [sbuf(p5, "sq5_%d" % i, [128, 512], BF16) for i in range(3)]
            sqr = Rot([(sq[i], "sq%d" % i) for i in range(3)])
            lntA = sbuf(p5, "lnt5a", [128, 512], F32)
            lntB = sbuf(p5, "lnt5b", [128, 512], F32)
            rbcA = sbuf(p5, "rbc5a", [128, 512], F32)
            rbcB = sbuf(p5, "rbc5b", [128, 512], F32)
            tm = [sbuf(p5, "tm5_%d" % i, [128, 512], F32) for i in range(3)]
            tmr = Rot([(tm[i], "tm%d" % i) for i in range(3)])
            ptfs = [sbuf(p5, "pt_f%d" % i, [128, 2, 512], F32) for i in range(2)]
            ptbs = [sbuf(p5, "pt_b%d" % i, [128, 2, 512], BF16) for i in range(2)]
            psr = Rot([(PB[i], PK[i]) for i in range(7)])

            def front_a(j):
                b = j % 2
                osl = slice(j * 512, (j + 1) * 512)
                h1 = h1s[b]
                dmac("sp", h1, h1d[:, osl], DC, ["h1d%d" % j], ["h1%d" % b])
                dmac("sp", ptfs[b], pT[:, osl], 2, [], ["ptf%d" % b])
                cp(ptbs[b][:], ptfs[b][:], ["ptf%d" % b], ["ptb%d" % b])
                pss, pssk = PB[7], PK[7]
                for c in range(DC):
                    sqt, sqk = sqr.next()
                    act(sqt[:], h1[:, c, :], AF.Square, ["h1%d#%d" % (b, c)], [sqk])
                    mm(pss[:], onesb[:], sqt[:], c == 0, c == DC - 1, ["onesb", sqk], [pssk])
                rms_scale(pss[:], D, rbcA[:], lntA[:], [pssk], "rbcA")

            def front_b(j):
                b = j % 2
                h1 = h1s[b]
                for c in range(DC):
                    stt(fn[:, c, :], h1[:, c, :], cs[:, G_FPRE + c:G_FPRE + c + 1], rbcA[:], ALU.mult, ALU.mult,
                        ["h1%d" % b, "cs", "rbcA"], ["fn%d" % c])

            def gateup(j, mid=None):
                for n in range(NFF):
                    if n == NFF // 2 and mid is not None:
                        mid()
                    w_, wk = wgr.next()
                    dma("sp", w_[:].rearrange("p c f -> p (c f)"), wgu_s[n, :, :], [], [wk])
                    pg, pgk = psr.next()
                    pu, puk = psr.next()
                    for c in range(DC):
                        mm(pg[:], w_[:, c, 0:128], fn[:, c, :], c == 0, c == DC - 1, [wk, "fn%d" % c], [pgk])
                    for c in range(DC):
                        mm(pu[:], w_[:, c, 128:256], fn[:, c, :], c == 0, c == DC - 1, [wk, "fn%d" % c], [puk])
                    t_, tk = tmr.next()
                    act(t_[:], pg[:], AF.Silu, [pgk], [tk])
                    tt(actT[:, n, :], pu[:], t_[:], ALU.mult, [puk, tk], ["actT%d" % n])

            def down_post_ple(j):
                b = j % 2
                osl = slice(j * 512, (j + 1) * 512)
                h1 = h1s[b]
                hk = "h1%d" % b
                pss, pssk = PB[7], PK[7]
                for ct in range(DC):
                    w_, wk = wdr.next()
                    dma("sp", w_[:].rearrange("p n f -> p (n f)"), wdn_s[ct, :, :], [], [wk])
                    ps, pk = psr.next()
                    for n in range(NFF):
                        mm(ps[:], w_[:, n, :], actT[:, n, :], n == 0, n == NFF - 1, [wk, "actT%d" % n], [pk])
                    sqt, sqk = sqr.next()
                    act(sqt[:], ps[:], AF.Square, [pk], [sqk])
                    cp(fo[:, ct, :], ps[:], [pk], ["fo%d" % ct])
                    mm(pss[:], onesb[:], sqt[:], ct == 0, ct == DC - 1, ["onesb", sqk], [pssk])
                rms_scale(pss[:], D, rbcB[:], lntB[:], [pssk], "rbcB")
                for ct in range(DC):
                    t_, tk = tmr.next()
                    tt(t_[:], fo[:, ct, :], rbcB[:], ALU.mult, ["fo%d" % ct, "rbcB"], [tk])
                    stt(h1[:, ct, :], t_[:], cs[:, G_FPOST + ct:G_FPOST + ct + 1], h1[:, ct, :], ALU.mult, ALU.add,
                        [tk, "cs", hk], [hk])
                    cp(h2b[:, ct, :], h1[:, ct, :], [hk], ["h2b%d" % ct], eng="act")
                for ct in range(DC):
                    pg, pgk = psr.next()
                    pp, ppk = psr.next()
                    for c in range(DC):
                        mm(pg[:], wpg[:, c, ct * 128:(ct + 1) * 128], h2b[:, c, :], c == 0, c == DC - 1, ["wpg", "h2b%d" % c], [pgk])
                    for c in range(2):
                        mm(pp[:], wpp[:, c, ct * 128:(ct + 1) * 128], ptbs[b][:, c, :], c == 0, c == 1, ["wpp", "ptb%d" % b], [ppk])
                    t_, tk = tmr.next()
                    act(t_[:], pg[:], AF.Sigmoid, [pgk], [tk])
                    tt(t_[:], pp[:], t_[:], ALU.mult, [ppk, tk], [tk])
                    tt(fo[:, ct, :], t_[:], h1[:, ct, :], ALU.add, [tk, hk], ["fo%d" % ct])
                dmac_out("sp", yT[:, osl], fo, DC, ["fo%d" % c for c in range(DC)], ["yT%d" % j])

            if "C" in phases:
                front_a(0)
                front_b(0)
                for j in range(NO):
                    if j + 1 < NO:
                        gateup(j, mid=lambda j=j: front_a(j + 1))
                    else:
                        gateup(j)
                    if j + 1 < NO:
                        front_b(j + 1)
                    down_post_ple(j)
            sch.barrier()
        sch.emit()
    return nc


def _consts():
    cmv = np.zeros((128, 4 * 512 + 4 * 128), np.float32)
    k = np.arange(128)[:, None]
    q = np.arange(512)[None, :]
    for a in range(4):
        cmv[:, a * 512:(a + 1) * 512] = np.where(k + a * 128 <= q, 0.0, NEG)
    t = np.arange(128)[:, None]
    jj = np.arange(128)[None, :]
    cmv[:, 2048:2176] = (t <= jj).astype(np.float32)
    cmv[:, 2176:2304] = np.where(t > jj, -NEG, 0.0)
    cmv[:, 2304:2432] = np.eye(128, dtype=np.float32)
    return cmv


def prep(inputs, S, ncores):
    f = lambda a: np.ascontiguousarray(np.asarray(a, dtype=np.float32))
    x = np.asarray(inputs["x"], np.float32)
    p = np.asarray(inputs["p"], np.float32)[0]
    positions = np.asarray(inputs["positions"]).astype(np.int32)
    SO = S // 2
    NB = S // 128
    cst = np.zeros((128, 96), np.float32)
    cst[:, 0:8] = f(inputs["attn_pre_norm"])[0].reshape(8, 128).T
    cst[:, 8:16] = f(inputs["attn_post_norm"])[0].reshape(8, 128).T
    cst[:, 16:24] = f(inputs["ffn_pre_norm"])[0].reshape(8, 128).T
    cst[:, 24:32] = f(inputs["ffn_post_norm"])[0].reshape(8, 128).T
    cst[:, 32:34] = f(inputs["q_norm"])[0].reshape(2, 128).T
    cst[:, 34:35] = f(inputs["kv_norm"])[0].reshape(1, 128).T
    cw = f(inputs["conv_w"])[0]
    cst[:, 35:67] = cw.reshape(4, 8, 128).transpose(2, 1, 0).reshape(128, 32)
    cst[:, 67:75] = f(inputs["conv_b"])[0].reshape(8, 128).T
    inv = (10000.0 ** (-np.arange(0, 64, 2, dtype=np.float32) / 64)).astype(np.float32)
    cst[0:64, 75] = np.concatenate([inv, inv]) / np.float32(2 * np.pi)
    gainrow = f(inputs["mlstm_norm"])[0].reshape(1, 512)
    gbrow = np.concatenate([f(inputs["gate_bias_i"])[0], f(inputs["gate_bias_f"])[0]]).reshape(1, 8)
    cmv = _consts()
    wg = f(inputs["w_gate"])[0].reshape(8, 128, NFF, 128)
    wu = f(inputs["w_up"])[0].reshape(8, 128, NFF, 128)
    wgu = np.ascontiguousarray(np.concatenate([wg, wu], axis=3).transpose(2, 1, 0, 3))
    wdn = np.ascontiguousarray(f(inputs["w_down"])[0].reshape(NFF, 128, 8, 128).transpose(2, 1, 0, 3))
    shared = {
        "cst": cst, "gainrow": gainrow, "gbrow": gbrow, "cm": cmv,
        "w_in": f(inputs["w_in"])[0], "w_uq": f(inputs["w_uq"])[0], "w_ukv": f(inputs["w_ukv"])[0],
        "w_out": f(inputs["w_out"])[0], "wgu": wgu, "wdn": wdn,
        "w_pp": f(inputs["w_ple_proj"])[0], "w_pg": f(inputs["w_ple_gate"])[0],
    }
    maps = []
    own_idx = []
    for core in range(ncores):
        b, par = core // 2, core % 2
        xb = x[b]
        if par == 1:
            xv = xb
            pv = positions[b]
            valid = np.ones(S, np.float32)
        else:
            xv = np.concatenate([np.zeros((512, D), np.float32), xb[:S - 512]], axis=0)
            pv = np.concatenate([np.zeros(512, np.int32), positions[b][:S - 512]])
            valid = np.concatenate([np.zeros(512, np.float32), np.ones(S - 512, np.float32)])
        view_tok = np.arange(S).reshape(S // 512, 512)[1::2].reshape(-1)
        glob = view_tok if par == 1 else view_tok - 512
        own_idx.append((b, glob))
        m = dict(shared)
        m["xT"] = np.ascontiguousarray(xv.T)
        m["pos"] = np.ascontiguousarray(pv.reshape(1, S))
        m["pT"] = np.ascontiguousarray(p[b][glob].T)
        m["kflag"] = np.where(valid > 0, 0.0, NEG).astype(np.float32).reshape(1, S)
        m["validT"] = np.ascontiguousarray(valid.reshape(NB, 128).T)
        maps.append(m)
    return maps, own_idx


_NC_CACHE = {}


def kernel(**inputs):
    x = np.asarray(inputs["x"])
    B, S, _ = x.shape
    ncores = 2 * B
    if S not in _NC_CACHE:
        _NC_CACHE[S] = build(S)
    nc = _NC_CACHE[S]
    maps, own_idx = prep(inputs, S, ncores)
    res = run_bass_kernel_spmd(nc, maps, core_ids=list(range(ncores)))
    out = np.zeros((B, S, D), np.float32)
    for core in range(ncores):
        b, glob = own_idx[core]
        out[b, glob, :] = np.asarray(res.results[core]["yT"]).T
    return out
```

```python
import math
import os
from contextlib import ExitStack

import numpy as np
import concourse.bass as bass
import concourse.mybir as mybir
from concourse.bass_utils import run_bass_kernel_spmd

F32 = mybir.dt.float32
BF16 = mybir.dt.bfloat16
I32 = mybir.dt.int32
AF = mybir.ActivationFunctionType
ALU = mybir.AluOpType

D = 1024
DC = 8
DIN = 2504
DFF = 2816
NFF = 22
PLE = 256
EPS = 1e-6
QSCALE = 192.0 ** -0.5
LNK = math.log(128.0 ** -0.5)
TWO_PI = 2.0 * math.pi * (1.0 - 1e-6)
NEG = -30000.0

EPOCH = 12000
NDMA = 4


class Tok:
    __slots__ = ("sem", "val", "implied", "eng")

    def __init__(self, sem, val, implied, eng):
        self.sem = sem
        self.val = val
        self.implied = implied
        self.eng = eng


class Sched:
    ENGS = ("pe", "act", "dve", "pool", "sp")

    def __init__(self, nc, stack):
        self.nc = nc
        self.stack = stack
        self.ops = {e: [] for e in self.ENGS}
        self.cnt = {e: 0 for e in self.ENGS}
        self.esems = {e: [] for e in self.ENGS}
        self.known = {e: {} for e in self.ENGS}
        self.dsems = {}
        self.dcnt = {}
        self.dlast = {}
        self.ndma = {e: 0 for e in self.ENGS}
        self.last_write = {}
        self.readers = {}
        self.alias = {}
        self.rec = None

    def _expand(self, keys):
        out = []
        for k in keys:
            out.extend(self.alias.get(k, (k,)))
        return out

    def _newsem(self, name):
        return self.stack.enter_context(self.nc.semaphore(name))

    def _esem(self, eng, epoch):
        lst = self.esems[eng]
        while len(lst) <= epoch:
            lst.append(self._newsem("s_%s_%d" % (eng, len(lst))))
        return lst[epoch]

    def _need(self, eng, tok, waits):
        if tok is None:
            return
        kn = self.known[eng]
        if kn.get(tok.sem, 0) >= tok.val:
            return
        if tok.eng == "pe" and eng == "pe":
            return
        waits[tok.sem] = max(waits.get(tok.sem, 0), tok.val)
        kn[tok.sem] = tok.val
        for s, v in tok.implied.items():
            if kn.get(s, 0) < v:
                kn[s] = v

    def replay_rr(self, streams):
        idx = [0] * len(streams)
        left = sum(len(x) for x in streams)
        while left:
            for i, st in enumerate(streams):
                if idx[i] < len(st):
                    self.op(*st[idx[i]])
                    idx[i] += 1
                    left -= 1

    def op(self, eng, fn, reads=(), writes=(), dma=False):
        if self.rec is not None:
            self.rec.append((eng, fn, list(reads), list(writes), dma))
            return None
        reads = self._expand(reads)
        writes = self._expand(writes)
        waits = {}
        for k in reads:
            self._need(eng, self.last_write.get(k), waits)
            if k.startswith("pb"):
                for t in self.readers.get(k, ()):
                    if t.eng != eng:
                        self._need(eng, t, waits)
        for k in writes:
            self._need(eng, self.last_write.get(k), waits)
            for t in self.readers.get(k, ()):
                self._need(eng, t, waits)
        if dma:
            i = self.ndma[eng]
            self.ndma[eng] += 1
            key = (eng, i % NDMA)
            if key not in self.dsems:
                self.dsems[key] = self._newsem("d_%s_%d" % key)
                self.dcnt[key] = 0
            self._need(eng, self.dlast.get(key), waits)
            self.dcnt[key] += 1
            sem = self.dsems[key]
            val = 16 * self.dcnt[key]
            inc = 16
            implied = dict(self.known[eng])
        else:
            n = self.cnt[eng]
            self.cnt[eng] += 1
            sem = self._esem(eng, n // EPOCH)
            val = n % EPOCH + 1
            inc = 1
            implied = dict(self.known[eng])
            for ep in range(n // EPOCH):
                implied[self.esems[eng][ep]] = EPOCH
        tok = Tok(sem, val, implied, eng)
        if dma:
            self.dlast[key] = tok
        self.ops[eng].append((list(waits.items()), fn, sem, inc))
        for k in reads:
            self.readers.setdefault(k, []).append(tok)
        for k in writes:
            self.last_write[k] = tok
            self.readers[k] = []
        return tok

    def barrier(self):
        toks = []
        for e in self.ENGS:
            n = self.cnt[e]
            if n:
                imp = {self.esems[e][ep]: EPOCH for ep in range((n - 1) // EPOCH)}
                toks.append(Tok(self._esem(e, (n - 1) // EPOCH), (n - 1) % EPOCH + 1, imp, e))
        toks += list(self.dlast.values())
        for e in self.ENGS:
            waits = {}
            for t in toks:
                self._need(e, t, waits)
            if waits:
                self.ops[e].append((list(waits.items()), None, None, 0))
        self.last_write = {}
        self.readers = {}

    def emit(self):
        nc = self.nc
        engmap = {"pe": "tensor", "act": "scalar", "dve": "vector", "pool": "gpsimd", "sp": "sync"}
        with nc.Block() as block:
            for e in self.ENGS:
                ops = self.ops[e]
                if not ops:
                    continue

                def body(engobj, ops=ops):
                    for waits, fn, sem, inc in ops:
                        for s, v in waits:
                            engobj.wait_ge(s, v)
                        if fn is not None:
                            fn(engobj).then_inc(sem, inc)

                getattr(block, engmap[e])(body)


class Rot:
    def __init__(self, items):
        self.items = items
        self.i = 0

    def next(self):
        it = self.items[self.i % len(self.items)]
        self.i += 1
        return it


def build(S, debug=False, phases=("A1", "B", "A2", "Bp", "C")):
    NT = S // 512
    NO = NT // 2
    SO = S // 2
    NB = S // 128
    nc = bass.Bass("TRN2", target_bir_lowering=False)
    dr = lambda n, s, d, k="ExternalInput": nc.dram_tensor(n, s, d, kind=k)
    xT = dr("xT", [D, S], F32)
    pos = dr("pos", [1, S], I32)
    pT = dr("pT", [PLE, SO], F32)
    cst = dr("cst", [128, 96], F32)
    gainrow = dr("gainrow", [1, 512], F32)
    gbrow = dr("gbrow", [1, 8], F32)
    kflag = dr("kflag", [1, S], F32)
    validT = dr("validT", [128, NB], F32)
    cm = dr("cm", [128, 4 * 512 + 4 * 128], F32)
    w_in = dr("w_in", [D, DIN], F32)
    w_uq = dr("w_uq", [256, 768], F32)
    w_ukv = dr("w_ukv", [128, 1024], F32)
    w_out = dr("w_out", [D, D], F32)
    wgu = dr("wgu", [NFF, 128, DC, 256], F32)
    wdn = dr("wdn", [DC, 128, NFF, 128], F32)
    w_pp = dr("w_pp", [PLE, D], F32)
    w_pg = dr("w_pg", [D, D], F32)
    yT = dr("yT", [D, SO], F32, "ExternalOutput")
    h1d = dr("h1d", [D, SO], F32, "Internal")
    wgu_s = dr("wgu_s", [NFF, 128, DC * 256], BF16, "Internal")
    wdn_s = dr("wdn_s", [DC, 128, NFF * 128], BF16, "Internal")
    if debug:
        dbg_cat = dr("dbg_cat", [128, 8 * SO], F32, "ExternalOutput")
        dbg_h1 = dr("dbg_h1", [D, SO], F32, "ExternalOutput")

    G_APRE, G_APOST, G_FPRE, G_FPOST = 0, 8, 16, 24
    C_QN, C_KVN, C_CW, C_CB, C_INVF = 32, 34, 35, 67, 75

    with ExitStack() as top:
        sch = Sched(nc, top)
        OP = sch.op

        def sbuf(st, n, s, d):
            return st.enter_context(nc.sbuf_tensor(n, s, d))

        def mm(out, lhsT, rhs, start, stop, reads, writes):
            OP("pe", lambda e: e.matmul(out, lhsT, rhs, start=start, stop=stop), reads, writes)

        def act(out, in_, func, reads, writes, eng="act", **kw):
            OP(eng, lambda e: e.activation(out=out, in_=in_, func=func, **kw), reads, writes)

        def tt(out, in0, in1, op, reads, writes, eng="dve"):
            OP(eng, lambda e: e.tensor_tensor(out=out, in0=in0, in1=in1, op=op), reads, writes)

        def ts(out, in0, s1, s2, op0, op1, reads, writes, eng="dve"):
            if s2 is None:
                OP(eng, lambda e: e.tensor_scalar(out=out, in0=in0, scalar1=s1, scalar2=None, op0=op0), reads, writes)
            else:
                OP(eng, lambda e: e.tensor_scalar(out=out, in0=in0, scalar1=s1, scalar2=s2, op0=op0, op1=op1), reads, writes)

        def stt(out, in0, sc, in1, op0, op1, reads, writes, eng="dve"):
            OP(eng, lambda e: e.scalar_tensor_tensor(out=out, in0=in0, scalar=sc, in1=in1, op0=op0, op1=op1), reads, writes)

        def cp(out, in_, reads, writes, eng="dve"):
            if eng == "act":
                OP("act", lambda e: e.copy(out=out, in_=in_), reads, writes)
            else:
                OP(eng, lambda e: e.tensor_copy(out=out, in_=in_), reads, writes)

        def dma(eng, out, in_, reads, writes):
            OP(eng, lambda e: e.dma_start(out=out, in_=in_), reads, writes, dma=True)

        def dmac(eng, dst, src, nch, reads, writes):
            for w in writes:
                sch.alias.setdefault(w, ["%s#%d" % (w, c) for c in range(nch)])
            for c in range(nch):
                dma(eng, dst[:, c, :], src[c * 128:(c + 1) * 128, :], reads, ["%s#%d" % (w, c) for w in writes])

        def dmac_out(eng, dst, src, nch, reads, writes):
            for c in range(nch):
                dma(eng, dst[c * 128:(c + 1) * 128, :], src[:, c, :], reads, writes)

        PB = [top.enter_context(nc.psum_tensor("pb%d" % i, [128, 512], F32)) for i in range(8)]
        PK = ["pb%d" % i for i in range(8)]

        cs = sbuf(top, "cs", [128, 96], F32)
        identb = sbuf(top, "identb", [128, 128], BF16)
        identf = sbuf(top, "identf", [128, 128], F32)
        onesb = sbuf(top, "onesb", [128, 128], BF16)
        onesf = sbuf(top, "onesf", [128, 128], F32)
        trif = sbuf(top, "trif", [128, 128], F32)
        mposf = sbuf(top, "mposf", [128, 128], F32)
        catA = sbuf(top, "catA", [128, 4, SO], BF16)
        dma("sp", cs[:], cst[:, :], [], ["cs"])
        dma("sp", trif[:], cm[:, 2048:2176], [], ["trif"])
        dma("sp", mposf[:], cm[:, 2176:2304], [], ["mposf"])
        dma("sp", identf[:], cm[:, 2304:2432], [], ["identf"])
        dma("pool", identb[:], cm[:, 2304:2432], [], ["identb"])
        OP("dve", lambda e: e.memset(onesb[:], 1.0), [], ["onesb"])
        OP("dve", lambda e: e.memset(onesf[:], 1.0), [], ["onesf"])

        def rms_scale(ps_ap, n, out_ap, tmp_ap, rk, wk):
            act(tmp_ap, ps_ap, AF.Ln, rk, [wk + "_t"], bias=EPS, scale=1.0 / n)
            act(out_ap, tmp_ap, AF.Exp, [wk + "_t"], [wk], scale=-0.5)

        with ExitStack() as pa:
            wA = sbuf(pa, "wA", [128, DC, 448], BF16)
            wArot = sbuf(pa, "wArot", [128, DC, 64], BF16)
            wuq = sbuf(pa, "wuq", [128, 2, 768], BF16)
            wuqrot = sbuf(pa, "wuqrot", [128, 2, 4, 64], BF16)
            wukv = sbuf(pa, "wukv", [128, 1024], BF16)
            cqn = sbuf(pa, "cqn", [128, 2, SO], BF16)
            ckvn = sbuf(pa, "ckvn", [128, S], BF16)
            kpe = sbuf(pa, "kpe", [65, S], BF16)
            qpe = sbuf(pa, "qpe", [65, 4, SO], BF16)
            m4 = sbuf(pa, "m4", [128, 4, 512], BF16)
            dma("pool", m4[:].rearrange("p a q -> p (a q)"), cm[:, 0:2048], [], ["m4"])
            stg = [sbuf(pa, "stg%d" % i, [128, 448], F32) for i in range(2)]
            sch.alias["wA"] = ["wA#%d" % c for c in range(DC)]
            for c in range(DC):
                dma("sp", stg[c % 2][:], w_in[c * 128:(c + 1) * 128, 0:448], [], ["stg%d" % (c % 2)])
                cp(wA[:, c, :], stg[c % 2][:], ["stg%d" % (c % 2)], ["wA#%d" % c], eng=("act" if c % 2 else "dve"))
            dmac("pool", wuq, w_uq[:, :], 2, [], ["wuq"])
            dma("pool", wukv[:], w_ukv[:, :], [], ["wukv"])
            dma("pool", kpe[64:65, :], kflag[:, :], [], ["kpe_flag"])
            OP("dve", lambda e: e.memset(qpe[64:65, :, :], 1.0), [], ["qpe_one"])
            OP("act", lambda e: e.mul(out=wArot[:, :, 0:32], in_=wA[:, :, 416:448], mul=-1.0), ["wA"], ["wArot_a"])
            cp(wArot[:, :, 32:64], wA[:, :, 384:416], ["wA"], ["wArot_b"])
            for h in range(4):
                b0 = h * 192 + 128
                OP("act", lambda e, h=h, b0=b0: e.mul(out=wuqrot[:, :, h, 0:32], in_=wuq[:, :, b0 + 32:b0 + 64], mul=-1.0),
                   ["wuq"], ["wuqrot_a%d" % h])
                cp(wuqrot[:, :, h, 32:64], wuq[:, :, b0:b0 + 32], ["wuq"], ["wuqrot_b%d" % h])
            WROT = ["wArot_a", "wArot_b"]
            WQROT = ["wuqrot_a%d" % h for h in range(4)] + ["wuqrot_b%d" % h for h in range(4)]

            with ExitStack() as p1:
                xt = sbuf(p1, "xt", [128, DC, 512], F32)
                u = sbuf(p1, "u", [128, DC, 512], BF16)
                sq = [sbuf(p1, "sq%d" % i, [128, 512], BF16) for i in range(3)]
                sqr = Rot([(sq[i], "sq%d" % i) for i in range(3)])
                lnt = sbuf(p1, "lnt", [128, 512], F32)
                rbc = sbuf(p1, "rbc", [128, 512], F32)
                r2 = sbuf(p1, "r2", [128, 512], F32)
                raw = sbuf(p1, "raw", [128, 2, 512], F32)
                pi = sbuf(p1, "pi", [64, 512], I32)
                tr = [sbuf(p1, "tr%d" % i, [64, 512], F32) for i in range(4)]
                tri_i = sbuf(p1, "tri_i", [64, 512], I32)
                sinT = sbuf(p1, "sinT", [64, 512], F32)
                cosT = sbuf(p1, "cosT", [64, 512], F32)
                t1 = sbuf(p1, "t1", [64, 512], F32)
                t2 = sbuf(p1, "t2", [64, 512], F32)
                psr = Rot([(PB[i], PK[i]) for i in range(8)])

                for v in (range(NT) if "A1" in phases else []):
                    own = (v % 2 == 1)
                    j = v // 2
                    tsl = slice(v * 512, (v + 1) * 512)
                    osl = slice(j * 512, (j + 1) * 512)
                    dmac("sp", xt, xT[:, tsl], DC, [], ["xt"])
                    dma("sp", pi[:], pos[0:1, tsl].partition_broadcast(64), [], ["pi"])
                    pss, pssk = psr.next()
                    for c in range(DC):
                        sqt, sqk = sqr.next()
                        act(sqt[:], xt[:, c, :], AF.Square, ["xt#%d" % c], [sqk])
                        mm(pss[:], onesb[:], sqt[:], c == 0, c == DC - 1, ["onesb", sqk], [pssk])
                    rms_scale(pss[:], D, rbc[:], lnt[:], [pssk], "rbc")
                    for c in range(DC):
                        stt(u[:, c, :], xt[:, c, :], cs[:, G_APRE + c:G_APRE + c + 1], rbc[:], ALU.mult, ALU.mult,
                            ["xt#%d" % c, "cs", "rbc"], ["u%d" % c])
                    UK = ["u%d" % c for c in range(DC)]
                    LVL = int(os.environ.get("A1LVL", "9"))
                    if LVL < 2:
                        continue
                    cp(tr[0][:], pi[:], ["pi"], ["tr0"])
                    ts(tr[1][:], tr[0][:], cs[0:64, C_INVF:C_INVF + 1], 0.0, ALU.mult, ALU.add, ["tr0", "cs"], ["tr1"])
                    cp(tri_i[:], tr[1][:], ["tr1"], ["tri_i"])
                    cp(tr[2][:], tri_i[:], ["tri_i"], ["tr2"])
                    tt(tr[3][:], tr[1][:], tr[2][:], ALU.subtract, ["tr1", "tr2"], ["tr3"])
                    act(sinT[:], tr[3][:], AF.Sin, ["tr3"], ["sinT"], scale=TWO_PI)
                    ts(tr[1][:], tr[1][:], 0.25, None, ALU.add, None, ["tr1"], ["tr1"])
                    cp(tri_i[:], tr[1][:], ["tr1"], ["tri_i"])
                    cp(tr[2][:], tri_i[:], ["tri_i"], ["tr2"])
                    tt(tr[3][:], tr[1][:], tr[2][:], ALU.subtract, ["tr1", "tr2"], ["tr3"])
                    act(cosT[:], tr[3][:], AF.Sin, ["tr3"], ["cosT"], scale=TWO_PI)
                    if LVL < 3:
                        continue
                    pkv, pkvk = psr.next()
                    for c in range(DC):
                        mm(pkv[:], wA[:, c, 256:384], u[:, c, :], c == 0, c == DC - 1, ["wA", "u%d" % c], [pkvk])
                    sqt, sqk = sqr.next()
                    act(sqt[:], pkv[:], AF.Square, [pkvk], [sqk])
                    cp(raw[:, 0, :], pkv[:], [pkvk], ["raw0"], eng="act")
                    ps2, ps2k = psr.next()
                    mm(ps2[:], onesb[:], sqt[:], True, True, ["onesb", sqk], [ps2k])
                    rms_scale(ps2[:], 128, r2[:], lnt[:], [ps2k], "r2")
                    stt(ckvn[:, tsl], raw[:, 0, :], cs[:, C_KVN:C_KVN + 1], r2[:], ALU.mult, ALU.mult,
                        ["raw0", "cs", "r2"], ["ckvn%d" % v])
                    if LVL < 4:
                        continue
                    pkr, pkrk = psr.next()
                    pkq, pkqk = psr.next()
                    for c in range(DC):
                        mm(pkr[0:64, :], wA[:, c, 384:448], u[:, c, :], c == 0, c == DC - 1, ["wA", "u%d" % c], [pkrk])
                    for c in range(DC):
                        mm(pkq[0:64, :], wArot[:, c, :], u[:, c, :], c == 0, c == DC - 1, WROT + ["u%d" % c], [pkqk])
                    tt(t1[:], pkr[0:64, :], cosT[:], ALU.mult, [pkrk, "cosT"], ["t1"])
                    tt(t2[:], pkq[0:64, :], sinT[:], ALU.mult, [pkqk, "sinT"], ["t2"])
                    tt(kpe[0:64, tsl], t1[:], t2[:], ALU.add, ["t1", "t2"], ["kpe%d" % v])
                    if not own or LVL < 5:
                        continue
                    pq = [psr.next(), psr.next()]
                    for l in range(2):
                        for c in range(DC):
                            mm(pq[l][0][:], wA[:, c, l * 128:(l + 1) * 128], u[:, c, :], c == 0, c == DC - 1,
                               ["wA", "u%d" % c], [pq[l][1]])
                    ps2, ps2k = psr.next()
                    for l in range(2):
                        sqt, sqk = sqr.next()
                        act(sqt[:], pq[l][0][:], AF.Square, [pq[l][1]], [sqk])
                        cp(raw[:, l, :], pq[l][0][:], [pq[l][1]], ["raw%d" % l], eng="act")
                        mm(ps2[:], onesb[:], sqt[:], l == 0, l == 1, ["onesb", sqk], [ps2k])
                    rms_scale(ps2[:], 256, r2[:], lnt[:], [ps2k], "r2")
                    for l in range(2):
                        stt(cqn[:, l, osl], raw[:, l, :], cs[:, C_QN + l:C_QN + l + 1], r2[:], ALU.mult, ALU.mult,
                            ["raw%d" % l, "cs", "r2"], ["cqn%d_%d" % (j, l)])
                    for h in (range(4) if LVL >= 6 else []):
                        pqr, pqrk = psr.next()
                        pqq, pqqk = psr.next()
                        b0 = h * 192 + 128
                        for l in range(2):
                            mm(pqr[0:64, :], wuq[:, l, b0:b0 + 64], cqn[:, l, osl], l == 0, l == 1,
                               ["wuq", "cqn%d_%d" % (j, l)], [pqrk])
                        for l in range(2):
                            mm(pqq[0:64, :], wuqrot[:, l, h, :], cqn[:, l, osl], l == 0, l == 1,
                               WQROT + ["cqn%d_%d" % (j, l)], [pqqk])
                        tt(t1[:], pqr[0:64, :], cosT[:], ALU.mult, [pqrk, "cosT"], ["t1"])
                        tt(t2[:], pqq[0:64, :], sinT[:], ALU.mult, [pqqk, "sinT"], ["t2"])
                        tt(t1[:], t1[:], t2[:], ALU.add, ["t1", "t2"], ["t1"])
                        ts(qpe[0:64, h, osl], t1[:], QSCALE, None, ALU.mult, None, ["t1"], ["qpe%d_%d" % (j, h)])
            sch.barrier()

            for n in range(NFF):
                dma("pool", wgu_s[n, :, :], wgu[n, :, :, :].rearrange("p c f -> p (c f)"), [], ["wgus%d" % n])
            for ct in range(DC):
                dma("pool", wdn_s[ct, :, :], wdn[ct, :, :, :].rearrange("p n f -> p (n f)"), [], ["wdns%d" % ct])
            with ExitStack() as p2:
                Kh = sbuf(p2, "Kh", [128, S], BF16)
                Vh = sbuf(p2, "Vh", [128, NB, 128], BF16)
                Qn = sbuf(p2, "Qn", [128, SO], BF16)
                ptl = [sbuf(p2, "pt%d" % i, [128, 512], BF16) for i in range(5)]
                rden = sbuf(p2, "rden", [128, 512], F32)
                dsum = sbuf(p2, "dsum", [128, 512], F32)
                for h in (range(4) if "B" in phases else []):
                    psr = Rot([(PB[i], PK[i]) for i in range(4)])
                    ci = 0
                    for v in range(NT):
                        tsl = slice(v * 512, (v + 1) * 512)
                        ps, pk = psr.next()
                        mm(ps[:], wukv[:, h * 256:h * 256 + 128], ckvn[:, tsl], True, True, ["wukv"], [pk])
                        cp(Kh[:, tsl], ps[:], [pk], ["Kh%d" % v], eng=("act" if ci % 2 else "dve"))
                        ci += 1
                        ps, pk = psr.next()
                        for bl in range(4):
                            mm(ps[:, bl * 128:(bl + 1) * 128], ckvn[:, v * 512 + bl * 128:v * 512 + (bl + 1) * 128],
                               wukv[:, h * 256 + 128:h * 256 + 256], True, True, ["wukv"], [pk])
                        cp(Vh[:, v * 4:(v + 1) * 4, :].rearrange("p a d -> p (a d)"), ps[:], [pk], ["Vh%d" % v],
                           eng=("act" if ci % 2 else "dve"))
                        ci += 1
                    for j in range(NO):
                        osl = slice(j * 512, (j + 1) * 512)
                        ps, pk = psr.next()
                        for l in range(2):
                            mm(ps[:], wuq[:, l, h * 192:h * 192 + 128], cqn[:, l, osl], l == 0, l == 1, ["wuq"], [pk])
                        OP("act", lambda e, ps=ps, osl=osl: e.mul(out=Qn[:, osl], in_=ps[:], mul=QSCALE), [pk], ["Qn%d" % j])
                    pss = Rot([(PB[i], PK[i]) for i in range(5)])
                    pacc = Rot([((PB[5], PK[5]), (PB[7], PK[7])), ((PB[6], PK[6]), (PB[7], PK[7]))])
                    ptr = Rot([(ptl[i], "pt%d" % i) for i in range(5)])
                    for j in range(NO):
                        osl = slice(j * 512, (j + 1) * 512)
                        nfull = 4 * (2 * j + 1)
                        nkb = nfull + 4
                        (oacc, oak), (dacc, dak) = pacc.next()
                        pend = []

                        def issue_qk(kb):
                            st, stk = pss.next()
                            ksl = slice(kb * 128, (kb + 1) * 128)
                            mm(st[:], Kh[:, ksl], Qn[:, osl], True, False, ["Kh%d" % (kb // 4), "Qn%d" % j], [stk])
                            if kb >= nfull:
                                mm(st[:], identb[:], m4[:, kb - nfull, :], False, False, ["identb", "m4"], [stk])
                            mm(st[:], kpe[0:65, ksl], qpe[0:65, h, osl], False, True, [], [stk])
                            pt, ptk = ptr.next()
                            act(pt[:], st[:], AF.Exp, [stk], [ptk])
                            pend.append((kb, pt, ptk))

                        def issue_pv():
                            kb, pt, ptk = pend.pop(0)
                            mm(oacc[:], Vh[:, kb, :], pt[:], kb == 0, kb == nkb - 1, ["Vh%d" % (kb // 4), ptk], [oak])
                            if kb == 0:
                                cp(dsum[:], pt[:], [ptk], ["dsum"])
                            else:
                                tt(dsum[:], dsum[:], pt[:], ALU.add, ["dsum", ptk], ["dsum"])

                        for kb in range(nkb):
                            issue_qk(kb)
                            if len(pend) > 3:
                                issue_pv()
                        while pend:
                            issue_pv()
                        mm(dacc[:], onesf[:], dsum[:], True, True, ["onesf", "dsum"], [dak])
                        OP("dve", lambda e, dacc=dacc: e.reciprocal(out=rden[:], in_=dacc[:]), [dak], ["rden"])
                        tt(catA[:, h, osl], oacc[:], rden[:], ALU.mult, [oak, "rden"], ["catA%d_%d" % (h, j)])
            sch.barrier()

        with ExitStack() as pm:
            catB = sbuf(pm, "catB", [128, 4, SO], BF16)
            with ExitStack() as p3:
                wM = sbuf(p3, "wM", [128, DC, 2056], BF16)
                dmac("pool", wM, w_in[:, 448:2504], DC, [], ["wM"])
                gainbc = sbuf(p3, "gainbc", [128, 512], F32)
                gbbc = sbuf(p3, "gbbc", [128, 8], F32)
                vld = sbuf(p3, "vld", [128, NB], F32)
                vldb = sbuf(p3, "vldb", [128, NB], BF16)
                dma("sp", gainbc[:], gainrow[0:1, :].partition_broadcast(128), [], ["gainbc"])
                dma("sp", gbbc[:], gbrow[0:1, :].partition_broadcast(128), [], ["gbbc"])
                dma("sp", vld[:], validT[:, :], [], ["vld"])
                cp(vldb[:], vld[:], ["vld"], ["vldb"])
                xt = sbuf(p3, "xt2", [128, DC, 512], F32)
                u = sbuf(p3, "u2", [128, DC, 512], BF16)
                sq = [sbuf(p3, "sq2_%d" % i, [128, 512], BF16) for i in range(3)]
                sqr = Rot([(sq[i], "sq%d" % i) for i in range(3)])
                lnt = sbuf(p3, "lnt2", [128, 512], F32)
                rbc = sbuf(p3, "rbc2", [128, 512], F32)
                pc = sbuf(p3, "pc", [128, 8, 515], F32)
                cacc = [sbuf(p3, "cacc%d" % i, [128, 512], F32) for i in range(2)]
                caccr = Rot([(cacc[i], "cacc%d" % i) for i in range(2)])
                ctmp = sbuf(p3, "ctmp", [128, 512], F32)
                qkT = sbuf(p3, "qkT", [128, 8, 512], BF16)
                vtm = sbuf(p3, "vtm", [128, 4, 512], BF16)
                og = sbuf(p3, "og", [128, 4, 512], BF16)
                gsbs = [sbuf(p3, "gsb%d" % i, [128, 8], F32) for i in range(4)]
                lfns = [sbuf(p3, "lfn%d" % i, [128, 4], F32) for i in range(4)]
                sms = [sbuf(p3, "sm%d" % i, [128, 64], F32) for i in range(4)]
                trilfs = [sbuf(p3, "trilf%d" % i, [128, 4, 128], F32) for i in range(4)]
                dTs = [sbuf(p3, "dT%d" % i, [128, 4, 128], BF16) for i in range(4)]
                sTs = [sbuf(p3, "sT%d" % i, [128, 4, 128], BF16) for i in range(4)]
                tmpn = sbuf(p3, "tmpn", [128, 4, 128], F32)
                num = sbuf(p3, "num", [128, 4, 128], F32)
                sqj = sbuf(p3, "sqj", [128, 128], F32)
                GO = sbuf(p3, "GO", [128, 512], F32)
                mouts = [sbuf(p3, "mout%d" % i, [128, 512], BF16) for i in range(2)]
                kws = [sbuf(p3, "kw%d" % i, [128, 4, 128], BF16) for i in range(4)]
                Cst = sbuf(p3, "Cst", [128, 4, 128], F32)
                nst = sbuf(p3, "nst", [128, 4], F32)
                Cbf = sbuf(p3, "Cbf", [128, 4, 128], BF16)
                nbf = sbuf(p3, "nbf", [128, 4], BF16)
                OP("dve", lambda e: e.memset(pc[:], 0.0), [], ["pc"])
                OP("dve", lambda e: e.memset(Cst[:], 0.0), [], ["Cst"])
                OP("dve", lambda e: e.memset(nst[:], 0.0), [], ["nst"])
                OP("dve", lambda e: e.memset(Cbf[:], 0.0), [], ["Cbf"])
                OP("dve", lambda e: e.memset(nbf[:], 0.0), [], ["nbf"])
                psr = Rot([(PB[i], PK[i]) for i in range(7)])
                MQ, MK, MV, MO, MG = 0, 512, 1024, 1536, 2048
                pst = Rot([(PB[i], PK[i]) for i in (4, 5, 6)])

                for v in (range(NT) if "A2" in phases else []):
                    own = (v % 2 == 1)
                    j = v // 2
                    tsl = slice(v * 512, (v + 1) * 512)
                    dmac("sp", xt, xT[:, tsl], DC, [], ["xt"])
                    pss, pssk = psr.next()
                    for c in range(DC):
                        sqt, sqk = sqr.next()
                        act(sqt[:], xt[:, c, :], AF.Square, ["xt#%d" % c], [sqk])
                        mm(pss[:], onesb[:], sqt[:], c == 0, c == DC - 1, ["onesb", sqk], [pssk])
                    rms_scale(pss[:], D, rbc[:], lnt[:], [pssk], "rbc")
                    for c in range(DC):
                        stt(u[:, c, :], xt[:, c, :], cs[:, G_APRE + c:G_APRE + c + 1], rbc[:], ALU.mult, ALU.mult,
                            ["xt#%d" % c, "cs", "rbc"], ["u%d" % c])
                    UK = ["u%d" % c for c in range(DC)]
                    cp(pc[:, :, 0:3], pc[:, :, 512:515], ["pc"] + ["pc%d" % ct for ct in range(8)], ["pc"])
                    for ct in range(8):
                        ps, pk = psr.next()
                        for c in range(DC):
                            mm(ps[:], wM[:, c, ct * 128:(ct + 1) * 128], u[:, c, :], c == 0, c == DC - 1,
                               ["wM", "u%d" % c], [pk])
                        cp(pc[:, ct, 3:515], ps[:], [pk, "pc"], ["pc%d" % ct], eng="act")
                        if ct < 4 and not own:
                            continue
                        ca, cak = caccr.next()
                        w0 = C_CW + ct * 4
                        ceng = "dve"
                        ts(ca[:], pc[:, ct, 0:512], cs[:, w0:w0 + 1], cs[:, C_CB + ct:C_CB + ct + 1], ALU.mult, ALU.add,
                           ["pc", "pc%d" % ct, "cs"], [cak], eng=ceng)
                        for tap in range(1, 4):
                            if ceng == "pool":
                                ts(ctmp[:], pc[:, ct, tap:tap + 512], cs[:, w0 + tap:w0 + tap + 1], None, ALU.mult, None,
                                   ["pc", "pc%d" % ct, "cs"], ["ctmp"], eng="pool")
                                tt(ca[:], ca[:], ctmp[:], ALU.add, [cak, "ctmp"], [cak], eng="pool")
                            else:
                                stt(ca[:], pc[:, ct, tap:tap + 512], cs[:, w0 + tap:w0 + tap + 1], ca[:], ALU.mult, ALU.add,
                                    ["pc", "pc%d" % ct, "cs", cak], [cak])
                        act(qkT[:, ct, :], ca[:], AF.Silu, [cak], ["qkT%d" % ct])
                    gps, gpk = PB[7], PK[7]
                    for bl in range(4):
                        bs = slice(bl * 128, (bl + 1) * 128)
                        ps, pk = psr.next()
                        for c in range(DC):
                            mm(ps[:], u[:, c, bs], wM[:, c, MV:MV + 512], c == 0, c == DC - 1, ["wM", "u%d" % c], [pk])
                        cp(vtm[:, bl, :], ps[:], [pk], ["vtm%d" % bl], eng="act")
                        if own:
                            ps, pk = psr.next()
                            for c in range(DC):
                                mm(ps[:], u[:, c, bs], wM[:, c, MO:MO + 512], c == 0, c == DC - 1, ["wM", "u%d" % c], [pk])
                            act(og[:, bl, :], ps[:], AF.Sigmoid, [pk], ["og%d" % bl])
                        for c in range(DC):
                            mm(gps[:, bl * 8:(bl + 1) * 8], u[:, c, bs], wM[:, c, MG:MG + 8], c == 0, c == DC - 1,
                               ["wM", "u%d" % c], [gpk])
                    def blk_pre(bl):
                        X = "_%d" % bl
                        gsb, lfn, sm, sT, kw = gsbs[bl], lfns[bl], sms[bl], sTs[bl], kws[bl]
                        trilf, trk = trilfs[bl], "trilf%d" % bl
                        dT, dtk = dTs[bl], "dT%d" % bl
                        bs = slice(bl * 128, (bl + 1) * 128)
                        tt(gsb[:], gps[:, bl * 8:(bl + 1) * 8], gbbc[:], ALU.add, [gpk, "gbbc"], ["gsb" + X])
                        act(sm[:, 0:4], gsb[:, 4:8], AF.Exp, ["gsb" + X], ["sm_e" + X], scale=-1.0)
                        act(lfn[:], sm[:, 0:4], AF.Ln, ["sm_e" + X], ["lfn" + X], bias=1.0)
                        cps, cpk = PB[bl], PK[bl]
                        mm(cps[:, 0:4], trif[:], lfn[:], True, True, ["trif", "lfn" + X], [cpk])
                        mm(cps[:, 4:8], onesf[:], lfn[:], True, True, ["onesf", "lfn" + X], [cpk])
                        ts(sm[:, 4:8], gsb[:, 0:4], LNK, None, ALU.add, None, ["gsb" + X], ["sm_bd" + X])
                        tt(sm[:, 4:8], sm[:, 4:8], cps[:, 0:4], ALU.add, ["sm_bd" + X, cpk], ["sm_bd" + X])
                        tt(sm[:, 8:12], sm[:, 4:8], cps[:, 4:8], ALU.subtract, ["sm_bd" + X, cpk], ["sm_wl" + X])
                        act(sm[:, 12:16], sm[:, 8:12], AF.Exp, ["sm_wl" + X], ["sm_w" + X])
                        act(sm[:, 16:20], cps[:, 4:8], AF.Exp, [cpk], ["sm_eg" + X], scale=-1.0)
                        act(sm[:, 20:24], cps[:, 0:4], AF.Exp, [cpk], ["sm_eb" + X], scale=-1.0)
                        if own:
                            tt(trilf[:], trif[:].unsqueeze(1).to_broadcast([128, 4, 128]),
                               lfn[:].unsqueeze(2).to_broadcast([128, 4, 128]), ALU.mult, ["trif", "lfn" + X], [trk])
                            bps, bpk = PB[bl], PK[bl]
                            for h in range(4):
                                mm(bps[:, h * 128:(h + 1) * 128], onesf[:], trilf[:, h, :], True, False, ["onesf", trk], [bpk])
                                mm(bps[:, h * 128:(h + 1) * 128], identf[:], mposf[:], False, True, ["identf", "mposf"], [bpk])
                            for h in range(4):
                                act(dT[:, h, :], bps[:, h * 128:(h + 1) * 128], AF.Exp, [bpk, "sm_bd" + X], [dtk + "h%d" % h],
                                    bias=sm[:, 4 + h:5 + h], scale=-1.0)
                            kps, kpk = PB[bl], PK[bl]
                            for h in range(4):
                                mm(kps[:, h * 128:(h + 1) * 128], qkT[:, 4 + h, bs], qkT[:, h, bs], True, True,
                                   ["qkT%d" % (4 + h), "qkT%d" % h], [kpk])
                            tt(sT[:].rearrange("p h j -> p (h j)"), kps[:], dT[:].rearrange("p h j -> p (h j)"), ALU.mult,
                               [kpk] + [dtk + "h%d" % h for h in range(4)], ["sT" + X])
                        tps, tpk = PB[bl], PK[bl]
                        for h in range(4):
                            mm(tps[:, h * 128:(h + 1) * 128], qkT[:, 4 + h, bs], identb[:], True, True,
                               ["qkT%d" % (4 + h), "identb"], [tpk])
                        tt(kw[:], tps[:].rearrange("p (h d) -> p h d", h=4),
                           sm[:, 12:16].unsqueeze(2).to_broadcast([128, 4, 128]), ALU.mult, [tpk, "sm_w" + X], ["kw" + X])

                    def blk_tail(bl):
                        X = "_%d" % bl
                        sm, sT, kw = sms[bl], sTs[bl], kws[bl]
                        gb = v * 4 + bl
                        bs = slice(bl * 128, (bl + 1) * 128)
                        tok0 = j * 512 + bl * 128
                        if own:
                            ips, ipk = pst.next()
                            eps_, epk = pst.next()
                            for h in range(4):
                                hs = slice(h * 128, (h + 1) * 128)
                                mm(ips[:, hs], sT[:, h, :], vtm[:, bl, hs], True, True, ["sT" + X, "vtm%d" % bl], [ipk])
                                mm(PB[7][:, 32 + h:33 + h], sT[:, h, :], vldb[:, gb:gb + 1], True, True, ["sT" + X, "vldb"], [PK[7]])
                                mm(eps_[:, hs], qkT[:, h, bs], Cbf[:, h, :], True, True, ["qkT%d" % h, "Cbf"], [epk])
                                mm(PB[7][:, 36 + h:37 + h], qkT[:, h, bs], nbf[:, h:h + 1], True, True, ["qkT%d" % h, "nbf"], [PK[7]])
                        ups, upk = pst.next()
                        for h in range(4):
                            hs = slice(h * 128, (h + 1) * 128)
                            mm(ups[:, hs], kw[:, h, :], vtm[:, bl, hs], True, True, ["kw" + X, "vtm%d" % bl], [upk])
                            mm(PB[7][:, 40 + h:41 + h], kw[:, h, :], vldb[:, gb:gb + 1], True, True, ["kw" + X, "vldb"], [PK[7]])
                        tt(Cst[:], Cst[:], sm[:, 16:20].unsqueeze(2).to_broadcast([128, 4, 128]), ALU.mult, ["Cst", "sm_eg" + X], ["Cst"])
                        tt(Cst[:].rearrange("p h d -> p (h d)"), Cst[:].rearrange("p h d -> p (h d)"), ups[:], ALU.add,
                           ["Cst", upk], ["Cst"])
                        tt(nst[:], nst[:], sm[:, 16:20], ALU.mult, ["nst", "sm_eg" + X], ["nst"])
                        tt(nst[:], nst[:], PB[7][:, 40:44], ALU.add, ["nst", PK[7]], ["nst"])
                        cp(Cbf[:], Cst[:], ["Cst"], ["Cbf"], eng="act")
                        cp(nbf[:], nst[:], ["nst"], ["nbf"])
                        if not own:
                            return
                        tt(sm[:, 24:28], PB[7][:, 36:40], sm[:, 20:24], ALU.mult, [PK[7], "sm_eb" + X], ["sm_d1" + X])
                        tt(sm[:, 28:32], sm[:, 24:28], PB[7][:, 32:36], ALU.add, ["sm_d1" + X, PK[7]], ["sm_den" + X])
                        act(sm[:, 32:36], sm[:, 28:32], AF.Abs, ["sm_den" + X], ["sm_abs" + X])
                        ts(sm[:, 32:36], sm[:, 32:36], 1.0, None, ALU.max, None, ["sm_abs" + X], ["sm_abs" + X])
                        OP("dve", lambda e: e.reciprocal(out=sm[:, 36:40], in_=sm[:, 32:36]), ["sm_abs" + X], ["sm_rden" + X])
                        tt(tmpn[:], eps_[:].rearrange("p (h d) -> p h d", h=4),
                           sm[:, 20:24].unsqueeze(2).to_broadcast([128, 4, 128]), ALU.mult, [epk, "sm_eb" + X], ["tmpn"])
                        tt(num[:].rearrange("p h d -> p (h d)"), ips[:], tmpn[:].rearrange("p h d -> p (h d)"), ALU.add,
                           [ipk, "tmpn"], ["num"])
                        OP("dve", lambda e: e.memset(sm[:, 40:44], 0.0), [], ["sm_ss%d" % h + X for h in range(4)])
                        for h in range(4):
                            act(sqj[:], num[:, h, :], AF.Square, ["num"], ["sqj", "sm_ss%d" % h + X], accum_out=sm[:, 40 + h:41 + h])
                        tt(sm[:, 44:48], sm[:, 36:40], sm[:, 36:40], ALU.mult, ["sm_rden" + X], ["sm_r2" + X])
                        tt(sm[:, 44:48], sm[:, 44:48], sm[:, 40:44], ALU.mult, ["sm_r2" + X] + ["sm_ss%d" % h + X for h in range(4)], ["sm_r2" + X])
                        act(sm[:, 48:52], sm[:, 44:48], AF.Ln, ["sm_r2" + X], ["sm_ln" + X], bias=EPS, scale=1.0 / 128)
                        act(sm[:, 52:56], sm[:, 48:52], AF.Exp, ["sm_ln" + X], ["sm_rs" + X], scale=-0.5)
                        tt(sm[:, 56:60], sm[:, 52:56], sm[:, 36:40], ALU.mult, ["sm_rs" + X, "sm_rden" + X], ["sm_f" + X])
                        tt(GO[:], gainbc[:], og[:, bl, :], ALU.mult, ["gainbc", "og%d" % bl], ["GO"], eng=("dve" if os.environ.get("NOPOOL") else "pool"))
                        tt(tmpn[:], num[:], sm[:, 56:60].unsqueeze(2).to_broadcast([128, 4, 128]), ALU.mult,
                           ["num", "sm_f" + X], ["tmpn"])
                        mout = mouts[bl % 2]
                        tt(mout[:], tmpn[:].rearrange("p h d -> p (h d)"), GO[:], ALU.mult, ["tmpn", "GO"], ["mout%d" % (bl % 2)])

                    def blk_tail_b(bl):
                        if not own:
                            return
                        mout = mouts[bl % 2]
                        tok0 = j * 512 + bl * 128
                        tps, tpk = PB[bl], PK[bl]
                        for h in range(4):
                            mm(tps[:, h * 128:(h + 1) * 128], mout[:, h * 128:(h + 1) * 128], identb[:], True, True,
                               ["mout%d" % (bl % 2), "identb"], [tpk])
                        cp(catB[:, :, tok0:tok0 + 128], tps[:].rearrange("p (h t) -> p h t", h=4), [tpk],
                           ["catB%d_%d" % (j, bl)], eng="act")

                    streams = []
                    for bl in range(4):
                        sch.rec = []
                        blk_pre(bl)
                        streams.append(sch.rec)
                        sch.rec = None
                    sch.replay_rr(streams)
                    for bl in range(4):
                        blk_tail(bl)
                        if bl > 0:
                            blk_tail_b(bl - 1)
                    blk_tail_b(3)
            sch.barrier()

            with ExitStack() as p4:
                wo = sbuf(p4, "wo", [128, DC, D], BF16)
                dmac("pool", wo, w_out[:, :], DC, [], ["wo"])
                xts4 = [sbuf(p4, "xt4_%d" % i, [128, DC, 512], F32) for i in range(2)]
                mix = sbuf(p4, "mix", [128, DC, 512], F32)
                sq = [sbuf(p4, "sq4_%d" % i, [128, 512], BF16) for i in range(3)]
                sqr = Rot([(sq[i], "sq%d" % i) for i in range(3)])
                lnt = sbuf(p4, "lnt4", [128, 512], F32)
                rbc = sbuf(p4, "rbc4", [128, 512], F32)
                tm = [sbuf(p4, "tm4_%d" % i, [128, 512], F32) for i in range(2)]
                tmr = Rot([(tm[i], "tm%d" % i) for i in range(2)])
                h1 = sbuf(p4, "h1_4", [128, DC, 512], F32)
                psr = Rot([(PB[i], PK[i]) for i in range(6)])
                for j in (range(NO) if "Bp" in phases else []):
                    v = 2 * j + 1
                    tsl = slice(v * 512, (v + 1) * 512)
                    osl = slice(j * 512, (j + 1) * 512)
                    xt = xts4[j % 2]
                    xk = "xq%d" % (j % 2)
                    dmac("sp", xt, xT[:, tsl], DC, [], [xk])
                    pss, pssk = PB[7], PK[7]
                    for ct in range(DC):
                        ps, pk = psr.next()
                        for f in range(8):
                            rhs = catA[:, f, osl] if f < 4 else catB[:, f - 4, osl]
                            mm(ps[:], wo[:, f, ct * 128:(ct + 1) * 128], rhs, f == 0, f == 7, ["wo"], [pk])
                        sqt, sqk = sqr.next()
                        act(sqt[:], ps[:], AF.Square, [pk], [sqk])
                        cp(mix[:, ct, :], ps[:], [pk], ["mix%d" % ct])
                        mm(pss[:], onesb[:], sqt[:], ct == 0, ct == DC - 1, ["onesb", sqk], [pssk])
                    rms_scale(pss[:], D, rbc[:], lnt[:], [pssk], "rbc")
                    for ct in range(DC):
                        t_, tk = tmr.next()
                        tt(t_[:], mix[:, ct, :], rbc[:], ALU.mult, ["mix%d" % ct, "rbc"], [tk])
                        stt(h1[:, ct, :], t_[:], cs[:, G_APOST + ct:G_APOST + ct + 1], xt[:, ct, :], ALU.mult, ALU.add,
                            [tk, "cs", xk], ["h1"])
                    dmac_out("sp", h1d[:, osl], h1, DC, ["h1"], ["h1d%d" % j])
                    if debug:
                        dmac_out("sp", dbg_h1[:, osl], h1, DC, ["h1"], [])
                if debug:
                    for hh in range(4):
                        dma("pool", dbg_cat[:, hh * SO:(hh + 1) * SO], catA[:, hh, :], [], [])
                        dma("pool", dbg_cat[:, (4 + hh) * SO:(5 + hh) * SO], catB[:, hh, :], [], [])
            sch.barrier()

        with ExitStack() as p5:
            wpg = sbuf(p5, "wpg", [128, DC, D], BF16)
            wpp = sbuf(p5, "wpp", [128, 2, D], BF16)
            dmac("pool", wpg, w_pg[:, :], DC, [], ["wpg"])
            dmac("pool", wpp, w_pp[:, :], 2, [], ["wpp"])
            wg = [sbuf(p5, "wg%d" % i, [128, DC, 256], BF16) for i in range(3)]
            wgr = Rot([(wg[i], "wg%d" % i) for i in range(3)])
            wd = [sbuf(p5, "wd%d" % i, [128, NFF, 128], BF16) for i in range(2)]
            wdr = Rot([(wd[i], "wd%d" % i) for i in range(2)])
            h1s = [sbuf(p5, "h1_5%d" % i, [128, DC, 512], F32) for i in range(2)]
            fo = sbuf(p5, "fo", [128, DC, 512], F32)
            fn = sbuf(p5, "fn", [128, DC, 512], BF16)
            h2b = sbuf(p5, "h2b", [128, DC, 512], BF16)
            actT = sbuf(p5, "actT", [128, NFF, 512], BF16)
            sq = [sbuf(p5, "sq5_%d" % i, [128, 512], BF16) for i in range(3)]
            sqr = Rot([(sq[i], "sq%d" % i) for i in range(3)])
            lntA = sbuf(p5, "lnt5a", [128, 512], F32)
            lntB = sbuf(p5, "lnt5b", [128, 512], F32)
            rbcA = sbuf(p5, "rbc5a", [128, 512], F32)
            rbcB = sbuf(p5, "rbc5b", [128, 512], F32)
            tm = [sbuf(p5, "tm5_%d" % i, [128, 512], F32) for i in range(3)]
            tmr = Rot([(tm[i], "tm%d" % i) for i in range(3)])
            ptfs = [sbuf(p5, "pt_f%d" % i, [128, 2, 512], F32) for i in range(2)]
            ptbs = [sbuf(p5, "pt_b%d" % i, [128, 2, 512], BF16) for i in range(2)]
            psr = Rot([(PB[i], PK[i]) for i in range(7)])

            def front_a(j):
                b = j % 2
                osl = slice(j * 512, (j + 1) * 512)
                h1 = h1s[b]
                dmac("sp", h1, h1d[:, osl], DC, ["h1d%d" % j], ["h1%d" % b])
                dmac("sp", ptfs[b], pT[:, osl], 2, [], ["ptf%d" % b])
                cp(ptbs[b][:], ptfs[b][:], ["ptf%d" % b], ["ptb%d" % b])
                pss, pssk = PB[7], PK[7]
                for c in range(DC):
                    sqt, sqk = sqr.next()
                    act(sqt[:], h1[:, c, :], AF.Square, ["h1%d#%d" % (b, c)], [sqk])
                    mm(pss[:], onesb[:], sqt[:], c == 0, c == DC - 1, ["onesb", sqk], [pssk])
                rms_scale(pss[:], D, rbcA[:], lntA[:], [pssk], "rbcA")

            def front_b(j):
                b = j % 2
                h1 = h1s[b]
                for c in range(DC):
                    stt(fn[:, c, :], h1[:, c, :], cs[:, G_FPRE + c:G_FPRE + c + 1], rbcA[:], ALU.mult, ALU.mult,
                        ["h1%d" % b, "cs", "rbcA"], ["fn%d" % c])

            def gateup(j, n0=0, n1=NFF, mid=None):
                for n in range(n0, n1):
                    if n == NFF // 2 and mid is not None:
                        mid()
                    w_, wk = wgr.next()
                    dma("sp", w_[:].rearrange("p c f -> p (c f)"), wgu_s[n, :, :], [], [wk])
                    pg, pgk = psr.next()
                    pu, puk = psr.next()
                    for c in range(DC):
                        mm(pg[:], w_[:, c, 0:128], fn[:, c, :], c == 0, c == DC - 1, [wk, "fn%d" % c], [pgk])
                    for c in range(DC):
                        mm(pu[:], w_[:, c, 128:256], fn[:, c, :], c == 0, c == DC - 1, [wk, "fn%d" % c], [puk])
                    t_, tk = tmr.next()
                    act(t_[:], pg[:], AF.Silu, [pgk], [tk])
                    tt(actT[:, n, :], pu[:], t_[:], ALU.mult, [puk, tk], ["actT%d" % n])

            def down_post_ple(j):
                b = j % 2
                osl = slice(j * 512, (j + 1) * 512)
                h1 = h1s[b]
                hk = "h1%d" % b
                pss, pssk = PB[7], PK[7]
                for ct in range(DC):
                    w_, wk = wdr.next()
                    dma("sp", w_[:].rearrange("p n f -> p (n f)"), wdn_s[ct, :, :], [], [wk])
                    ps, pk = psr.next()
                    for n in range(NFF):
                        mm(ps[:], w_[:, n, :], actT[:, n, :], n == 0, n == NFF - 1, [wk, "actT%d" % n], [pk])
                    sqt, sqk = sqr.next()
                    act(sqt[:], ps[:], AF.Square, [pk], [sqk])
                    cp(fo[:, ct, :], ps[:], [pk], ["fo%d" % ct])
                    mm(pss[:], onesb[:], sqt[:], ct == 0, ct == DC - 1, ["onesb", sqk], [pssk])
                rms_scale(pss[:], D, rbcB[:], lntB[:], [pssk], "rbcB")
                for ct in range(DC):
                    t_, tk = tmr.next()
                    tt(t_[:], fo[:, ct, :], rbcB[:], ALU.mult, ["fo%d" % ct, "rbcB"], [tk])
                    stt(h1[:, ct, :], t_[:], cs[:, G_FPOST + ct:G_FPOST + ct + 1], h1[:, ct, :], ALU.mult, ALU.add,
                        [tk, "cs", hk], [hk])
                    cp(h2b[:, ct, :], h1[:, ct, :], [hk], ["h2b%d" % ct], eng="act")

            def ple(j):
                b = j % 2
                osl = slice(j * 512, (j + 1) * 512)
                h1 = h1s[b]
                hk = "h1%d" % b
                for ct in range(DC):
                    pg, pgk = psr.next()
                    pp, ppk = psr.next()
                    for c in range(DC):
                        mm(pg[:], wpg[:, c, ct * 128:(ct + 1) * 128], h2b[:, c, :], c == 0, c == DC - 1, ["wpg", "h2b%d" % c], [pgk])
                    for c in range(2):
                        mm(pp[:], wpp[:, c, ct * 128:(ct + 1) * 128], ptbs[b][:, c, :], c == 0, c == 1, ["wpp", "ptb%d" % b], [ppk])
                    t_, tk = tmr.next()
                    act(t_[:], pg[:], AF.Sigmoid, [pgk], [tk])
                    tt(t_[:], pp[:], t_[:], ALU.mult, [ppk, tk], [tk])
                    tt(fo[:, ct, :], t_[:], h1[:, ct, :], ALU.add, [tk, hk], ["fo%d" % ct])
                dmac_out("sp", yT[:, osl], fo, DC, ["fo%d" % c for c in range(DC)], ["yT%d" % j])

            if "C" in phases:
                front_a(0)
                front_b(0)
                NSPLIT = 6
                gateup(0, mid=(lambda: front_a(1)) if NO > 1 else None)
                for j in range(NO):
                    if j + 1 < NO:
                        front_b(j + 1)
                    down_post_ple(j)
                    if j + 1 < NO:
                        gateup(j + 1, 0, NSPLIT)
                    ple(j)
                    if j + 1 < NO:
                        gateup(j + 1, NSPLIT, NFF, mid=(lambda j=j: front_a(j + 2)) if j + 2 < NO else None)
            sch.barrier()
        sch.emit()
    return nc


def _consts():
    cmv = np.zeros((128, 4 * 512 + 4 * 128), np.float32)
    k = np.arange(128)[:, None]
    q = np.arange(512)[None, :]
    for a in range(4):
        cmv[:, a * 512:(a + 1) * 512] = np.where(k + a * 128 <= q, 0.0, NEG)
    t = np.arange(128)[:, None]
    jj = np.arange(128)[None, :]
    cmv[:, 2048:2176] = (t <= jj).astype(np.float32)
    cmv[:, 2176:2304] = np.where(t > jj, -NEG, 0.0)
    cmv[:, 2304:2432] = np.eye(128, dtype=np.float32)
    return cmv


def prep(inputs, S, ncores):
    f = lambda a: np.ascontiguousarray(np.asarray(a, dtype=np.float32))
    x = np.asarray(inputs["x"], np.float32)
    p = np.asarray(inputs["p"], np.float32)[0]
    positions = np.asarray(inputs["positions"]).astype(np.int32)
    SO = S // 2
    NB = S // 128
    cst = np.zeros((128, 96), np.float32)
    cst[:, 0:8] = f(inputs["attn_pre_norm"])[0].reshape(8, 128).T
    cst[:, 8:16] = f(inputs["attn_post_norm"])[0].reshape(8, 128).T
    cst[:, 16:24] = f(inputs["ffn_pre_norm"])[0].reshape(8, 128).T
    cst[:, 24:32] = f(inputs["ffn_post_norm"])[0].reshape(8, 128).T
    cst[:, 32:34] = f(inputs["q_norm"])[0].reshape(2, 128).T
    cst[:, 34:35] = f(inputs["kv_norm"])[0].reshape(1, 128).T
    cw = f(inputs["conv_w"])[0]
    cst[:, 35:67] = cw.reshape(4, 8, 128).transpose(2, 1, 0).reshape(128, 32)
    cst[:, 67:75] = f(inputs["conv_b"])[0].reshape(8, 128).T
    inv = (10000.0 ** (-np.arange(0, 64, 2, dtype=np.float32) / 64)).astype(np.float32)
    cst[0:64, 75] = np.concatenate([inv, inv]) / np.float32(2 * np.pi)
    gainrow = f(inputs["mlstm_norm"])[0].reshape(1, 512)
    gbrow = np.concatenate([f(inputs["gate_bias_i"])[0], f(inputs["gate_bias_f"])[0]]).reshape(1, 8)
    cmv = _consts()
    wg = f(inputs["w_gate"])[0].reshape(8, 128, NFF, 128)
    wu = f(inputs["w_up"])[0].reshape(8, 128, NFF, 128)
    wgu = np.ascontiguousarray(np.concatenate([wg, wu], axis=3).transpose(2, 1, 0, 3))
    wdn = np.ascontiguousarray(f(inputs["w_down"])[0].reshape(NFF, 128, 8, 128).transpose(2, 1, 0, 3))
    shared = {
        "cst": cst, "gainrow": gainrow, "gbrow": gbrow, "cm": cmv,
        "w_in": f(inputs["w_in"])[0], "w_uq": f(inputs["w_uq"])[0], "w_ukv": f(inputs["w_ukv"])[0],
        "w_out": f(inputs["w_out"])[0], "wgu": wgu, "wdn": wdn,
        "w_pp": f(inputs["w_ple_proj"])[0], "w_pg": f(inputs["w_ple_gate"])[0],
    }
    maps = []
    own_idx = []
    for core in range(ncores):
        b, par = core // 2, core % 2
        xb = x[b]
        if par == 1:
            xv = xb
            pv = positions[b]
            valid = np.ones(S, np.float32)
        else:
            xv = np.concatenate([np.zeros((512, D), np.float32), xb[:S - 512]], axis=0)
            pv = np.concatenate([np.zeros(512, np.int32), positions[b][:S - 512]])
            valid = np.concatenate([np.zeros(512, np.float32), np.ones(S - 512, np.float32)])
        view_tok = np.arange(S).reshape(S // 512, 512)[1::2].reshape(-1)
        glob = view_tok if par == 1 else view_tok - 512
        own_idx.append((b, glob))
        m = dict(shared)
        m["xT"] = np.ascontiguousarray(xv.T)
        m["pos"] = np.ascontiguousarray(pv.reshape(1, S))
        m["pT"] = np.ascontiguousarray(p[b][glob].T)
        m["kflag"] = np.where(valid > 0, 0.0, NEG).astype(np.float32).reshape(1, S)
        m["validT"] = np.ascontiguousarray(valid.reshape(NB, 128).T)
        maps.append(m)
    return maps, own_idx


_NC_CACHE = {}


def kernel(**inputs):
    x = np.asarray(inputs["x"])
    B, S, _ = x.shape
    ncores = 2 * B
    if S not in _NC_CACHE:
        _NC_CACHE[S] = build(S)
    nc = _NC_CACHE[S]
    maps, own_idx = prep(inputs, S, ncores)
    res = run_bass_kernel_spmd(nc, maps, core_ids=list(range(ncores)))
    out = np.zeros((B, S, D), np.float32)
    for core in range(ncores):
        b, glob = own_idx[core]
        out[b, glob, :] = np.asarray(res.results[core]["yT"]).T
    return out
```

```python
import math
import os
from contextlib import ExitStack

import numpy as np
import concourse.bass as bass
import concourse.mybir as mybir
from concourse.bass_utils import run_bass_kernel_spmd

F32 = mybir.dt.float32
BF16 = mybir.dt.bfloat16
I32 = mybir.dt.int32
AF = mybir.ActivationFunctionType
ALU = mybir.AluOpType

D = 1024
DC = 8
DIN = 2504
DFF = 2816
NFF = 22
PLE = 256
EPS = 1e-6
QSCALE = 192.0 ** -0.5
LNK = math.log(128.0 ** -0.5)
TWO_PI = 2.0 * math.pi * (1.0 - 1e-6)
NEG = -30000.0

EPOCH = 12000
NDMA = 4


class Tok:
    __slots__ = ("sem", "val", "implied", "eng")

    def __init__(self, sem, val, implied, eng):
        self.sem = sem
        self.val = val
        self.implied = implied
        self.eng = eng


class Sched:
    ENGS = ("pe", "act", "dve", "pool", "sp")

    def __init__(self, nc, stack):
        self.nc = nc
        self.stack = stack
        self.ops = {e: [] for e in self.ENGS}
        self.cnt = {e: 0 for e in self.ENGS}
        self.esems = {e: [] for e in self.ENGS}
        self.known = {e: {} for e in self.ENGS}
        self.dsems = {}
        self.dcnt = {}
        self.dlast = {}
        self.ndma = {e: 0 for e in self.ENGS}
        self.last_write = {}
        self.readers = {}
        self.alias = {}
        self.rec = None

    def _expand(self, keys):
        out = []
        for k in keys:
            out.extend(self.alias.get(k, (k,)))
        return out

    def _newsem(self, name):
        return self.stack.enter_context(self.nc.semaphore(name))

    def _esem(self, eng, epoch):
        lst = self.esems[eng]
        while len(lst) <= epoch:
            lst.append(self._newsem("s_%s_%d" % (eng, len(lst))))
        return lst[epoch]

    def _need(self, eng, tok, waits):
        if tok is None:
            return
        kn = self.known[eng]
        if kn.get(tok.sem, 0) >= tok.val:
            return
        if tok.eng == "pe" and eng == "pe":
            return
        waits[tok.sem] = max(waits.get(tok.sem, 0), tok.val)
        kn[tok.sem] = tok.val
        for s, v in tok.implied.items():
            if kn.get(s, 0) < v:
                kn[s] = v

    def replay_rr(self, streams):
        idx = [0] * len(streams)
        left = sum(len(x) for x in streams)
        while left:
            for i, st in enumerate(streams):
                if idx[i] < len(st):
                    self.op(*st[idx[i]])
                    idx[i] += 1
                    left -= 1

    def op(self, eng, fn, reads=(), writes=(), dma=False):
        if self.rec is not None:
            self.rec.append((eng, fn, list(reads), list(writes), dma))
            return None
        reads = self._expand(reads)
        writes = self._expand(writes)
        waits = {}
        for k in reads:
            self._need(eng, self.last_write.get(k), waits)
            if k.startswith("pb"):
                for t in self.readers.get(k, ()):
                    if t.eng != eng:
                        self._need(eng, t, waits)
        for k in writes:
            self._need(eng, self.last_write.get(k), waits)
            for t in self.readers.get(k, ()):
                self._need(eng, t, waits)
        if dma:
            i = self.ndma[eng]
            self.ndma[eng] += 1
            key = (eng, i % NDMA)
            if key not in self.dsems:
                self.dsems[key] = self._newsem("d_%s_%d" % key)
                self.dcnt[key] = 0
            self._need(eng, self.dlast.get(key), waits)
            self.dcnt[key] += 1
            sem = self.dsems[key]
            val = 16 * self.dcnt[key]
            inc = 16
            implied = dict(self.known[eng])
        else:
            n = self.cnt[eng]
            self.cnt[eng] += 1
            sem = self._esem(eng, n // EPOCH)
            val = n % EPOCH + 1
            inc = 1
            implied = dict(self.known[eng])
            for ep in range(n // EPOCH):
                implied[self.esems[eng][ep]] = EPOCH
        tok = Tok(sem, val, implied, eng)
        if dma:
            self.dlast[key] = tok
        self.ops[eng].append((list(waits.items()), fn, sem, inc))
        for k in reads:
            self.readers.setdefault(k, []).append(tok)
        for k in writes:
            self.last_write[k] = tok
            self.readers[k] = []
        return tok

    def barrier(self):
        toks = []
        for e in self.ENGS:
            n = self.cnt[e]
            if n:
                imp = {self.esems[e][ep]: EPOCH for ep in range((n - 1) // EPOCH)}
                toks.append(Tok(self._esem(e, (n - 1) // EPOCH), (n - 1) % EPOCH + 1, imp, e))
        toks += list(self.dlast.values())
        for e in self.ENGS:
            waits = {}
            for t in toks:
                self._need(e, t, waits)
            if waits:
                self.ops[e].append((list(waits.items()), None, None, 0))
        self.last_write = {}
        self.readers = {}

    def emit(self):
        nc = self.nc
        engmap = {"pe": "tensor", "act": "scalar", "dve": "vector", "pool": "gpsimd", "sp": "sync"}
        with nc.Block() as block:
            for e in self.ENGS:
                ops = self.ops[e]
                if not ops:
                    continue

                def body(engobj, ops=ops):
                    for waits, fn, sem, inc in ops:
                        for s, v in waits:
                            engobj.wait_ge(s, v)
                        if fn is not None:
                            fn(engobj).then_inc(sem, inc)

                getattr(block, engmap[e])(body)


class Rot:
    def __init__(self, items):
        self.items = items
        self.i = 0

    def next(self):
        it = self.items[self.i % len(self.items)]
        self.i += 1
        return it


def build(S, debug=False, phases=("A1", "B", "A2", "Bp", "C")):
    NT = S // 512
    NO = NT // 2
    SO = S // 2
    NB = S // 128
    nc = bass.Bass("TRN2", target_bir_lowering=False)
    dr = lambda n, s, d, k="ExternalInput": nc.dram_tensor(n, s, d, kind=k)
    xT = dr("xT", [D, S], F32)
    pos = dr("pos", [1, S], I32)
    pT = dr("pT", [PLE, SO], F32)
    cst = dr("cst", [128, 96], F32)
    gainrow = dr("gainrow", [1, 512], F32)
    gbrow = dr("gbrow", [1, 8], F32)
    kflag = dr("kflag", [1, S], F32)
    validT = dr("validT", [128, NB], F32)
    cm = dr("cm", [128, 4 * 512 + 4 * 128], F32)
    w_in = dr("w_in", [D, DIN], F32)
    w_uq = dr("w_uq", [256, 768], F32)
    w_ukv = dr("w_ukv", [128, 1024], F32)
    w_out = dr("w_out", [D, D], F32)
    wgu = dr("wgu", [NFF, 128, DC, 256], F32)
    wdn = dr("wdn", [DC, 128, NFF, 128], F32)
    w_pp = dr("w_pp", [PLE, D], F32)
    w_pg = dr("w_pg", [D, D], F32)
    yT = dr("yT", [D, SO], F32, "ExternalOutput")
    h1d = dr("h1d", [D, SO], F32, "Internal")
    wgu_s = dr("wgu_s", [NFF, 128, DC * 256], BF16, "Internal")
    wdn_s = dr("wdn_s", [DC, 128, NFF * 128], BF16, "Internal")
    if debug:
        dbg_cat = dr("dbg_cat", [128, 8 * SO], F32, "ExternalOutput")
        dbg_h1 = dr("dbg_h1", [D, SO], F32, "ExternalOutput")

    G_APRE, G_APOST, G_FPRE, G_FPOST = 0, 8, 16, 24
    C_QN, C_KVN, C_CW, C_CB, C_INVF = 32, 34, 35, 67, 75

    with ExitStack() as top:
        sch = Sched(nc, top)
        OP = sch.op

        def sbuf(st, n, s, d):
            return st.enter_context(nc.sbuf_tensor(n, s, d))

        def mm(out, lhsT, rhs, start, stop, reads, writes):
            OP("pe", lambda e: e.matmul(out, lhsT, rhs, start=start, stop=stop), reads, writes)

        def act(out, in_, func, reads, writes, eng="act", **kw):
            OP(eng, lambda e: e.activation(out=out, in_=in_, func=func, **kw), reads, writes)

        def tt(out, in0, in1, op, reads, writes, eng="dve"):
            OP(eng, lambda e: e.tensor_tensor(out=out, in0=in0, in1=in1, op=op), reads, writes)

        def ts(out, in0, s1, s2, op0, op1, reads, writes, eng="dve"):
            if s2 is None:
                OP(eng, lambda e: e.tensor_scalar(out=out, in0=in0, scalar1=s1, scalar2=None, op0=op0), reads, writes)
            else:
                OP(eng, lambda e: e.tensor_scalar(out=out, in0=in0, scalar1=s1, scalar2=s2, op0=op0, op1=op1), reads, writes)

        def stt(out, in0, sc, in1, op0, op1, reads, writes, eng="dve"):
            OP(eng, lambda e: e.scalar_tensor_tensor(out=out, in0=in0, scalar=sc, in1=in1, op0=op0, op1=op1), reads, writes)

        def cp(out, in_, reads, writes, eng="dve"):
            if eng == "act":
                OP("act", lambda e: e.copy(out=out, in_=in_), reads, writes)
            else:
                OP(eng, lambda e: e.tensor_copy(out=out, in_=in_), reads, writes)

        def dma(eng, out, in_, reads, writes):
            OP(eng, lambda e: e.dma_start(out=out, in_=in_), reads, writes, dma=True)

        def dmac(eng, dst, src, nch, reads, writes):
            for w in writes:
                sch.alias.setdefault(w, ["%s#%d" % (w, c) for c in range(nch)])
            for c in range(nch):
                dma(eng, dst[:, c, :], src[c * 128:(c + 1) * 128, :], reads, ["%s#%d" % (w, c) for w in writes])

        def dmac_out(eng, dst, src, nch, reads, writes):
            for c in range(nch):
                dma(eng, dst[c * 128:(c + 1) * 128, :], src[:, c, :], reads, writes)

        PB = [top.enter_context(nc.psum_tensor("pb%d" % i, [128, 512], F32)) for i in range(8)]
        PK = ["pb%d" % i for i in range(8)]

        cs = sbuf(top, "cs", [128, 96], F32)
        identb = sbuf(top, "identb", [128, 128], BF16)
        identf = sbuf(top, "identf", [128, 128], F32)
        onesb = sbuf(top, "onesb", [128, 128], BF16)
        onesf = sbuf(top, "onesf", [128, 128], F32)
        trif = sbuf(top, "trif", [128, 128], F32)
        mposf = sbuf(top, "mposf", [128, 128], F32)
        catA = sbuf(top, "catA", [128, 4, SO], BF16)
        dma("sp", cs[:], cst[:, :], [], ["cs"])
        dma("sp", trif[:], cm[:, 2048:2176], [], ["trif"])
        dma("sp", mposf[:], cm[:, 2176:2304], [], ["mposf"])
        dma("sp", identf[:], cm[:, 2304:2432], [], ["identf"])
        dma("pool", identb[:], cm[:, 2304:2432], [], ["identb"])
        OP("dve", lambda e: e.memset(onesb[:], 1.0), [], ["onesb"])
        OP("dve", lambda e: e.memset(onesf[:], 1.0), [], ["onesf"])

        def rms_scale(ps_ap, n, out_ap, tmp_ap, rk, wk):
            act(tmp_ap, ps_ap, AF.Ln, rk, [wk + "_t"], bias=EPS, scale=1.0 / n)
            act(out_ap, tmp_ap, AF.Exp, [wk + "_t"], [wk], scale=-0.5)

        with ExitStack() as pa:
            wA = sbuf(pa, "wA", [128, DC, 448], BF16)
            wArot = sbuf(pa, "wArot", [128, DC, 64], BF16)
            wuq = sbuf(pa, "wuq", [128, 2, 768], BF16)
            wuqrot = sbuf(pa, "wuqrot", [128, 2, 4, 64], BF16)
            wukv = sbuf(pa, "wukv", [128, 1024], BF16)
            cqn = sbuf(pa, "cqn", [128, 2, SO], BF16)
            ckvn = sbuf(pa, "ckvn", [128, S], BF16)
            kpe = sbuf(pa, "kpe", [65, S], BF16)
            qpe = sbuf(pa, "qpe", [65, 4, SO], BF16)
            m4 = sbuf(pa, "m4", [128, 4, 512], BF16)
            dma("pool", m4[:].rearrange("p a q -> p (a q)"), cm[:, 0:2048], [], ["m4"])
            stg = [sbuf(pa, "stg%d" % i, [128, 448], F32) for i in range(2)]
            sch.alias["wA"] = ["wA#%d" % c for c in range(DC)]
            for c in range(DC):
                dma("sp", stg[c % 2][:], w_in[c * 128:(c + 1) * 128, 0:448], [], ["stg%d" % (c % 2)])
                cp(wA[:, c, :], stg[c % 2][:], ["stg%d" % (c % 2)], ["wA#%d" % c], eng=("act" if c % 2 else "dve"))
            dmac("pool", wuq, w_uq[:, :], 2, [], ["wuq"])
            dma("pool", wukv[:], w_ukv[:, :], [], ["wukv"])
            dma("pool", kpe[64:65, :], kflag[:, :], [], ["kpe_flag"])
            OP("dve", lambda e: e.memset(qpe[64:65, :, :], 1.0), [], ["qpe_one"])
            OP("act", lambda e: e.mul(out=wArot[:, :, 0:32], in_=wA[:, :, 416:448], mul=-1.0), ["wA"], ["wArot_a"])
            cp(wArot[:, :, 32:64], wA[:, :, 384:416], ["wA"], ["wArot_b"])
            for h in range(4):
                b0 = h * 192 + 128
                OP("act", lambda e, h=h, b0=b0: e.mul(out=wuqrot[:, :, h, 0:32], in_=wuq[:, :, b0 + 32:b0 + 64], mul=-1.0),
                   ["wuq"], ["wuqrot_a%d" % h])
                cp(wuqrot[:, :, h, 32:64], wuq[:, :, b0:b0 + 32], ["wuq"], ["wuqrot_b%d" % h])
            WROT = ["wArot_a", "wArot_b"]
            WQROT = ["wuqrot_a%d" % h for h in range(4)] + ["wuqrot_b%d" % h for h in range(4)]

            with ExitStack() as p1:
                xt = sbuf(p1, "xt", [128, DC, 512], F32)
                u = sbuf(p1, "u", [128, DC, 512], BF16)
                sq = [sbuf(p1, "sq%d" % i, [128, 512], BF16) for i in range(3)]
                sqr = Rot([(sq[i], "sq%d" % i) for i in range(3)])
                lnt = sbuf(p1, "lnt", [128, 512], F32)
                rbc = sbuf(p1, "rbc", [128, 512], F32)
                r2 = sbuf(p1, "r2", [128, 512], F32)
                raw = sbuf(p1, "raw", [128, 2, 512], F32)
                pi = sbuf(p1, "pi", [64, 512], I32)
                tr = [sbuf(p1, "tr%d" % i, [64, 512], F32) for i in range(4)]
                tri_i = sbuf(p1, "tri_i", [64, 512], I32)
                sinT = sbuf(p1, "sinT", [64, 512], F32)
                cosT = sbuf(p1, "cosT", [64, 512], F32)
                t1 = sbuf(p1, "t1", [64, 512], F32)
                t2 = sbuf(p1, "t2", [64, 512], F32)
                psr = Rot([(PB[i], PK[i]) for i in range(8)])

                for v in (range(NT) if "A1" in phases else []):
                    own = (v % 2 == 1)
                    j = v // 2
                    tsl = slice(v * 512, (v + 1) * 512)
                    osl = slice(j * 512, (j + 1) * 512)
                    dmac("sp", xt, xT[:, tsl], DC, [], ["xt"])
                    dma("sp", pi[:], pos[0:1, tsl].partition_broadcast(64), [], ["pi"])
                    pss, pssk = psr.next()
                    for c in range(DC):
                        sqt, sqk = sqr.next()
                        act(sqt[:], xt[:, c, :], AF.Square, ["xt#%d" % c], [sqk])
                        mm(pss[:], onesb[:], sqt[:], c == 0, c == DC - 1, ["onesb", sqk], [pssk])
                    rms_scale(pss[:], D, rbc[:], lnt[:], [pssk], "rbc")
                    for c in range(DC):
                        stt(u[:, c, :], xt[:, c, :], cs[:, G_APRE + c:G_APRE + c + 1], rbc[:], ALU.mult, ALU.mult,
                            ["xt#%d" % c, "cs", "rbc"], ["u%d" % c])
                    UK = ["u%d" % c for c in range(DC)]
                    LVL = int(os.environ.get("A1LVL", "9"))
                    if LVL < 2:
                        continue
                    cp(tr[0][:], pi[:], ["pi"], ["tr0"])
                    ts(tr[1][:], tr[0][:], cs[0:64, C_INVF:C_INVF + 1], 0.0, ALU.mult, ALU.add, ["tr0", "cs"], ["tr1"])
                    cp(tri_i[:], tr[1][:], ["tr1"], ["tri_i"])
                    cp(tr[2][:], tri_i[:], ["tri_i"], ["tr2"])
                    tt(tr[3][:], tr[1][:], tr[2][:], ALU.subtract, ["tr1", "tr2"], ["tr3"])
                    act(sinT[:], tr[3][:], AF.Sin, ["tr3"], ["sinT"], scale=TWO_PI)
                    ts(tr[1][:], tr[1][:], 0.25, None, ALU.add, None, ["tr1"], ["tr1"])
                    cp(tri_i[:], tr[1][:], ["tr1"], ["tri_i"])
                    cp(tr[2][:], tri_i[:], ["tri_i"], ["tr2"])
                    tt(tr[3][:], tr[1][:], tr[2][:], ALU.subtract, ["tr1", "tr2"], ["tr3"])
                    act(cosT[:], tr[3][:], AF.Sin, ["tr3"], ["cosT"], scale=TWO_PI)
                    if LVL < 3:
                        continue
                    pkv, pkvk = psr.next()
                    for c in range(DC):
                        mm(pkv[:], wA[:, c, 256:384], u[:, c, :], c == 0, c == DC - 1, ["wA", "u%d" % c], [pkvk])
                    sqt, sqk = sqr.next()
                    act(sqt[:], pkv[:], AF.Square, [pkvk], [sqk])
                    cp(raw[:, 0, :], pkv[:], [pkvk], ["raw0"], eng="act")
                    ps2, ps2k = psr.next()
                    mm(ps2[:], onesb[:], sqt[:], True, True, ["onesb", sqk], [ps2k])
                    rms_scale(ps2[:], 128, r2[:], lnt[:], [ps2k], "r2")
                    stt(ckvn[:, tsl], raw[:, 0, :], cs[:, C_KVN:C_KVN + 1], r2[:], ALU.mult, ALU.mult,
                        ["raw0", "cs", "r2"], ["ckvn%d" % v])
                    if LVL < 4:
                        continue
                    pkr, pkrk = psr.next()
                    pkq, pkqk = psr.next()
                    for c in range(DC):
                        mm(pkr[0:64, :], wA[:, c, 384:448], u[:, c, :], c == 0, c == DC - 1, ["wA", "u%d" % c], [pkrk])
                    for c in range(DC):
                        mm(pkq[0:64, :], wArot[:, c, :], u[:, c, :], c == 0, c == DC - 1, WROT + ["u%d" % c], [pkqk])
                    tt(t1[:], pkr[0:64, :], cosT[:], ALU.mult, [pkrk, "cosT"], ["t1"])
                    tt(t2[:], pkq[0:64, :], sinT[:], ALU.mult, [pkqk, "sinT"], ["t2"])
                    tt(kpe[0:64, tsl], t1[:], t2[:], ALU.add, ["t1", "t2"], ["kpe%d" % v])
                    if not own or LVL < 5:
                        continue
                    pq = [psr.next(), psr.next()]
                    for l in range(2):
                        for c in range(DC):
                            mm(pq[l][0][:], wA[:, c, l * 128:(l + 1) * 128], u[:, c, :], c == 0, c == DC - 1,
                               ["wA", "u%d" % c], [pq[l][1]])
                    ps2, ps2k = psr.next()
                    for l in range(2):
                        sqt, sqk = sqr.next()
                        act(sqt[:], pq[l][0][:], AF.Square, [pq[l][1]], [sqk])
                        cp(raw[:, l, :], pq[l][0][:], [pq[l][1]], ["raw%d" % l], eng="act")
                        mm(ps2[:], onesb[:], sqt[:], l == 0, l == 1, ["onesb", sqk], [ps2k])
                    rms_scale(ps2[:], 256, r2[:], lnt[:], [ps2k], "r2")
                    for l in range(2):
                        stt(cqn[:, l, osl], raw[:, l, :], cs[:, C_QN + l:C_QN + l + 1], r2[:], ALU.mult, ALU.mult,
                            ["raw%d" % l, "cs", "r2"], ["cqn%d_%d" % (j, l)])
                    for h in (range(4) if LVL >= 6 else []):
                        pqr, pqrk = psr.next()
                        pqq, pqqk = psr.next()
                        b0 = h * 192 + 128
                        for l in range(2):
                            mm(pqr[0:64, :], wuq[:, l, b0:b0 + 64], cqn[:, l, osl], l == 0, l == 1,
                               ["wuq", "cqn%d_%d" % (j, l)], [pqrk])
                        for l in range(2):
                            mm(pqq[0:64, :], wuqrot[:, l, h, :], cqn[:, l, osl], l == 0, l == 1,
                               WQROT + ["cqn%d_%d" % (j, l)], [pqqk])
                        tt(t1[:], pqr[0:64, :], cosT[:], ALU.mult, [pqrk, "cosT"], ["t1"])
                        tt(t2[:], pqq[0:64, :], sinT[:], ALU.mult, [pqqk, "sinT"], ["t2"])
                        tt(t1[:], t1[:], t2[:], ALU.add, ["t1", "t2"], ["t1"])
                        ts(qpe[0:64, h, osl], t1[:], QSCALE, None, ALU.mult, None, ["t1"], ["qpe%d_%d" % (j, h)])
            sch.barrier()

            for n in range(NFF):
                dma("pool", wgu_s[n, :, :], wgu[n, :, :, :].rearrange("p c f -> p (c f)"), [], ["wgus%d" % n])
            for ct in range(DC):
                dma("pool", wdn_s[ct, :, :], wdn[ct, :, :, :].rearrange("p n f -> p (n f)"), [], ["wdns%d" % ct])
            with ExitStack() as p2:
                Kh = sbuf(p2, "Kh", [128, S], BF16)
                Vh = sbuf(p2, "Vh", [128, NB, 128], BF16)
                Qn = sbuf(p2, "Qn", [128, SO], BF16)
                ptl = [sbuf(p2, "pt%d" % i, [128, 512], BF16) for i in range(5)]
                rden = sbuf(p2, "rden", [128, 512], F32)
                dsum = sbuf(p2, "dsum", [128, 512], F32)
                for h in (range(4) if "B" in phases else []):
                    psr = Rot([(PB[i], PK[i]) for i in range(4)])
                    ci = 0
                    for v in range(NT):
                        tsl = slice(v * 512, (v + 1) * 512)
                        ps, pk = psr.next()
                        mm(ps[:], wukv[:, h * 256:h * 256 + 128], ckvn[:, tsl], True, True, ["wukv"], [pk])
                        cp(Kh[:, tsl], ps[:], [pk], ["Kh%d" % v], eng=("act" if ci % 2 else "dve"))
                        ci += 1
                        ps, pk = psr.next()
                        for bl in range(4):
                            mm(ps[:, bl * 128:(bl + 1) * 128], ckvn[:, v * 512 + bl * 128:v * 512 + (bl + 1) * 128],
                               wukv[:, h * 256 + 128:h * 256 + 256], True, True, ["wukv"], [pk])
                        cp(Vh[:, v * 4:(v + 1) * 4, :].rearrange("p a d -> p (a d)"), ps[:], [pk], ["Vh%d" % v],
                           eng=("act" if ci % 2 else "dve"))
                        ci += 1
                    for j in range(NO):
                        osl = slice(j * 512, (j + 1) * 512)
                        ps, pk = psr.next()
                        for l in range(2):
                            mm(ps[:], wuq[:, l, h * 192:h * 192 + 128], cqn[:, l, osl], l == 0, l == 1, ["wuq"], [pk])
                        OP("act", lambda e, ps=ps, osl=osl: e.mul(out=Qn[:, osl], in_=ps[:], mul=QSCALE), [pk], ["Qn%d" % j])
                    pss = Rot([(PB[i], PK[i]) for i in range(5)])
                    pacc = Rot([((PB[5], PK[5]), (PB[7], PK[7])), ((PB[6], PK[6]), (PB[7], PK[7]))])
                    ptr = Rot([(ptl[i], "pt%d" % i) for i in range(5)])
                    for j in range(NO):
                        osl = slice(j * 512, (j + 1) * 512)
                        nfull = 4 * (2 * j + 1)
                        nkb = nfull + 4
                        (oacc, oak), (dacc, dak) = pacc.next()
                        pend = []

                        def issue_qk(kb):
                            st, stk = pss.next()
                            ksl = slice(kb * 128, (kb + 1) * 128)
                            mm(st[:], Kh[:, ksl], Qn[:, osl], True, False, ["Kh%d" % (kb // 4), "Qn%d" % j], [stk])
                            if kb >= nfull:
                                mm(st[:], identb[:], m4[:, kb - nfull, :], False, False, ["identb", "m4"], [stk])
                            mm(st[:], kpe[0:65, ksl], qpe[0:65, h, osl], False, True, [], [stk])
                            pt, ptk = ptr.next()
                            act(pt[:], st[:], AF.Exp, [stk], [ptk])
                            pend.append((kb, pt, ptk))

                        def issue_pv():
                            kb, pt, ptk = pend.pop(0)
                            mm(oacc[:], Vh[:, kb, :], pt[:], kb == 0, kb == nkb - 1, ["Vh%d" % (kb // 4), ptk], [oak])
                            if kb == 0:
                                cp(dsum[:], pt[:], [ptk], ["dsum"])
                            else:
                                tt(dsum[:], dsum[:], pt[:], ALU.add, ["dsum", ptk], ["dsum"])

                        for kb in range(nkb):
                            issue_qk(kb)
                            if len(pend) > 3:
                                issue_pv()
                        while pend:
                            issue_pv()
                        mm(dacc[:], onesf[:], dsum[:], True, True, ["onesf", "dsum"], [dak])
                        OP("dve", lambda e, dacc=dacc: e.reciprocal(out=rden[:], in_=dacc[:]), [dak], ["rden"])
                        tt(catA[:, h, osl], oacc[:], rden[:], ALU.mult, [oak, "rden"], ["catA%d_%d" % (h, j)])
            sch.barrier()

        with ExitStack() as pm:
            catB = sbuf(pm, "catB", [128, 4, SO], BF16)
            with ExitStack() as p3:
                wM = sbuf(p3, "wM", [128, DC, 2056], BF16)
                dmac("pool", wM, w_in[:, 448:2504], DC, [], ["wM"])
                gainbc = sbuf(p3, "gainbc", [128, 512], F32)
                gbbc = sbuf(p3, "gbbc", [128, 8], F32)
                vld = sbuf(p3, "vld", [128, NB], F32)
                vldb = sbuf(p3, "vldb", [128, NB], BF16)
                dma("sp", gainbc[:], gainrow[0:1, :].partition_broadcast(128), [], ["gainbc"])
                dma("sp", gbbc[:], gbrow[0:1, :].partition_broadcast(128), [], ["gbbc"])
                dma("sp", vld[:], validT[:, :], [], ["vld"])
                cp(vldb[:], vld[:], ["vld"], ["vldb"])
                xt = sbuf(p3, "xt2", [128, DC, 512], F32)
                u = sbuf(p3, "u2", [128, DC, 512], BF16)
                sq = [sbuf(p3, "sq2_%d" % i, [128, 512], BF16) for i in range(3)]
                sqr = Rot([(sq[i], "sq%d" % i) for i in range(3)])
                lnt = sbuf(p3, "lnt2", [128, 512], F32)
                rbc = sbuf(p3, "rbc2", [128, 512], F32)
                pc = sbuf(p3, "pc", [128, 8, 515], F32)
                cacc = [sbuf(p3, "cacc%d" % i, [128, 512], F32) for i in range(2)]
                caccr = Rot([(cacc[i], "cacc%d" % i) for i in range(2)])
                ctmp = sbuf(p3, "ctmp", [128, 512], F32)
                qkT = sbuf(p3, "qkT", [128, 8, 512], BF16)
                vtm = sbuf(p3, "vtm", [128, 4, 512], BF16)
                og = sbuf(p3, "og", [128, 4, 512], BF16)
                gsbs = [sbuf(p3, "gsb%d" % i, [128, 8], F32) for i in range(4)]
                lfns = [sbuf(p3, "lfn%d" % i, [128, 4], F32) for i in range(4)]
                sms = [sbuf(p3, "sm%d" % i, [128, 64], F32) for i in range(4)]
                trilfs = [sbuf(p3, "trilf%d" % i, [128, 4, 128], F32) for i in range(4)]
                dTs = [sbuf(p3, "dT%d" % i, [128, 4, 128], BF16) for i in range(4)]
                sTs = [sbuf(p3, "sT%d" % i, [128, 4, 128], BF16) for i in range(4)]
                tmpn = sbuf(p3, "tmpn", [128, 4, 128], F32)
                num = sbuf(p3, "num", [128, 4, 128], F32)
                sqj = sbuf(p3, "sqj", [128, 128], F32)
                GO = sbuf(p3, "GO", [128, 512], F32)
                mouts = [sbuf(p3, "mout%d" % i, [128, 512], BF16) for i in range(2)]
                kws = [sbuf(p3, "kw%d" % i, [128, 4, 128], BF16) for i in range(4)]
                Cst = sbuf(p3, "Cst", [128, 4, 128], F32)
                nst = sbuf(p3, "nst", [128, 4], F32)
                Cbf = sbuf(p3, "Cbf", [128, 4, 128], BF16)
                nbf = sbuf(p3, "nbf", [128, 4], BF16)
                OP("dve", lambda e: e.memset(pc[:], 0.0), [], ["pc"])
                OP("dve", lambda e: e.memset(Cst[:], 0.0), [], ["Cst"])
                OP("dve", lambda e: e.memset(nst[:], 0.0), [], ["nst"])
                OP("dve", lambda e: e.memset(Cbf[:], 0.0), [], ["Cbf"])
                OP("dve", lambda e: e.memset(nbf[:], 0.0), [], ["nbf"])
                psr = Rot([(PB[i], PK[i]) for i in range(7)])
                MQ, MK, MV, MO, MG = 0, 512, 1024, 1536, 2048
                pst = Rot([(PB[i], PK[i]) for i in (4, 5, 6)])

                for v in (range(NT) if "A2" in phases else []):
                    own = (v % 2 == 1)
                    j = v // 2
                    tsl = slice(v * 512, (v + 1) * 512)
                    dmac("sp", xt, xT[:, tsl], DC, [], ["xt"])
                    pss, pssk = psr.next()
                    for c in range(DC):
                        sqt, sqk = sqr.next()
                        act(sqt[:], xt[:, c, :], AF.Square, ["xt#%d" % c], [sqk])
                        mm(pss[:], onesb[:], sqt[:], c == 0, c == DC - 1, ["onesb", sqk], [pssk])
                    rms_scale(pss[:], D, rbc[:], lnt[:], [pssk], "rbc")
                    for c in range(DC):
                        stt(u[:, c, :], xt[:, c, :], cs[:, G_APRE + c:G_APRE + c + 1], rbc[:], ALU.mult, ALU.mult,
                            ["xt#%d" % c, "cs", "rbc"], ["u%d" % c])
                    UK = ["u%d" % c for c in range(DC)]
                    cp(pc[:, :, 0:3], pc[:, :, 512:515], ["pc"] + ["pc%d" % ct for ct in range(8)], ["pc"])
                    for ct in range(8):
                        ps, pk = psr.next()
                        for c in range(DC):
                            mm(ps[:], wM[:, c, ct * 128:(ct + 1) * 128], u[:, c, :], c == 0, c == DC - 1,
                               ["wM", "u%d" % c], [pk])
                        cp(pc[:, ct, 3:515], ps[:], [pk, "pc"], ["pc%d" % ct], eng="act")
                        if ct < 4 and not own:
                            continue
                        ca, cak = caccr.next()
                        w0 = C_CW + ct * 4
                        ceng = "dve"
                        ts(ca[:], pc[:, ct, 0:512], cs[:, w0:w0 + 1], cs[:, C_CB + ct:C_CB + ct + 1], ALU.mult, ALU.add,
                           ["pc", "pc%d" % ct, "cs"], [cak], eng=ceng)
                        for tap in range(1, 4):
                            if ceng == "pool":
                                ts(ctmp[:], pc[:, ct, tap:tap + 512], cs[:, w0 + tap:w0 + tap + 1], None, ALU.mult, None,
                                   ["pc", "pc%d" % ct, "cs"], ["ctmp"], eng="pool")
                                tt(ca[:], ca[:], ctmp[:], ALU.add, [cak, "ctmp"], [cak], eng="pool")
                            else:
                                stt(ca[:], pc[:, ct, tap:tap + 512], cs[:, w0 + tap:w0 + tap + 1], ca[:], ALU.mult, ALU.add,
                                    ["pc", "pc%d" % ct, "cs", cak], [cak])
                        act(qkT[:, ct, :], ca[:], AF.Silu, [cak], ["qkT%d" % ct])
                    gps, gpk = PB[7], PK[7]
                    for bl in range(4):
                        bs = slice(bl * 128, (bl + 1) * 128)
                        ps, pk = psr.next()
                        for c in range(DC):
                            mm(ps[:], u[:, c, bs], wM[:, c, MV:MV + 512], c == 0, c == DC - 1, ["wM", "u%d" % c], [pk])
                        cp(vtm[:, bl, :], ps[:], [pk], ["vtm%d" % bl], eng="act")
                        if own:
                            ps, pk = psr.next()
                            for c in range(DC):
                                mm(ps[:], u[:, c, bs], wM[:, c, MO:MO + 512], c == 0, c == DC - 1, ["wM", "u%d" % c], [pk])
                            act(og[:, bl, :], ps[:], AF.Sigmoid, [pk], ["og%d" % bl])
                        for c in range(DC):
                            mm(gps[:, bl * 8:(bl + 1) * 8], u[:, c, bs], wM[:, c, MG:MG + 8], c == 0, c == DC - 1,
                               ["wM", "u%d" % c], [gpk])
                    def blk_pre(bl):
                        X = "_%d" % bl
                        gsb, lfn, sm, sT, kw = gsbs[bl], lfns[bl], sms[bl], sTs[bl], kws[bl]
                        trilf, trk = trilfs[bl], "trilf%d" % bl
                        dT, dtk = dTs[bl], "dT%d" % bl
                        bs = slice(bl * 128, (bl + 1) * 128)
                        tt(gsb[:], gps[:, bl * 8:(bl + 1) * 8], gbbc[:], ALU.add, [gpk, "gbbc"], ["gsb" + X])
                        act(sm[:, 0:4], gsb[:, 4:8], AF.Exp, ["gsb" + X], ["sm_e" + X], scale=-1.0)
                        act(lfn[:], sm[:, 0:4], AF.Ln, ["sm_e" + X], ["lfn" + X], bias=1.0)
                        cps, cpk = PB[bl], PK[bl]
                        mm(cps[:, 0:4], trif[:], lfn[:], True, True, ["trif", "lfn" + X], [cpk])
                        mm(cps[:, 4:8], onesf[:], lfn[:], True, True, ["onesf", "lfn" + X], [cpk])
                        ts(sm[:, 4:8], gsb[:, 0:4], LNK, None, ALU.add, None, ["gsb" + X], ["sm_bd" + X])
                        tt(sm[:, 4:8], sm[:, 4:8], cps[:, 0:4], ALU.add, ["sm_bd" + X, cpk], ["sm_bd" + X])
                        tt(sm[:, 8:12], sm[:, 4:8], cps[:, 4:8], ALU.subtract, ["sm_bd" + X, cpk], ["sm_wl" + X])
                        act(sm[:, 12:16], sm[:, 8:12], AF.Exp, ["sm_wl" + X], ["sm_w" + X])
                        act(sm[:, 16:20], cps[:, 4:8], AF.Exp, [cpk], ["sm_eg" + X], scale=-1.0)
                        act(sm[:, 20:24], cps[:, 0:4], AF.Exp, [cpk], ["sm_eb" + X], scale=-1.0)
                        if own:
                            tt(trilf[:], trif[:].unsqueeze(1).to_broadcast([128, 4, 128]),
                               lfn[:].unsqueeze(2).to_broadcast([128, 4, 128]), ALU.mult, ["trif", "lfn" + X], [trk])
                            bps, bpk = PB[bl], PK[bl]
                            for h in range(4):
                                mm(bps[:, h * 128:(h + 1) * 128], onesf[:], trilf[:, h, :], True, False, ["onesf", trk], [bpk])
                                mm(bps[:, h * 128:(h + 1) * 128], identf[:], mposf[:], False, True, ["identf", "mposf"], [bpk])
                            for h in range(4):
                                act(dT[:, h, :], bps[:, h * 128:(h + 1) * 128], AF.Exp, [bpk, "sm_bd" + X], [dtk + "h%d" % h],
                                    bias=sm[:, 4 + h:5 + h], scale=-1.0)
                            kps, kpk = PB[bl], PK[bl]
                            for h in range(4):
                                mm(kps[:, h * 128:(h + 1) * 128], qkT[:, 4 + h, bs], qkT[:, h, bs], True, True,
                                   ["qkT%d" % (4 + h), "qkT%d" % h], [kpk])
                            tt(sT[:].rearrange("p h j -> p (h j)"), kps[:], dT[:].rearrange("p h j -> p (h j)"), ALU.mult,
                               [kpk] + [dtk + "h%d" % h for h in range(4)], ["sT" + X])
                        tps, tpk = PB[bl], PK[bl]
                        for h in range(4):
                            mm(tps[:, h * 128:(h + 1) * 128], qkT[:, 4 + h, bs], identb[:], True, True,
                               ["qkT%d" % (4 + h), "identb"], [tpk])
                        tt(kw[:], tps[:].rearrange("p (h d) -> p h d", h=4),
                           sm[:, 12:16].unsqueeze(2).to_broadcast([128, 4, 128]), ALU.mult, [tpk, "sm_w" + X], ["kw" + X])

                    def blk_tail(bl):
                        X = "_%d" % bl
                        sm, sT, kw = sms[bl], sTs[bl], kws[bl]
                        gb = v * 4 + bl
                        bs = slice(bl * 128, (bl + 1) * 128)
                        tok0 = j * 512 + bl * 128
                        if own:
                            ips, ipk = pst.next()
                            eps_, epk = pst.next()
                            for h in range(4):
                                hs = slice(h * 128, (h + 1) * 128)
                                mm(ips[:, hs], sT[:, h, :], vtm[:, bl, hs], True, True, ["sT" + X, "vtm%d" % bl], [ipk])
                                mm(PB[7][:, 32 + h:33 + h], sT[:, h, :], vldb[:, gb:gb + 1], True, True, ["sT" + X, "vldb"], [PK[7]])
                                mm(eps_[:, hs], qkT[:, h, bs], Cbf[:, h, :], True, True, ["qkT%d" % h, "Cbf"], [epk])
                                mm(PB[7][:, 36 + h:37 + h], qkT[:, h, bs], nbf[:, h:h + 1], True, True, ["qkT%d" % h, "nbf"], [PK[7]])
                        ups, upk = pst.next()
                        for h in range(4):
                            hs = slice(h * 128, (h + 1) * 128)
                            mm(ups[:, hs], kw[:, h, :], vtm[:, bl, hs], True, True, ["kw" + X, "vtm%d" % bl], [upk])
                            mm(PB[7][:, 40 + h:41 + h], kw[:, h, :], vldb[:, gb:gb + 1], True, True, ["kw" + X, "vldb"], [PK[7]])
                        tt(Cst[:], Cst[:], sm[:, 16:20].unsqueeze(2).to_broadcast([128, 4, 128]), ALU.mult, ["Cst", "sm_eg" + X], ["Cst"])
                        tt(Cst[:].rearrange("p h d -> p (h d)"), Cst[:].rearrange("p h d -> p (h d)"), ups[:], ALU.add,
                           ["Cst", upk], ["Cst"])
                        tt(nst[:], nst[:], sm[:, 16:20], ALU.mult, ["nst", "sm_eg" + X], ["nst"])
                        tt(nst[:], nst[:], PB[7][:, 40:44], ALU.add, ["nst", PK[7]], ["nst"])
                        cp(Cbf[:], Cst[:], ["Cst"], ["Cbf"], eng="act")
                        cp(nbf[:], nst[:], ["nst"], ["nbf"])
                        if not own:
                            return
                        tt(sm[:, 24:28], PB[7][:, 36:40], sm[:, 20:24], ALU.mult, [PK[7], "sm_eb" + X], ["sm_d1" + X])
                        tt(sm[:, 28:32], sm[:, 24:28], PB[7][:, 32:36], ALU.add, ["sm_d1" + X, PK[7]], ["sm_den" + X])
                        act(sm[:, 32:36], sm[:, 28:32], AF.Abs, ["sm_den" + X], ["sm_abs" + X])
                        ts(sm[:, 32:36], sm[:, 32:36], 1.0, None, ALU.max, None, ["sm_abs" + X], ["sm_abs" + X])
                        OP("dve", lambda e: e.reciprocal(out=sm[:, 36:40], in_=sm[:, 32:36]), ["sm_abs" + X], ["sm_rden" + X])
                        tt(tmpn[:], eps_[:].rearrange("p (h d) -> p h d", h=4),
                           sm[:, 20:24].unsqueeze(2).to_broadcast([128, 4, 128]), ALU.mult, [epk, "sm_eb" + X], ["tmpn"])
                        tt(num[:].rearrange("p h d -> p (h d)"), ips[:], tmpn[:].rearrange("p h d -> p (h d)"), ALU.add,
                           [ipk, "tmpn"], ["num"])
                        OP("dve", lambda e: e.memset(sm[:, 40:44], 0.0), [], ["sm_ss%d" % h + X for h in range(4)])
                        for h in range(4):
                            act(sqj[:], num[:, h, :], AF.Square, ["num"], ["sqj", "sm_ss%d" % h + X], accum_out=sm[:, 40 + h:41 + h])
                        tt(sm[:, 44:48], sm[:, 36:40], sm[:, 36:40], ALU.mult, ["sm_rden" + X], ["sm_r2" + X])
                        tt(sm[:, 44:48], sm[:, 44:48], sm[:, 40:44], ALU.mult, ["sm_r2" + X] + ["sm_ss%d" % h + X for h in range(4)], ["sm_r2" + X])
                        act(sm[:, 48:52], sm[:, 44:48], AF.Ln, ["sm_r2" + X], ["sm_ln" + X], bias=EPS, scale=1.0 / 128)
                        act(sm[:, 52:56], sm[:, 48:52], AF.Exp, ["sm_ln" + X], ["sm_rs" + X], scale=-0.5)
                        tt(sm[:, 56:60], sm[:, 52:56], sm[:, 36:40], ALU.mult, ["sm_rs" + X, "sm_rden" + X], ["sm_f" + X])
                        tt(GO[:], gainbc[:], og[:, bl, :], ALU.mult, ["gainbc", "og%d" % bl], ["GO"], eng=("dve" if os.environ.get("NOPOOL") else "pool"))
                        tt(tmpn[:], num[:], sm[:, 56:60].unsqueeze(2).to_broadcast([128, 4, 128]), ALU.mult,
                           ["num", "sm_f" + X], ["tmpn"])
                        mout = mouts[bl % 2]
                        tt(mout[:], tmpn[:].rearrange("p h d -> p (h d)"), GO[:], ALU.mult, ["tmpn", "GO"], ["mout%d" % (bl % 2)])

                    def blk_tail_b(bl):
                        if not own:
                            return
                        mout = mouts[bl % 2]
                        tok0 = j * 512 + bl * 128
                        tps, tpk = PB[bl], PK[bl]
                        for h in range(4):
                            mm(tps[:, h * 128:(h + 1) * 128], mout[:, h * 128:(h + 1) * 128], identb[:], True, True,
                               ["mout%d" % (bl % 2), "identb"], [tpk])
                        cp(catB[:, :, tok0:tok0 + 128], tps[:].rearrange("p (h t) -> p h t", h=4), [tpk],
                           ["catB%d_%d" % (j, bl)], eng="act")

                    streams = []
                    for bl in range(4):
                        sch.rec = []
                        blk_pre(bl)
                        streams.append(sch.rec)
                        sch.rec = None
                    sch.replay_rr(streams)
                    for bl in range(4):
                        blk_tail(bl)
                        if bl > 0:
                            blk_tail_b(bl - 1)
                    blk_tail_b(3)
            sch.barrier()

            with ExitStack() as p4:
                wo = sbuf(p4, "wo", [128, DC, D], BF16)
                dmac("pool", wo, w_out[:, :], DC, [], ["wo"])
                xts4 = [sbuf(p4, "xt4_%d" % i, [128, DC, 512], F32) for i in range(2)]
                mixs = [sbuf(p4, "mix%d" % i, [128, DC, 512], F32) for i in range(2)]
                sq = [sbuf(p4, "sq4_%d" % i, [128, 512], BF16) for i in range(3)]
                sqr = Rot([(sq[i], "sq%d" % i) for i in range(3)])
                lnt = sbuf(p4, "lnt4", [128, 512], F32)
                rbc = sbuf(p4, "rbc4", [128, 512], F32)
                tm = [sbuf(p4, "tm4_%d" % i, [128, 512], F32) for i in range(2)]
                tmr = Rot([(tm[i], "tm%d" % i) for i in range(2)])
                h1s4 = [sbuf(p4, "h1_4%d" % i, [128, DC, 512], F32) for i in range(2)]
                psr = Rot([(PB[i], PK[i]) for i in range(6)])
                for j in (range(NO) if "Bp" in phases else []):
                    v = 2 * j + 1
                    tsl = slice(v * 512, (v + 1) * 512)
                    osl = slice(j * 512, (j + 1) * 512)
                    xt = xts4[j % 2]
                    xk = "xq%d" % (j % 2)
                    mix = mixs[j % 2]
                    h1 = h1s4[j % 2]
                    mk = "mix%d_" % (j % 2)
                    hq = "h1q%d_" % (j % 2)
                    dmac("sp", xt, xT[:, tsl], DC, [], [xk])
                    pss, pssk = PB[7], PK[7]
                    for ct in range(DC):
                        ps, pk = psr.next()
                        for f in range(8):
                            rhs = catA[:, f, osl] if f < 4 else catB[:, f - 4, osl]
                            mm(ps[:], wo[:, f, ct * 128:(ct + 1) * 128], rhs, f == 0, f == 7, ["wo"], [pk])
                        sqt, sqk = sqr.next()
                        act(sqt[:], ps[:], AF.Square, [pk], [sqk])
                        cp(mix[:, ct, :], ps[:], [pk], [mk + str(ct)], eng="act")
                        mm(pss[:], onesb[:], sqt[:], ct == 0, ct == DC - 1, ["onesb", sqk], [pssk])
                    rms_scale(pss[:], D, rbc[:], lnt[:], [pssk], "rbc")
                    for ct in range(DC):
                        t_, tk = tmr.next()
                        tt(t_[:], mix[:, ct, :], rbc[:], ALU.mult, [mk + str(ct), "rbc"], [tk])
                        stt(h1[:, ct, :], t_[:], cs[:, G_APOST + ct:G_APOST + ct + 1], xt[:, ct, :], ALU.mult, ALU.add,
                            [tk, "cs", xk], [hq + str(ct)])
                        dma("sp", h1d[ct * 128:(ct + 1) * 128, osl], h1[:, ct, :], [hq + str(ct)], ["h1d%d_%d" % (j, ct)])
                    if debug:
                        dmac_out("sp", dbg_h1[:, osl], h1, DC, [hq + str(c) for c in range(DC)], [])
                if debug:
                    for hh in range(4):
                        dma("pool", dbg_cat[:, hh * SO:(hh + 1) * SO], catA[:, hh, :], [], [])
                        dma("pool", dbg_cat[:, (4 + hh) * SO:(5 + hh) * SO], catB[:, hh, :], [], [])
            sch.barrier()

        with ExitStack() as p5:
            wpg = sbuf(p5, "wpg", [128, DC, D], BF16)
            wpp = sbuf(p5, "wpp", [128, 2, D], BF16)
            dmac("pool", wpg, w_pg[:, :], DC, [], ["wpg"])
            dmac("pool", wpp, w_pp[:, :], 2, [], ["wpp"])
            wg = [sbuf(p5, "wg%d" % i, [128, DC, 256], BF16) for i in range(3)]
            wgr = Rot([(wg[i], "wg%d" % i) for i in range(3)])
            wd = [sbuf(p5, "wd%d" % i, [128, NFF, 128], BF16) for i in range(2)]
            wdr = Rot([(wd[i], "wd%d" % i) for i in range(2)])
            h1s = [sbuf(p5, "h1_5%d" % i, [128, DC, 512], F32) for i in range(2)]
            fo = sbuf(p5, "fo", [128, DC, 512], F32)
            fn = sbuf(p5, "fn", [128, DC, 512], BF16)
            h2b = sbuf(p5, "h2b", [128, DC, 512], BF16)
            actT = sbuf(p5, "actT", [128, NFF, 512], BF16)
            sq = [sbuf(p5, "sq5_%d" % i, [128, 512], BF16) for i in range(3)]
            sqr = Rot([(sq[i], "sq%d" % i) for i in range(3)])
            lntA = sbuf(p5, "lnt5a", [128, 512], F32)
            lntB = sbuf(p5, "lnt5b", [128, 512], F32)
            rbcA = sbuf(p5, "rbc5a", [128, 512], F32)
            rbcB = sbuf(p5, "rbc5b", [128, 512], F32)
            tm = [sbuf(p5, "tm5_%d" % i, [128, 512], F32) for i in range(3)]
            tmr = Rot([(tm[i], "tm%d" % i) for i in range(3)])
            ptfs = [sbuf(p5, "pt_f%d" % i, [128, 2, 512], F32) for i in range(2)]
            ptbs = [sbuf(p5, "pt_b%d" % i, [128, 2, 512], BF16) for i in range(2)]
            psr = Rot([(PB[i], PK[i]) for i in range(7)])

            def front_a(j):
                b = j % 2
                osl = slice(j * 512, (j + 1) * 512)
                h1 = h1s[b]
                dmac("sp", h1, h1d[:, osl], DC, ["h1d%d" % j], ["h1%d" % b])
                dmac("sp", ptfs[b], pT[:, osl], 2, [], ["ptf%d" % b])
                cp(ptbs[b][:], ptfs[b][:], ["ptf%d" % b], ["ptb%d" % b])
                pss, pssk = PB[7], PK[7]
                for c in range(DC):
                    sqt, sqk = sqr.next()
                    act(sqt[:], h1[:, c, :], AF.Square, ["h1%d#%d" % (b, c)], [sqk])
                    mm(pss[:], onesb[:], sqt[:], c == 0, c == DC - 1, ["onesb", sqk], [pssk])
                rms_scale(pss[:], D, rbcA[:], lntA[:], [pssk], "rbcA")

            def front_b(j):
                b = j % 2
                h1 = h1s[b]
                for c in range(DC):
                    stt(fn[:, c, :], h1[:, c, :], cs[:, G_FPRE + c:G_FPRE + c + 1], rbcA[:], ALU.mult, ALU.mult,
                        ["h1%d" % b, "cs", "rbcA"], ["fn%d" % c])

            def gateup(j, n0=0, n1=NFF, mid=None):
                for n in range(n0, n1):
                    if n == NFF // 2 and mid is not None:
                        mid()
                    w_, wk = wgr.next()
                    dma("sp", w_[:].rearrange("p c f -> p (c f)"), wgu_s[n, :, :], [], [wk])
                    pg, pgk = psr.next()
                    pu, puk = psr.next()
                    for c in range(DC):
                        mm(pg[:], w_[:, c, 0:128], fn[:, c, :], c == 0, c == DC - 1, [wk, "fn%d" % c], [pgk])
                    for c in range(DC):
                        mm(pu[:], w_[:, c, 128:256], fn[:, c, :], c == 0, c == DC - 1, [wk, "fn%d" % c], [puk])
                    t_, tk = tmr.next()
                    act(t_[:], pg[:], AF.Silu, [pgk], [tk])
                    tt(actT[:, n, :], pu[:], t_[:], ALU.mult, [puk, tk], ["actT%d" % n])

            def down_post_ple(j):
                b = j % 2
                osl = slice(j * 512, (j + 1) * 512)
                h1 = h1s[b]
                hk = "h1%d" % b
                pss, pssk = PB[7], PK[7]
                for ct in range(DC):
                    w_, wk = wdr.next()
                    dma("sp", w_[:].rearrange("p n f -> p (n f)"), wdn_s[ct, :, :], [], [wk])
                    ps, pk = psr.next()
                    for n in range(NFF):
                        mm(ps[:], w_[:, n, :], actT[:, n, :], n == 0, n == NFF - 1, [wk, "actT%d" % n], [pk])
                    sqt, sqk = sqr.next()
                    act(sqt[:], ps[:], AF.Square, [pk], [sqk])
                    cp(fo[:, ct, :], ps[:], [pk], ["fo%d" % ct])
                    mm(pss[:], onesb[:], sqt[:], ct == 0, ct == DC - 1, ["onesb", sqk], [pssk])
                rms_scale(pss[:], D, rbcB[:], lntB[:], [pssk], "rbcB")
                for ct in range(DC):
                    t_, tk = tmr.next()
                    tt(t_[:], fo[:, ct, :], rbcB[:], ALU.mult, ["fo%d" % ct, "rbcB"], [tk])
                    stt(h1[:, ct, :], t_[:], cs[:, G_FPOST + ct:G_FPOST + ct + 1], h1[:, ct, :], ALU.mult, ALU.add,
                        [tk, "cs", hk], [hk])
                    cp(h2b[:, ct, :], h1[:, ct, :], [hk], ["h2b%d" % ct], eng="act")

            def ple(j):
                b = j % 2
                osl = slice(j * 512, (j + 1) * 512)
                h1 = h1s[b]
                hk = "h1%d" % b
                for ct in range(DC):
                    pg, pgk = psr.next()
                    pp, ppk = psr.next()
                    for c in range(DC):
                        mm(pg[:], wpg[:, c, ct * 128:(ct + 1) * 128], h2b[:, c, :], c == 0, c == DC - 1, ["wpg", "h2b%d" % c], [pgk])
                    for c in range(2):
                        mm(pp[:], wpp[:, c, ct * 128:(ct + 1) * 128], ptbs[b][:, c, :], c == 0, c == 1, ["wpp", "ptb%d" % b], [ppk])
                    t_, tk = tmr.next()
                    act(t_[:], pg[:], AF.Sigmoid, [pgk], [tk])
                    tt(t_[:], pp[:], t_[:], ALU.mult, [ppk, tk], [tk])
                    tt(fo[:, ct, :], t_[:], h1[:, ct, :], ALU.add, [tk, hk], ["fo%d" % ct])
                dmac_out("sp", yT[:, osl], fo, DC, ["fo%d" % c for c in range(DC)], ["yT%d" % j])

            if "C" in phases:
                front_a(0)
                front_b(0)
                NSPLIT = 6
                gateup(0, mid=(lambda: front_a(1)) if NO > 1 else None)
                for j in range(NO):
                    if j + 1 < NO:
                        front_b(j + 1)
                    down_post_ple(j)
                    if j + 1 < NO:
                        gateup(j + 1, 0, NSPLIT)
                    ple(j)
                    if j + 1 < NO:
                        gateup(j + 1, NSPLIT, NFF, mid=(lambda j=j: front_a(j + 2)) if j + 2 < NO else None)
            sch.barrier()
        sch.emit()
    return nc


def _consts():
    cmv = np.zeros((128, 4 * 512 + 4 * 128), np.float32)
    k = np.arange(128)[:, None]
    q = np.arange(512)[None, :]
    for a in range(4):
        cmv[:, a * 512:(a + 1) * 512] = np.where(k + a * 128 <= q, 0.0, NEG)
    t = np.arange(128)[:, None]
    jj = np.arange(128)[None, :]
    cmv[:, 2048:2176] = (t <= jj).astype(np.float32)
    cmv[:, 2176:2304] = np.where(t > jj, -NEG, 0.0)
    cmv[:, 2304:2432] = np.eye(128, dtype=np.float32)
    return cmv


def prep(inputs, S, ncores):
    f = lambda a: np.ascontiguousarray(np.asarray(a, dtype=np.float32))
    x = np.asarray(inputs["x"], np.float32)
    p = np.asarray(inputs["p"], np.float32)[0]
    positions = np.asarray(inputs["positions"]).astype(np.int32)
    SO = S // 2
    NB = S // 128
    cst = np.zeros((128, 96), np.float32)
    cst[:, 0:8] = f(inputs["attn_pre_norm"])[0].reshape(8, 128).T
    cst[:, 8:16] = f(inputs["attn_post_norm"])[0].reshape(8, 128).T
    cst[:, 16:24] = f(inputs["ffn_pre_norm"])[0].reshape(8, 128).T
    cst[:, 24:32] = f(inputs["ffn_post_norm"])[0].reshape(8, 128).T
    cst[:, 32:34] = f(inputs["q_norm"])[0].reshape(2, 128).T
    cst[:, 34:35] = f(inputs["kv_norm"])[0].reshape(1, 128).T
    cw = f(inputs["conv_w"])[0]
    cst[:, 35:67] = cw.reshape(4, 8, 128).transpose(2, 1, 0).reshape(128, 32)
    cst[:, 67:75] = f(inputs["conv_b"])[0].reshape(8, 128).T
    inv = (10000.0 ** (-np.arange(0, 64, 2, dtype=np.float32) / 64)).astype(np.float32)
    cst[0:64, 75] = np.concatenate([inv, inv]) / np.float32(2 * np.pi)
    gainrow = f(inputs["mlstm_norm"])[0].reshape(1, 512)
    gbrow = np.concatenate([f(inputs["gate_bias_i"])[0], f(inputs["gate_bias_f"])[0]]).reshape(1, 8)
    cmv = _consts()
    wg = f(inputs["w_gate"])[0].reshape(8, 128, NFF, 128)
    wu = f(inputs["w_up"])[0].reshape(8, 128, NFF, 128)
    wgu = np.ascontiguousarray(np.concatenate([wg, wu], axis=3).transpose(2, 1, 0, 3))
    wdn = np.ascontiguousarray(f(inputs["w_down"])[0].reshape(NFF, 128, 8, 128).transpose(2, 1, 0, 3))
    shared = {
        "cst": cst, "gainrow": gainrow, "gbrow": gbrow, "cm": cmv,
        "w_in": f(inputs["w_in"])[0], "w_uq": f(inputs["w_uq"])[0], "w_ukv": f(inputs["w_ukv"])[0],
        "w_out": f(inputs["w_out"])[0], "wgu": wgu, "wdn": wdn,
        "w_pp": f(inputs["w_ple_proj"])[0], "w_pg": f(inputs["w_ple_gate"])[0],
    }
    maps = []
    own_idx = []
    for core in range(ncores):
        b, par = core // 2, core % 2
        xb = x[b]
        if par == 1:
            xv = xb
            pv = positions[b]
            valid = np.ones(S, np.float32)
        else:
            xv = np.concatenate([np.zeros((512, D), np.float32), xb[:S - 512]], axis=0)
            pv = np.concatenate([np.zeros(512, np.int32), positions[b][:S - 512]])
            valid = np.concatenate([np.zeros(512, np.float32), np.ones(S - 512, np.float32)])
        view_tok = np.arange(S).reshape(S // 512, 512)[1::2].reshape(-1)
        glob = view_tok if par == 1 else view_tok - 512
        own_idx.append((b, glob))
        m = dict(shared)
        m["xT"] = np.ascontiguousarray(xv.T)
        m["pos"] = np.ascontiguousarray(pv.reshape(1, S))
        m["pT"] = np.ascontiguousarray(p[b][glob].T)
        m["kflag"] = np.where(valid > 0, 0.0, NEG).astype(np.float32).reshape(1, S)
        m["validT"] = np.ascontiguousarray(valid.reshape(NB, 128).T)
        maps.append(m)
    return maps, own_idx


_NC_CACHE = {}


def kernel(**inputs):
    x = np.asarray(inputs["x"])
    B, S, _ = x.shape
    ncores = 2 * B
    if S not in _NC_CACHE:
        _NC_CACHE[S] = build(S)
    nc = _NC_CACHE[S]
    maps, own_idx = prep(inputs, S, ncores)
    res = run_bass_kernel_spmd(nc, maps, core_ids=list(range(ncores)))
    out = np.zeros((B, S, D), np.float32)
    for core in range(ncores):
        b, glob = own_idx[core]
        out[b, glob, :] = np.asarray(res.results[core]["yT"]).T
    return out
```
